# Optimizing a Trainium2 kernel written in Bass

```python
import math
import jax, jax.numpy as jnp
from jax import lax
import numpy as np

D_MODEL = 2048
BATCH = 2
SEQ = 4096
DEPTH = 2
DEC_BATCH = 8
DEC_SEQ = 4
PAST_LEN = 16384
PAGE_SIZE = 128

N_A_LAYERS = DEPTH // 2
N_B_LAYERS = DEPTH - N_A_LAYERS
RET_HEADS = 8
RET_DK = D_MODEL // RET_HEADS
RET_DV = 2 * RET_DK
RET_CHUNK = 128
ROPE_BASE = 10000.0
SB_HEADS = 16
SB_DH = D_MODEL // SB_HEADS
SB_BLOCK = 128
SB_BIAS_HI = -4.0
SB_BIAS_LO = -8.0
D_FF = 4 * D_MODEL
LN_EPS = 1e-5
GN_EPS = 1e-6
ALPHA = (2.0 * DEPTH) ** 0.25
BETA = (8.0 * DEPTH) ** -0.25
POOL_FACTOR = 1.25

kernel_name = "yoco_retention_stickbreaking_decoder_step"


def layer_norm(x, g, b):
    xf = x.astype(jnp.float32)
    mu = jnp.mean(xf, axis=-1, keepdims=True)
    var = jnp.mean(jnp.square(xf - mu), axis=-1, keepdims=True)
    return ((xf - mu) * lax.rsqrt(var + LN_EPS) * g.astype(jnp.float32) + b.astype(jnp.float32)).astype(x.dtype)


def post_norm(x, sub, g, b):
    return layer_norm(ALPHA * x + sub, g, b)


def rotary(x, pos):
    half = x.shape[-1] // 2
    inv = ROPE_BASE ** (-jnp.arange(half, dtype=jnp.float32) / half)
    ang = pos.astype(jnp.float32)[:, None] * inv[None, :]
    cos = jnp.cos(ang)[None, :, None, :]
    sin = jnp.sin(ang)[None, :, None, :]
    xf = x.astype(jnp.float32)
    x1, x2 = xf[..., :half], xf[..., half:]
    return jnp.concatenate([x1 * cos - x2 * sin, x1 * sin + x2 * cos], axis=-1).astype(x.dtype)


def retention_scan(q, k, v, state0):
    B, T, H, DK = q.shape
    DV = v.shape[-1]
    C = RET_CHUNK if T % RET_CHUNK == 0 else T
    n = T // C
    log_g = jnp.log1p(-jnp.power(2.0, -5.0 - jnp.arange(H, dtype=jnp.float32)))
    idx = jnp.arange(C, dtype=jnp.float32)
    diff = idx[:, None] - idx[None, :]
    decay_in = jnp.where(diff[None] >= 0, jnp.exp(log_g[:, None, None] * jnp.maximum(diff, 0.0)[None]), 0.0)
    decay_q = jnp.exp(log_g[:, None] * (idx + 1.0)[None, :])
    decay_k = jnp.exp(log_g[:, None] * (C - 1.0 - idx)[None, :])
    decay_c = jnp.exp(log_g * C)

    def to_chunks(a):
        return a.astype(jnp.float32).reshape(B, n, C, H, a.shape[-1]).transpose(1, 0, 3, 2, 4)

    def step(S, inp):
        qc, kc, vc = inp
        scores = jnp.einsum('bhid,bhjd->bhij', qc, kc) * decay_in[None]
        inner = jnp.einsum('bhij,bhjv->bhiv', scores, vc)
        cross = jnp.einsum('bhid,bhdv->bhiv', qc, S) * decay_q[None, :, :, None]
        S_new = decay_c[None, :, None, None] * S + jnp.einsum('bhjd,bhjv->bhdv', kc * decay_k[None, :, :, None], vc)
        return S_new, inner + cross

    S_final, out = lax.scan(step, state0, (to_chunks(q), to_chunks(k), to_chunks(v)))
    out = out.transpose(1, 0, 3, 2, 4).reshape(B, T, H, DV)
    return out, S_final


def retention_mixer(x, pos, state0, w_in, w_out):
    B, T, _ = x.shape
    HK = RET_HEADS * RET_DK
    HV = RET_HEADS * RET_DV
    proj = jnp.einsum('btd,de->bte', x, w_in)
    q, k, v, g = jnp.split(proj, [HK, 2 * HK, 2 * HK + HV], axis=-1)
    q = rotary(q.reshape(B, T, RET_HEADS, RET_DK), pos) * (RET_DK ** -0.5)
    k = rotary(k.reshape(B, T, RET_HEADS, RET_DK), pos)
    v = v.reshape(B, T, RET_HEADS, RET_DV)
    o, S = retention_scan(q, k, v, state0)
    mu = jnp.mean(o, axis=-1, keepdims=True)
    var = jnp.mean(jnp.square(o - mu), axis=-1, keepdims=True)
    o = ((o - mu) * lax.rsqrt(var + GN_EPS)).reshape(B, T, HV)
    o = jax.nn.silu(g.astype(jnp.float32)) * o
    return jnp.einsum('bte,ed->btd', o.astype(x.dtype), w_out), S


def stick_breaking_attend(q, k, v, bias, q_pos0):
    B, T, H, Dh = q.shape
    L = k.shape[1]
    blk = SB_BLOCK if T % SB_BLOCK == 0 else T
    nb = T // blk
    qb = q.astype(jnp.float32).reshape(B, nb, blk, H, Dh).transpose(1, 0, 2, 3, 4)
    kf = k.astype(jnp.float32)
    vf = v.astype(jnp.float32)
    bf = bias.astype(jnp.float32)[None, :, None, None]
    key_pos = jnp.arange(L, dtype=jnp.int32)
    scale = Dh ** -0.5

    def one_block(args):
        qblk, b_idx = args
        q_pos = q_pos0 + b_idx * blk + jnp.arange(blk, dtype=jnp.int32)
        z = jnp.einsum('bqhd,bkhd->bhqk', qblk, kf) * scale + bf
        visible = (key_pos[None, :] < q_pos[:, None])[None, None]
        log_rem = jnp.where(visible, jax.nn.log_sigmoid(-z), 0.0)
        after = lax.cumsum(log_rem, axis=3, reverse=True) - log_rem
        w = jnp.where(visible, jnp.exp(jax.nn.log_sigmoid(z) + after), 0.0)
        return jnp.einsum('bhqk,bkhd->bqhd', w, vf)

    out = lax.map(one_block, (qb, jnp.arange(nb, dtype=jnp.int32)))
    return out.transpose(1, 0, 2, 3, 4).reshape(B, T, H, Dh).astype(q.dtype)


def stick_breaking_mixer(x, k_all, v_all, q_pos0, w_q, w_o, bias):
    B, T, _ = x.shape
    q = jnp.einsum('btd,de->bte', x, w_q).reshape(B, T, SB_HEADS, SB_DH)
    o = stick_breaking_attend(q, k_all, v_all, bias, q_pos0)
    return jnp.einsum('bte,ed->btd', o.reshape(B, T, SB_HEADS * SB_DH), w_o)


def sq_relu_mlp(x, w1, w2):
    h = jnp.einsum('btd,df->btf', x, w1)
    return jnp.einsum('btf,fd->btd', jnp.square(jax.nn.relu(h)), w2)


def trunk(x, pos0, ret_states, past_k, past_v, w_ret_in, w_ret_out, w_kv, w_sb_q, w_sb_o, sb_bias,
          w_ff1, w_ff2, ln_g, ln_b):
    B, T, _ = x.shape
    pos = pos0 + jnp.arange(T, dtype=jnp.int32)
    new_ret = []
    k_new = v_new = k_all = v_all = None
    for layer in range(DEPTH):
        if layer < N_A_LAYERS:
            mix, S = retention_mixer(x, pos, ret_states[layer], w_ret_in[layer], w_ret_out[layer])
            new_ret.append(S)
        else:
            j = layer - N_A_LAYERS
            mix = stick_breaking_mixer(x, k_all, v_all, pos0, w_sb_q[j], w_sb_o[j], sb_bias[j])
        x = post_norm(x, mix, ln_g[layer, 0], ln_b[layer, 0])
        x = post_norm(x, sq_relu_mlp(x, w_ff1[layer], w_ff2[layer]), ln_g[layer, 1], ln_b[layer, 1])
        if layer == N_A_LAYERS - 1:
            kv = jnp.einsum('btd,de->bte', x, w_kv)
            k_new = kv[..., :D_MODEL].reshape(B, T, SB_HEADS, SB_DH)
            v_new = kv[..., D_MODEL:].reshape(B, T, SB_HEADS, SB_DH)
            if past_k is None:
                k_all, v_all = k_new, v_new
            else:
                k_all = jnp.concatenate([past_k.astype(k_new.dtype), k_new], axis=1)
                v_all = jnp.concatenate([past_v.astype(v_new.dtype), v_new], axis=1)
    return x, jnp.stack(new_ret), k_new, v_new


def setup_inputs(seed: int = 0) -> dict:
    key = jax.random.key(seed)
    ks = jax.random.split(key, 20)
    n_pages = PAST_LEN // PAGE_SIZE
    n_used = DEC_BATCH * n_pages
    n_pool = int(math.ceil(POOL_FACTOR * n_used))
    HK = RET_HEADS * RET_DK
    HV = RET_HEADS * RET_DV
    s_d = D_MODEL ** -0.5
    w_ret_in = jnp.concatenate([
        jax.random.normal(ks[0], (N_A_LAYERS, D_MODEL, 2 * HK), jnp.float32) * s_d,
        jax.random.normal(ks[1], (N_A_LAYERS, D_MODEL, HV), jnp.float32) * (s_d * BETA),
        jax.random.normal(ks[2], (N_A_LAYERS, D_MODEL, HV), jnp.float32) * s_d,
    ], axis=-1)
    w_ret_out = jax.random.normal(ks[3], (N_A_LAYERS, HV, D_MODEL), jnp.float32) * (HV ** -0.5 * BETA)
    w_kv = jnp.concatenate([
        jax.random.normal(ks[4], (D_MODEL, D_MODEL), jnp.float32) * s_d,
        jax.random.normal(ks[5], (D_MODEL, D_MODEL), jnp.float32) * (s_d * BETA),
    ], axis=-1)
    w_sb_q = jax.random.normal(ks[6], (N_B_LAYERS, D_MODEL, D_MODEL), jnp.float32) * s_d
    w_sb_o = jax.random.normal(ks[7], (N_B_LAYERS, D_MODEL, D_MODEL), jnp.float32) * (s_d * BETA)
    sb_bias = (jnp.linspace(SB_BIAS_HI, SB_BIAS_LO, SB_HEADS, dtype=jnp.float32)[None, :]
               + 0.05 * jax.random.normal(ks[18], (N_B_LAYERS, SB_HEADS), jnp.float32))
    w_ff1 = jax.random.normal(ks[8], (DEPTH, D_MODEL, D_FF), jnp.float32) * (s_d * BETA)
    w_ff2 = jax.random.normal(ks[9], (DEPTH, D_FF, D_MODEL), jnp.float32) * (D_FF ** -0.5 * BETA)
    ln_g = 1.0 + 0.02 * jax.random.normal(ks[10], (DEPTH, 2, D_MODEL), jnp.float32)
    ln_b = 0.02 * jax.random.normal(ks[11], (DEPTH, 2, D_MODEL), jnp.float32)
    x_prompt = jax.random.normal(ks[12], (BATCH, SEQ, D_MODEL), jnp.float32)
    x_sample = jax.random.normal(ks[13], (DEC_BATCH, DEC_SEQ, D_MODEL), jnp.float32)
    state_ret = jax.random.normal(ks[14], (N_A_LAYERS, DEC_BATCH, RET_HEADS, RET_DK, RET_DV), jnp.float32)
    cache_k = jax.random.normal(ks[15], (n_pool, PAGE_SIZE, SB_HEADS, SB_DH), jnp.float32)
    cache_v = jax.random.normal(ks[16], (n_pool, PAGE_SIZE, SB_HEADS, SB_DH), jnp.float32) * BETA
    page_table = jax.random.permutation(ks[17], n_pool)[:n_used].reshape(DEC_BATCH, n_pages).astype(jnp.int32)
    return {"x_prompt": x_prompt, "x_sample": x_sample, "state_ret": state_ret,
            "cache_k": cache_k, "cache_v": cache_v, "page_table": page_table,
            "w_ret_in": w_ret_in, "w_ret_out": w_ret_out, "w_kv": w_kv,
            "w_sb_q": w_sb_q, "w_sb_o": w_sb_o, "sb_bias": sb_bias, "w_ff1": w_ff1, "w_ff2": w_ff2,
            "ln_g": ln_g, "ln_b": ln_b}


def reference(x_prompt, x_sample, state_ret, cache_k, cache_v, page_table,
              w_ret_in, w_ret_out, w_kv, w_sb_q, w_sb_o, sb_bias, w_ff1, w_ff2, ln_g, ln_b):
    zero_ret = jnp.zeros((N_A_LAYERS, x_prompt.shape[0], RET_HEADS, RET_DK, RET_DV), jnp.float32)
    y_prompt, ret_prompt, k_prompt, v_prompt = trunk(
        x_prompt, 0, zero_ret, None, None, w_ret_in, w_ret_out, w_kv, w_sb_q, w_sb_o, sb_bias,
        w_ff1, w_ff2, ln_g, ln_b)
    db, n_pages = page_table.shape
    past_len = n_pages * PAGE_SIZE
    past_k = cache_k[page_table].reshape(db, past_len, SB_HEADS, SB_DH)
    past_v = cache_v[page_table].reshape(db, past_len, SB_HEADS, SB_DH)
    y_sample, ret_sample, k_sample, v_sample = trunk(
        x_sample, past_len, state_ret.astype(jnp.float32), past_k, past_v, w_ret_in, w_ret_out, w_kv,
        w_sb_q, w_sb_o, sb_bias, w_ff1, w_ff2, ln_g, ln_b)
    ret_prompt = ret_prompt.astype(state_ret.dtype)
    ret_sample = ret_sample.astype(state_ret.dtype)
    return (y_prompt, y_sample, ret_prompt, k_prompt, v_prompt, ret_sample, k_sample, v_sample)
```

```python
import contextlib
import math
import numpy as np
import concourse.bass as bass
import concourse.mybir as mybir
from concourse.bass_utils import run_bass_kernel_spmd

F32 = mybir.dt.float32
BF16 = mybir.dt.bfloat16
I32 = mybir.dt.int32
AF = mybir.ActivationFunctionType
ALU = mybir.AluOpType

D = 2048
KC = 16
RH = 8
SH = 16
DFF = 8192
TT = 512
LN_EPS = 1e-5
GN_EPS = 1e-6
DEPTH = 2
ALPHA = (2.0 * DEPTH) ** 0.25
ROPE_BASE = 10000.0
PAGE = 128
NCORE = 8
PIECE = 4096


class Buf:
    __slots__ = ("name", "w", "r")

    def __init__(self, name):
        self.name = name
        self.w = {}
        self.r = {}


class Sched:
    NDMA = 8

    def __init__(self, nc, stack):
        self.nc = nc
        self.eng = {"pe": nc.tensor, "act": nc.scalar, "dve": nc.vector, "pool": nc.gpsimd, "sp": nc.sync}
        self.sem = {}
        self.cnt = {}
        self.waited = {k: {} for k in self.eng}
        for k in ("pe", "act", "dve", "pool"):
            self.sem[k] = stack.enter_context(nc.semaphore("sem_" + k))
            self.cnt[k] = 0
        self.dsem = {}
        self.dn = {}
        for q in ("sp", "act", "pool"):
            self.dsem[q] = [stack.enter_context(nc.semaphore("dsem_%s_%d" % (q, i))) for i in range(self.NDMA)]
            self.dn[q] = 0
        self.n_inst = 0
        self.n_wait = 0

    def _wait(self, e, need):
        wd = self.waited[e]
        pes = self.sem["pe"]
        for sid, (s, v) in need.items():
            if wd.get(sid, 0) >= v:
                continue
            if e == "pe" and s is pes:
                continue
            self.eng[e].wait_ge(s, v)
            self.n_wait += 1
            wd[sid] = v

    @staticmethod
    def _merge(need, d):
        for sid, sv in d.items():
            o = need.get(sid)
            if o is None or o[1] < sv[1]:
                need[sid] = sv

    def _deps(self, reads, writes):
        need = {}
        for b in reads:
            self._merge(need, b.w)
        for b in writes:
            self._merge(need, b.w)
            self._merge(need, b.r)
        return need

    @staticmethod
    def _mark(ev, reads, writes):
        sid = id(ev[0])
        for b in reads:
            b.r[sid] = ev
        for b in writes:
            b.w[sid] = ev

    def op(self, e, fn, reads=(), writes=()):
        self._wait(e, self._deps(reads, writes))
        ins = fn(self.eng[e])
        self.cnt[e] += 1
        ev = (self.sem[e], self.cnt[e])
        ins.then_inc(ev[0], 1)
        self._mark(ev, reads, writes)
        self.n_inst += 1

    def dma(self, q, out, in_, reads=(), writes=(), fn=None):
        n = self.dn[q]
        K = self.NDMA
        s = self.dsem[q][n % K]
        prev = 16 * (n // K)
        need = self._deps(reads, writes)
        if prev > 0:
            self._merge(need, {id(s): (s, prev)})
        self._wait(q, need)
        if fn is None:
            ins = self.eng[q].dma_start(out=out, in_=in_)
        else:
            ins = fn(self.eng[q])
        ins.then_inc(s, 16)
        self.dn[q] = n + 1
        self._mark((s, prev + 16), reads, writes)
        self.n_inst += 1

    def finish(self, bufs):
        need = {}
        for b in bufs:
            self._merge(need, b.w)
            self._merge(need, b.r)
        self._wait("sp", need)


def piece_table():
    t = {}
    n = 0
    for name, cnt in (("qk", 2 * RH), ("g", 2 * RH), ("v", 2 * RH), ("out", 2 * RH),
                      ("ff1_0", 32), ("ff2_0", 32), ("k", 8), ("v2", 8), ("q", 8), ("o", 8),
                      ("ff1_1", 32), ("ff2_1", 32)):
        t[name] = (n, cnt)
        n += cnt
    return t, n


def tile_fm(Wm, fcs):
    K = Wm.shape[0]
    out = np.empty((len(fcs), 128, K // 128, 128), np.float32)
    for i, fc in enumerate(fcs):
        out[i] = Wm[:, fc * 128:(fc + 1) * 128].reshape(K // 128, 128, 128).transpose(1, 0, 2)
    return out


def pack_weights(w_ret_in, w_ret_out, w_kv, w_sb_q, w_sb_o, w_ff1, w_ff2):
    tab, n = piece_table()
    Wp = np.empty((n, 128, PIECE), np.float32)
    win = w_ret_in[0]
    HK = 2048
    for h in range(RH):
        Wp[tab["qk"][0] + 2 * h] = tile_fm(win, [(h * 256) // 128, (h * 256) // 128 + 1]).transpose(1, 0, 2, 3).reshape(128, PIECE)
        Wp[tab["qk"][0] + 2 * h + 1] = tile_fm(win, [(HK + h * 256) // 128, (HK + h * 256) // 128 + 1]).transpose(1, 0, 2, 3).reshape(128, PIECE)
        g0 = (2 * HK + 4096 + h * 512) // 128
        for j in range(2):
            Wp[tab["g"][0] + 2 * h + j] = tile_fm(win, [g0 + 2 * j, g0 + 2 * j + 1]).transpose(1, 0, 2, 3).reshape(128, PIECE)
        v0 = 2 * HK + h * 512
        for j in range(2):
            Wp[tab["v"][0] + 2 * h + j] = win[:, v0 + j * 256:v0 + (j + 1) * 256].reshape(KC, 128, 256).transpose(1, 0, 2).reshape(128, PIECE)
        wo = w_ret_out[0][h * 512:(h + 1) * 512]
        for j in range(2):
            Wp[tab["out"][0] + 2 * h + j] = tile_fm(wo, list(range(8 * j, 8 * j + 8))).transpose(1, 0, 2, 3).reshape(128, PIECE)
    for l in range(2):
        w1 = w_ff1[l]
        for i in range(32):
            Wp[tab["ff1_%d" % l][0] + i] = tile_fm(w1, [2 * i, 2 * i + 1]).transpose(1, 0, 2, 3).reshape(128, PIECE)
        w2 = w_ff2[l]
        for g in range(4):
            w2g = w2[g * 2048:(g + 1) * 2048]
            for i in range(8):
                Wp[tab["ff2_%d" % l][0] + g * 8 + i] = tile_fm(w2g, [2 * i, 2 * i + 1]).transpose(1, 0, 2, 3).reshape(128, PIECE)
    for i in range(8):
        Wp[tab["k"][0] + i] = tile_fm(w_kv, [2 * i, 2 * i + 1]).transpose(1, 0, 2, 3).reshape(128, PIECE)
        Wp[tab["v2"][0] + i] = w_kv[:, D + i * 256:D + (i + 1) * 256].reshape(KC, 128, 256).transpose(1, 0, 2).reshape(128, PIECE)
        Wp[tab["q"][0] + i] = tile_fm(w_sb_q[0], [2 * i, 2 * i + 1]).transpose(1, 0, 2, 3).reshape(128, PIECE)
        Wp[tab["o"][0] + i] = tile_fm(w_sb_o[0], [2 * i, 2 * i + 1]).transpose(1, 0, 2, 3).reshape(128, PIECE)
    return Wp


def const_tables(SEQ, NS, past_len):
    half = 128
    inv = ROPE_BASE ** (-np.arange(half, dtype=np.float32) / half)
    pos = np.concatenate([np.arange(SEQ), past_len + np.arange(NS)]).astype(np.float32)
    ang = inv[:, None].astype(np.float32) * pos[None, :].astype(np.float32)
    rope = np.stack([np.cos(ang), np.sin(ang)], axis=1).astype(np.float32)
    log_g = np.log1p(-np.power(2.0, -5.0 - np.arange(RH, dtype=np.float64)))
    idx = np.arange(128, dtype=np.float64)
    diff = idx[None, :] - idx[:, None]
    dt = np.where(diff[None] >= 0, np.exp(log_g[:, None, None] * np.maximum(diff, 0)[None]), 0.0) / 16.0
    dt = dt.transpose(1, 0, 2).astype(np.float32)
    dq = (np.exp(log_g[:, None] * (idx + 1.0)[None, :]) / 16.0).astype(np.float32)
    dq = np.broadcast_to(dq[None], (128, RH, 128)).copy()
    dk = np.zeros((128, 2 * RH), np.float32)
    dk[:, :RH] = np.exp(log_g[None, :] * (127.0 - idx)[:, None])
    dk[:NS, RH:] = np.exp(log_g[None, :] * (NS - 1.0 - idx[:NS])[:, None])
    ident = np.eye(128, dtype=np.float32)
    tri = (idx[:, None] >= idx[None, :]).astype(np.float32)
    ustr = (idx[:, None] > idx[None, :]).astype(np.float32)
    dmat = (np.arange(512, dtype=np.float32)[None, :] - np.arange(128, dtype=np.float32)[:, None])
    m4 = np.zeros((128, SH, NS), np.float32)
    for j in range(NS):
        for i in range(NS):
            if j < i:
                m4[j, :, i] = 1.0
    iota = np.broadcast_to(np.arange(128, dtype=np.float32)[None, :], (128, 128))
    misc = np.concatenate([ident, tri, ustr, dmat, m4.reshape(128, SH * NS), iota], axis=1).astype(np.float32)
    decay_c = [float(np.exp(log_g[h] * 128.0)) for h in range(RH)]
    decay_cs = [float(np.exp(log_g[h] * NS)) for h in range(RH)]
    return rope, dt.reshape(128, RH * 128), dq.reshape(128, RH * 128), dk, misc, decay_c, decay_cs


class Ctx:
    def __init__(self, xacc, xb, Bx, Bxb):
        self.xacc, self.xb, self.Bx, self.Bxb = xacc, xb, Bx, Bxb


def build_program(SEQ, NS, NPG, NPOOL, stop_after=None):
    NT = SEQ // TT
    assert NT % 4 == 0
    NOWN = NT // 4
    NKB = SEQ // 128
    HQ = SH * NS
    tab, NPIECE = piece_table()
    _, _, _, _, _, decay_c, decay_cs = const_tables(SEQ, NS, NPG * PAGE)
    scale = 128 ** -0.5
    MISCW = 128 * 3 + 512 + HQ + 128

    nc = bass.Bass("TRN2", target_bir_lowering=False)

    def din(name, shape, dt=F32):
        return nc.dram_tensor(name, list(shape), dt, kind="ExternalInput").ap()

    def dout(name, shape, dt=F32):
        return nc.dram_tensor(name, list(shape), dt, kind="ExternalOutput").ap()

    xT = din("xT", [128, KC, SEQ])
    xsT = din("xsT", [128, KC, NS])
    state = din("state", [RH, 256, 512])
    cache_k = din("cache_k", [NPOOL * PAGE, D])
    cache_v = din("cache_v", [NPOOL * PAGE, D])
    pt = din("pt", [NPG, 1], I32)
    Wd = din("W", [NPIECE, 128, PIECE])
    rope_d = din("rope", [128, 2, SEQ + NS])
    dt_d = din("dtab", [128, RH * 128])
    dq_d = din("dqtab", [128, RH * 128])
    dk_d = din("dktab", [128, 2 * RH])
    misc_d = din("misc", [128, MISCW])
    lnp_d = din("lnp", [128, 2 * 4 * KC])
    sbb_d = din("sbb", [1, SH])
    pcf_d = din("pcf", [128, NOWN])
    pci_d = din("pci", [128, NOWN], I32)

    yT = dout("yT", [NOWN, 128, KC * TT])
    ysT = dout("ysT", [128, KC * NS])
    retp = dout("retp", [RH, 256, 512])
    kTo = dout("kTo", [SH, 128, SEQ])
    vo = dout("vo", [SEQ, D])
    rets = dout("rets", [RH, 256, 512])
    ksTo = dout("ksTo", [128, HQ])
    vso = dout("vso", [NS, D])

    Ks = nc.dram_tensor("Ks", [SH, 128, SEQ], BF16).ap()
    Vs = nc.dram_tensor("Vs", [SH, 128, NKB * 128], BF16).ap()
    X1s = nc.dram_tensor("X1s", [NT * 128, KC * TT], F32).ap()
    XBs = nc.dram_tensor("XBs", [NT * 128, KC * TT], BF16).ap()
    Wbc = nc.dram_tensor("Wbc", [NPIECE, 128, PIECE], BF16).ap()

    with contextlib.ExitStack() as st:
        def sb(name, shape, dt):
            return st.enter_context(nc.sbuf_tensor("s_" + name, list(shape), dt))

        def psum(name, shape, dt):
            return st.enter_context(nc.psum_tensor("p_" + name, list(shape), dt))

        xacc = sb("xacc", [128, KC, TT], F32)
        xb = sb("xb", [128, KC, TT], BF16)
        NWB = 3
        wbuf = [sb("wb%d" % i, [128, PIECE], BF16) for i in range(NWB)]
        hbuf = sb("hbuf", [128, KC, TT], BF16)
        kvb = sb("kvb", [128, max(2 * NKB * 128, D + HQ)], BF16)
        Ssb = sb("S", [128, RH * 2, 512], F32)
        Sbf = sb("Sbf", [128, 2, 512], BF16)
        rope_t = sb("rope_t", [128, 2, TT], F32)
        rq = sb("rq", [128, 2, TT], BF16)
        qh = sb("qh", [128, 2, TT], BF16)
        rk = sb("rk", [128, 2, TT], BF16)
        vsb = sb("vsb", [128, 4, 512], BF16)
        sg = sb("sg", [128, 4, TT], F32)
        og = sb("og", [128, 4, TT], BF16)
        pool = sb("tpool", [128, 8, TT], F32)
        tmp = [pool[:, i, :] for i in range(8)]
        scb = sb("scb", [128, 128], BF16)
        kd = sb("kd", [128, 256], BF16)
        onb = sb("onb", [128, 512], BF16)
        stats = sb("stats", [128, 6], F32)
        mv = sb("mv", [128, 2], F32)
        rstd = sb("rstd", [128, 1], F32)
        dt_t = sb("dt_t", [128, RH, 128], F32)
        dq_t = sb("dq_t", [128, RH, 128], F32)
        dk_t = sb("dk_t", [128, 2 * RH], F32)
        misc_t = sb("misc_t", [128, MISCW], F32)
        identb = sb("identb", [128, 128], BF16)
        ones = sb("ones", [128, 128], F32)
        lnp = sb("lnp", [128, 2 * 4 * KC], F32)
        lnpa = sb("lnpa", [128, 2 * 4 * KC], F32)
        sbb = sb("sbb", [128, SH], F32)
        pcf = sb("pcf", [128, NOWN], F32)
        pci = sb("pci", [128, NOWN], I32)
        qT2 = sb("qT2", [128, 2, TT], BF16)
        w_t = [sb("w_t%d" % i, [128, TT], BF16) for i in range(2)]
        kb16 = [sb("kb16_%d" % i, [128, TT], BF16) for i in range(2)]
        vf = [sb("vf%d" % i, [128, 256], F32) for i in range(2)]
        vb16 = [sb("vb16_%d" % i, [128, 256], BF16) for i in range(2)]
        xsacc = sb("xsacc", [128, KC, NS], F32)
        xsb = sb("xsb", [128, KC, NS], BF16)
        qTs = sb("qTs", [128, SH, NS], BF16)
        pti = sb("pti", [128, 1], I32)
        ptf = sb("ptf", [128, 1], F32)
        idxi = sb("idxi", [128, PAGE], I32)
        rflat = rope_t[:].rearrange("p a n -> p (a n)")
        sbb64 = rflat[:, 0:HQ]
        Tt = rflat[:, HQ:2 * HQ]
        en = rflat[:, 2 * HQ:3 * HQ]
        spn = rflat[:, 3 * HQ:4 * HQ]
        un = rflat[:, 4 * HQ:5 * HQ]
        oacc = rflat[:, 5 * HQ:6 * HQ]
        idxf = rflat[:, 6 * HQ:6 * HQ + PAGE]
        wn = sb("wn", [128, HQ], BF16)
        obs = sb("obs", [128, SH, NS], BF16)
        ksn = kvb[:, D:D + HQ].rearrange("p (h q) -> p h q", h=SH)
        vsn = kvb[:, 0:D]

        NRING = 5
        pring = [psum("pr%d" % i, [128, 512], F32) for i in range(NRING)]
        accA = psum("accA", [128, 512], F32)
        accB = psum("accB", [128, 512], F32)
        tpb = psum("tpb", [128, 1024], BF16)

        blk = st.enter_context(nc.Block())
        S = Sched(nc, st)
        B = {}

        def bf(name):
            if name not in B:
                B[name] = Buf(name)
            return B[name]

        Bx = [bf("xacc%d" % c) for c in range(KC)]
        Bxb = [bf("xb%d" % c) for c in range(KC)]
        Bh = [bf("hbuf%d" % c) for c in range(KC)]
        Bw = [bf("wb%d" % i) for i in range(NWB)]
        Bring = [bf("pr%d" % i) for i in range(NRING)]
        Bt = [bf("tmp%d" % i) for i in range(8)]
        ring_pos = [0]
        ctxP = Ctx(xacc, xb, Bx, Bxb)
        ctxS = Ctx(xsacc, xsb, [bf("xsacc")] * KC, [bf("xsb")] * KC)

        def ring():
            i = ring_pos[0] % NRING
            ring_pos[0] += 1
            return pring[i], Bring[i]

        plan = []
        wpos = [0]
        wissued = [0]

        Bwc = {}

        def w_issue_upto(k):
            while wissued[0] < min(len(plan), k):
                i = wissued[0]
                pid = plan[i]
                if pid not in Bwc:
                    S.dma("pool", wbuf[i % NWB][:], Wd[pid], writes=[Bw[i % NWB]])
                    Bwc[pid] = Buf("wc%d" % pid)
                    S.dma("sp", Wbc[pid], wbuf[i % NWB][:], reads=[Bw[i % NWB]], writes=[Bwc[pid]])
                else:
                    S.dma("sp", wbuf[i % NWB][:], Wbc[pid], reads=[Bwc[pid]], writes=[Bw[i % NWB]])
                wissued[0] += 1

        def wget(pid):
            i = wpos[0]
            assert plan[i] == pid, (i, plan[i], pid)
            w_issue_upto(i + NWB)
            wpos[0] += 1
            return wbuf[i % NWB], Bw[i % NWB]

        def wprefetch():
            w_issue_upto(wpos[0] + NWB - 1)

        def plan_mlp(l):
            p = []
            for g in range(4):
                p += [tab["ff1_%d" % l][0] + g * 8 + i for i in range(8)]
                p += [tab["ff2_%d" % l][0] + g * 8 + i for i in range(8)]
            return p

        def plan_layer0():
            p = []
            for h in range(RH):
                p += [tab["qk"][0] + 2 * h, tab["qk"][0] + 2 * h + 1]
                p += [tab["v"][0] + 2 * h, tab["v"][0] + 2 * h + 1]
                p += [tab["g"][0] + 2 * h, tab["g"][0] + 2 * h + 1]
                p += [tab["out"][0] + 2 * h, tab["out"][0] + 2 * h + 1]
            p += plan_mlp(0)
            p += [tab["k"][0] + i for i in range(8)]
            p += [tab["v2"][0] + i for i in range(8)]
            return p

        def plan_layer1():
            p = [tab["q"][0] + i for i in range(8)]
            p += [tab["o"][0] + i for i in range(8)]
            p += plan_mlp(1)
            return p

        for t in range(NT):
            plan.extend(plan_layer0())
        for j in range(NOWN):
            plan.extend(plan_layer1())
        if stop_after is None:
            plan.extend(plan_layer0())
            plan.extend(plan_layer1())

        def lin_fm(pids, cpp, KCn, rhs_fn, N, consume):
            fc = 0
            for pid in pids:
                wap, wb_ = wget(pid)
                wv = wap[:].rearrange("p (j k m) -> p j k m", j=cpp, k=KCn)
                for j in range(cpp):
                    pb, pbb = ring()
                    for kc in range(KCn):
                        rap, rbufs = rhs_fn(kc)
                        S.op("pe", lambda e, pb=pb, l=wv[:, j, kc, :], r=rap, kc=kc: e.matmul(
                            pb[:, 0:N], lhsT=l, rhs=r, start=(kc == 0), stop=(kc == KCn - 1)),
                            [wb_] + rbufs, [pbb])
                    consume(fc, pb, pbb)
                    fc += 1

        def xb_rhs(cx, N):
            return lambda kc: (cx.xb[:, kc, 0:N], [cx.Bxb[kc]])

        def add_into_xacc(cx, N):
            def consume(fc, pb, pbb):
                S.op("dve", lambda e: e.tensor_tensor(out=cx.xacc[:, fc, 0:N], in0=pb[:, 0:N], in1=cx.xacc[:, fc, 0:N], op=ALU.add),
                     [pbb, cx.Bx[fc]], [cx.Bx[fc]])
            return consume

        def layer_norm(cx, N, li, oscale_alpha):
            xa, xbt = cx.xacc, cx.xb
            s1, b1 = ring()
            s2, b2 = ring()
            mean_t, rstd_t = tmp[3], tmp[4]
            for c in range(KC):
                S.op("pe", lambda e, c=c: e.matmul(s1[:, 0:N], lhsT=ones[:], rhs=xa[:, c, 0:N], start=(c == 0), stop=(c == KC - 1)),
                     [bf("ones"), cx.Bx[c]], [b1])
            for c in range(KC):
                sq = tmp[c % 3]
                S.op("act", lambda e, c=c, sq=sq: e.activation(out=sq[:, 0:N], in_=xa[:, c, 0:N], func=AF.Square),
                     [cx.Bx[c]], [Bt[c % 3]])
                S.op("pe", lambda e, c=c, sq=sq: e.matmul(s2[:, 0:N], lhsT=ones[:], rhs=sq[:, 0:N], start=(c == 0), stop=(c == KC - 1)),
                     [bf("ones"), Bt[c % 3]], [b2])
            S.op("dve", lambda e: e.tensor_scalar(out=mean_t[:, 0:N], in0=s1[:, 0:N], scalar1=1.0 / D, scalar2=None, op0=ALU.mult),
                 [b1], [Bt[3]])
            S.op("dve", lambda e: e.tensor_tensor(out=tmp[0][:, 0:N], in0=mean_t[:, 0:N], in1=mean_t[:, 0:N], op=ALU.mult),
                 [Bt[3]], [Bt[0]])
            S.op("dve", lambda e: e.scalar_tensor_tensor(out=rstd_t[:, 0:N], in0=s2[:, 0:N], scalar=1.0 / D, in1=tmp[0][:, 0:N],
                                                         op0=ALU.mult, op1=ALU.subtract),
                 [b2, Bt[0]], [Bt[4]])
            S.op("act", lambda e: e.activation(out=rstd_t[:, 0:N], in_=rstd_t[:, 0:N], func=AF.Ln, bias=LN_EPS), [Bt[4]], [Bt[4]])
            S.op("act", lambda e: e.activation(out=rstd_t[:, 0:N], in_=rstd_t[:, 0:N], func=AF.Exp, scale=-0.5), [Bt[4]], [Bt[4]])
            gp = lnpa if oscale_alpha else lnp
            for c in range(KC):
                tm = tmp[c % 3]
                tb_ = Bt[c % 3]
                S.op("dve", lambda e, c=c, tm=tm: e.tensor_tensor(out=tm[:, 0:N], in0=xa[:, c, 0:N], in1=mean_t[:, 0:N], op=ALU.subtract),
                     [cx.Bx[c], Bt[3]], [tb_])
                S.op("dve", lambda e, c=c, tm=tm: e.tensor_tensor(out=tm[:, 0:N], in0=tm[:, 0:N], in1=rstd_t[:, 0:N], op=ALU.mult),
                     [tb_, Bt[4]], [tb_])
                gcol = li * KC + c
                bcol = 4 * KC + li * KC + c
                S.op("act", lambda e, c=c, tm=tm, gcol=gcol, bcol=bcol: e.activation(
                    out=xbt[:, c, 0:N], in_=tm[:, 0:N], func=AF.Identity, scale=lnp[:, gcol:gcol + 1], bias=lnp[:, bcol:bcol + 1]),
                    [tb_, bf("lnp")], [cx.Bxb[c]])
                S.op("act", lambda e, c=c, tm=tm, gcol=gcol, bcol=bcol: e.activation(
                    out=xa[:, c, 0:N], in_=tm[:, 0:N], func=AF.Identity, scale=gp[:, gcol:gcol + 1], bias=gp[:, bcol:bcol + 1]),
                    [tb_, bf("lnp")], [cx.Bx[c]])

        def mlp(cx, N, l):
            for g in range(4):
                def cons1(fc, pb, pbb):
                    tm = tmp[5 + fc % 3]
                    tb_ = Bt[5 + fc % 3]
                    S.op("act", lambda e: e.activation(out=tm[:, 0:N], in_=pb[:, 0:N], func=AF.Relu), [pbb], [tb_])
                    S.op("dve", lambda e: e.tensor_tensor(out=hbuf[:, fc, 0:N], in0=tm[:, 0:N], in1=tm[:, 0:N], op=ALU.mult),
                         [tb_], [Bh[fc]])
                lin_fm([tab["ff1_%d" % l][0] + g * 8 + i for i in range(8)], 2, KC, xb_rhs(cx, N), N, cons1)
                lin_fm([tab["ff2_%d" % l][0] + g * 8 + i for i in range(8)], 2, KC,
                       lambda kc: (hbuf[:, kc, 0:N], [Bh[kc]]), N, add_into_xacc(cx, N))

        def retention(cx, N, cs, nchunk, rope_col0, sample):
            xbt = cx.xb
            S.dma("sp", rope_t[:, :, 0:N], rope_d[:, :, rope_col0:rope_col0 + N], writes=[bf("rope")])
            dkoff = RH if sample else 0
            for h in range(RH):
                dcy = decay_cs[h] if sample else decay_c[h]
                S.op("act", lambda e, h=h: e.copy(out=Sbf[:, :, :], in_=Ssb[:, 2 * h:2 * h + 2, :]), [bf("S%d" % h)], [bf("Sbf")])
                for which, dst, dbuf in ((0, rq, "rq"), (1, rk, "rk")):
                    held = []

                    def cons(fc, pb, pbb, held=held):
                        held.append((pb, pbb))
                    lin_fm([tab["qk"][0] + 2 * h + which], 2, KC, xb_rhs(cx, N), N, cons)
                    (p1, b1), (p2, b2) = held
                    cosv = rope_t[:, 0, 0:N]
                    sinv = rope_t[:, 1, 0:N]
                    S.op("dve", lambda e: e.tensor_tensor(out=tmp[0][:, 0:N], in0=p1[:, 0:N], in1=cosv, op=ALU.mult), [b1, bf("rope")], [Bt[0]])
                    S.op("dve", lambda e: e.tensor_tensor(out=tmp[1][:, 0:N], in0=p2[:, 0:N], in1=sinv, op=ALU.mult), [b2, bf("rope")], [Bt[1]])
                    S.op("dve", lambda e: e.tensor_tensor(out=tmp[2][:, 0:N], in0=p1[:, 0:N], in1=sinv, op=ALU.mult), [b1, bf("rope")], [Bt[2]])
                    S.op("dve", lambda e: e.tensor_tensor(out=tmp[3][:, 0:N], in0=p2[:, 0:N], in1=cosv, op=ALU.mult), [b2, bf("rope")], [Bt[3]])
                    S.op("dve", lambda e, dst=dst: e.tensor_tensor(out=dst[:, 0, 0:N], in0=tmp[0][:, 0:N], in1=tmp[1][:, 0:N], op=ALU.subtract),
                         [Bt[0], Bt[1]], [bf(dbuf)])
                    S.op("dve", lambda e, dst=dst: e.tensor_tensor(out=dst[:, 1, 0:N], in0=tmp[2][:, 0:N], in1=tmp[3][:, 0:N], op=ALU.add),
                         [Bt[2], Bt[3]], [bf(dbuf)])
                for half in range(2):
                    S.op("dve", lambda e, half=half: e.tensor_tensor(
                        out=qh[:, half, 0:N].rearrange("p (c i) -> p c i", i=cs),
                        in0=rq[:, half, 0:N].rearrange("p (c i) -> p c i", i=cs),
                        in1=dq_t[:, h, 0:cs].unsqueeze(1).to_broadcast([128, nchunk, cs]), op=ALU.mult),
                        [bf("rq"), bf("dq")], [bf("qh")])
                for j in range(2):
                    wap, wb_ = wget(tab["v"][0] + 2 * h + j)
                    wv = wap[:].rearrange("p (k n) -> p k n", k=KC)
                    for c in range(nchunk):
                        pb, pbb = ring()
                        for kc in range(KC):
                            S.op("pe", lambda e, pb=pb, kc=kc, c=c: e.matmul(pb[0:cs, 0:256], lhsT=xbt[:, kc, c * 128:c * 128 + cs], rhs=wv[:, kc, :],
                                                                          start=(kc == 0), stop=(kc == KC - 1)), [wb_, cx.Bxb[kc]], [pbb])
                        S.op("act", lambda e, pb=pb, c=c, j=j: e.copy(out=vsb[0:cs, c, j * 256:(j + 1) * 256], in_=pb[0:cs, 0:256]), [pbb], [bf("vsb")])

                def consg(fc, pb, pbb):
                    S.op("act", lambda e: e.activation(out=sg[:, fc, 0:N], in_=pb[:, 0:N], func=AF.Silu), [pbb], [bf("sg")])
                lin_fm([tab["g"][0] + 2 * h, tab["g"][0] + 2 * h + 1], 2, KC, xb_rhs(cx, N), N, consg)
                wprefetch()
                for c in range(nchunk):
                    cr = slice(c * 128, c * 128 + cs)
                    pb, pbb = ring()
                    for half in range(2):
                        S.op("pe", lambda e, half=half: e.matmul(pb[0:cs, 0:cs], lhsT=rk[:, half, cr], rhs=rq[:, half, cr], start=(half == 0), stop=(half == 1)),
                             [bf("rk"), bf("rq")], [pbb])
                    S.op("dve", lambda e: e.tensor_tensor(out=scb[0:cs, 0:cs], in0=pb[0:cs, 0:cs], in1=dt_t[0:cs, h, 0:cs], op=ALU.mult),
                         [pbb, bf("dt")], [bf("scb")])
                    S.op("pe", lambda e: e.matmul(accA[0:cs, :], lhsT=scb[0:cs, 0:cs], rhs=vsb[0:cs, c, :], start=True, stop=False),
                         [bf("scb"), bf("vsb")], [bf("accA")])
                    for half in range(2):
                        S.op("pe", lambda e, half=half: e.matmul(accA[0:cs, :], lhsT=qh[:, half, cr], rhs=Sbf[:, half, :], start=False, stop=(half == 1)),
                             [bf("qh"), bf("Sbf")], [bf("accA")])
                    for half in range(2):
                        S.op("pe", lambda e, half=half: e.transpose(tpb[0:cs, half * 128:(half + 1) * 128], rk[:, half, cr], identb[:, :]),
                             [bf("rk"), bf("identb")], [bf("tpbA")])
                    S.op("act", lambda e: e.activation(out=kd[0:cs, :], in_=tpb[0:cs, 0:256], func=AF.Identity, scale=dk_t[0:cs, dkoff + h:dkoff + h + 1], bias=0.0),
                         [bf("tpbA"), bf("dk")], [bf("kd")])
                    for half in range(2):
                        pd, pdb = ring()
                        S.op("pe", lambda e, half=half, pd=pd: e.matmul(pd[:, :], lhsT=kd[0:cs, half * 128:(half + 1) * 128], rhs=vsb[0:cs, c, :], start=True, stop=True),
                             [bf("kd"), bf("vsb")], [pdb])
                        S.op("dve", lambda e, half=half, pd=pd: e.scalar_tensor_tensor(out=Ssb[:, 2 * h + half, :], in0=Ssb[:, 2 * h + half, :], scalar=dcy, in1=pd[:, :],
                                                                                      op0=ALU.mult, op1=ALU.add), [pdb, bf("S%d" % h)], [bf("S%d" % h)])
                    S.op("dve", lambda e: e.bn_stats(out=stats[0:cs, :], in_=accA[0:cs, :]), [bf("accA")], [bf("stats")])
                    S.op("dve", lambda e: e.bn_aggr(out=mv[0:cs, :], in_=stats[0:cs, :]), [bf("stats")], [bf("mv")])
                    S.op("act", lambda e: e.activation(out=rstd[0:cs, :], in_=mv[0:cs, 1:2], func=AF.Ln, bias=GN_EPS), [bf("mv")], [bf("rstd")])
                    S.op("act", lambda e: e.activation(out=rstd[0:cs, :], in_=rstd[0:cs, :], func=AF.Exp, scale=-0.5), [bf("rstd")], [bf("rstd")])
                    S.op("dve", lambda e: e.tensor_scalar(out=onb[0:cs, :], in0=accA[0:cs, :], scalar1=mv[0:cs, 0:1], scalar2=rstd[0:cs, 0:1],
                                                          op0=ALU.subtract, op1=ALU.mult), [bf("accA"), bf("mv"), bf("rstd")], [bf("onb")])
                    if c + 1 < nchunk:
                        S.op("act", lambda e: e.copy(out=Sbf[:, :, :], in_=Ssb[:, 2 * h:2 * h + 2, :]), [bf("S%d" % h)], [bf("Sbf")])
                    for vc in range(4):
                        S.op("pe", lambda e, vc=vc: e.transpose(tpb[:, 512 + vc * 128:512 + vc * 128 + cs], onb[0:cs, vc * 128:(vc + 1) * 128], identb[0:cs, 0:cs]),
                             [bf("onb"), bf("identb")], [bf("tpbB")])
                    S.op("dve", lambda e: e.tensor_tensor(out=og[:, :, cr], in0=tpb[:, 512:1024].rearrange("p (v i) -> p v i", v=4)[:, :, 0:cs],
                                                          in1=sg[:, :, cr], op=ALU.mult), [bf("tpbB"), bf("sg")], [bf("og")])
                for j in range(2):
                    wap, wb_ = wget(tab["out"][0] + 2 * h + j)
                    wv = wap[:].rearrange("p (f k m) -> p f k m", f=8, k=4)
                    for fl in range(8):
                        fc = 8 * j + fl
                        pb, pbb = ring()
                        for kc in range(4):
                            S.op("pe", lambda e, pb=pb, fl=fl, kc=kc: e.matmul(pb[:, 0:N], lhsT=wv[:, fl, kc, :], rhs=og[:, kc, 0:N], start=(kc == 0), stop=(kc == 3)),
                                 [wb_, bf("og")], [pbb])
                        add_into_xacc(cx, N)(fc, pb, pbb)

        def kv_proj(cx, N, cs, nchunk, t, sample):
            xbt = cx.xb

            def consk(fc, pb, pbb):
                i = fc % 2
                S.op("act", lambda e: e.copy(out=tmp[i][:, 0:N], in_=pb[:, 0:N]), [pbb], [Bt[i]])
                if sample:
                    S.op("dve", lambda e: e.tensor_copy(out=ksn[:, fc, :], in_=tmp[i][:, 0:N]), [Bt[i]], [bf("ksn")])
                    S.dma("sp", ksTo[:, fc * NS:(fc + 1) * NS], tmp[i][:, 0:N], reads=[Bt[i]], writes=[bf("o_ksT")])
                else:
                    S.op("dve", lambda e: e.tensor_copy(out=kb16[i][:, 0:N], in_=tmp[i][:, 0:N]), [Bt[i]], [bf("kb16_%d" % i)])
                    S.dma("sp", kTo[fc, :, t * TT:t * TT + N], tmp[i][:, 0:N], reads=[Bt[i]], writes=[bf("o_kT")])
                    S.dma("sp", Ks[fc, :, t * TT:t * TT + N], kb16[i][:, 0:N], reads=[bf("kb16_%d" % i)], writes=[bf("Ks%d" % t)])
            lin_fm([tab["k"][0] + i for i in range(8)], 2, KC, xb_rhs(cx, N), N, consk)
            cnt = 0
            for j in range(8):
                wap, wb_ = wget(tab["v2"][0] + j)
                wv = wap[:].rearrange("p (k n) -> p k n", k=KC)
                for c in range(nchunk):
                    pb, pbb = ring()
                    for kc in range(KC):
                        S.op("pe", lambda e, pb=pb, kc=kc, c=c: e.matmul(pb[0:cs, 0:256], lhsT=xbt[:, kc, c * 128:c * 128 + cs], rhs=wv[:, kc, :],
                                                                      start=(kc == 0), stop=(kc == KC - 1)), [wb_, cx.Bxb[kc]], [pbb])
                    i = cnt % 2
                    cnt += 1
                    S.op("act", lambda e, pb=pb, i=i: e.copy(out=vf[i][0:cs, :], in_=pb[0:cs, 0:256]), [pbb], [bf("vf%d" % i)])
                    if sample:
                        S.op("dve", lambda e, i=i, j=j: e.tensor_copy(out=vsn[0:cs, j * 256:(j + 1) * 256], in_=vf[i][0:cs, :]), [bf("vf%d" % i)], [bf("vsn")])
                        S.dma("sp", vso[0:cs, j * 256:(j + 1) * 256], vf[i][0:cs, :], reads=[bf("vf%d" % i)], writes=[bf("o_vs")])
                    else:
                        S.op("dve", lambda e, i=i: e.tensor_copy(out=vb16[i][0:cs, :], in_=vf[i][0:cs, :]), [bf("vf%d" % i)], [bf("vb16_%d" % i)])
                        r0 = t * TT + c * 128
                        S.dma("sp", vo[r0:r0 + cs, j * 256:(j + 1) * 256], vf[i][0:cs, :], reads=[bf("vf%d" % i)], writes=[bf("o_v")])
                        kb = t * 4 + c
                        for hh in range(2):
                            S.dma("sp", Vs[2 * j + hh, 0:cs, kb * 128:(kb + 1) * 128],
                                  vb16[i][0:cs, hh * 128:(hh + 1) * 128], reads=[bf("vb16_%d" % i)], writes=[bf("Vs%d" % t)])

        def sb_attention_prompt(cx, j):
            N = TT
            KTh = kvb[:, 0:SEQ]
            Vh = kvb[:, NKB * 128:2 * NKB * 128]
            ksb = [bf("Ks%d" % t) for t in range(NT)]
            vsb_ = [bf("Vs%d" % t) for t in range(NT)]
            e_t = [tmp[0], tmp[1], tmp[2]]
            sp_t = [tmp[3], tmp[4], tmp[5]]
            Be = [Bt[0], Bt[1], Bt[2]]
            Bsp = [Bt[3], Bt[4], Bt[5]]
            dqm, lacc = sg[:, 0, :], sg[:, 1, :]
            u_t = [sg[:, 2, :], sg[:, 3, :]]
            Bdqm, Blacc, Bu = bf("sg"), bf("lacc"), [bf("u0"), bf("u1")]
            S.op("dve", lambda e: e.tensor_scalar(out=dqm, in0=misc_t[:, 384:896], scalar1=pcf[:, j:j + 1], scalar2=None, op0=ALU.add),
                 [bf("misc"), bf("pcf")], [Bdqm])
            kbs = list(reversed(range(NKB)))
            for hp in range(8):
                def consq(fc, pb, pbb):
                    S.op("act", lambda e: e.copy(out=qT2[:, fc, 0:N], in_=pb[:, 0:N]), [pbb], [bf("qT2")])
                lin_fm([tab["q"][0] + hp], 2, KC, xb_rhs(cx, N), N, consq)
                wprefetch()
                for jj in range(2):
                    h = 2 * hp + jj
                    S.dma("sp", KTh, Ks[h], reads=ksb, writes=[bf("KTh")])
                    S.dma("sp", Vh, Vs[h], reads=vsb_, writes=[bf("Vh")])
                    S.op("dve", lambda e: e.memset(lacc, 0.0), [], [Blacc])

                    def stage1(n):
                        kb = kbs[n]
                        i3 = n % 3
                        z, zb = ring()
                        S.op("pe", lambda e: e.matmul(z[:, :], lhsT=KTh[:, kb * 128:(kb + 1) * 128], rhs=qT2[:, jj, :], start=True, stop=True),
                             [bf("KTh"), bf("qT2")], [zb])
                        S.op("act", lambda e: e.activation(out=e_t[i3], in_=z[:, :], func=AF.Exp, scale=scale, bias=sbb[:, h:h + 1]),
                             [zb, bf("sbb")], [Be[i3]])
                        S.op("dve", lambda e: e.scalar_tensor_tensor(out=e_t[i3], in0=dqm, scalar=float(kb * 128), in1=e_t[i3],
                                                                     op0=ALU.is_gt, op1=ALU.mult), [Bdqm, Be[i3]], [Be[i3]])
                        S.op("act", lambda e: e.activation(out=sp_t[i3], in_=e_t[i3], func=AF.Ln, bias=1.0), [Be[i3]], [Bsp[i3]])

                    def stage2(n):
                        i3, i2 = n % 3, n % 2
                        a, ab = ring()
                        S.op("pe", lambda e: e.matmul(a[:, :], lhsT=misc_t[:, 128:256], rhs=sp_t[i3], start=True, stop=False),
                             [bf("misc"), Bsp[i3]], [ab])
                        S.op("pe", lambda e: e.matmul(a[:, :], lhsT=ones[:, :], rhs=lacc, start=False, stop=True), [bf("ones"), Blacc], [ab])
                        S.op("act", lambda e: e.activation(out=u_t[i2], in_=a[:, :], func=AF.Exp, scale=-1.0), [ab], [Bu[i2]])
                        S.op("dve", lambda e: e.tensor_tensor(out=w_t[i2][:, :], in0=e_t[i3], in1=u_t[i2], op=ALU.mult),
                             [Be[i3], Bu[i2]], [bf("w%d" % i2)])
                        S.op("dve", lambda e: e.tensor_tensor(out=lacc, in0=lacc, in1=sp_t[i3], op=ALU.add),
                             [Blacc, Bsp[i3]], [Blacc])

                    def stage3(n):
                        kb = kbs[n]
                        i2 = n % 2
                        S.op("pe", lambda e: e.matmul(accB[:, :], lhsT=Vh[:, kb * 128:(kb + 1) * 128], rhs=w_t[i2][:, :], start=(n == 0), stop=(n == NKB - 1)),
                             [bf("Vh"), bf("w%d" % i2)], [bf("accB")])

                    for n in range(NKB + 2):
                        if n < NKB:
                            stage1(n)
                        if 0 <= n - 1 < NKB:
                            stage2(n - 1)
                        if 0 <= n - 2 < NKB:
                            stage3(n - 2)
                    S.op("act", lambda e: e.copy(out=hbuf[:, h, :], in_=accB[:, :]), [bf("accB")], [Bh[h]])

        def layer1_tail(cx, N, ob_rhs):
            lin_fm([tab["o"][0] + i for i in range(8)], 2, KC, ob_rhs, N, add_into_xacc(cx, N))
            layer_norm(cx, N, 2, True)
            mlp(cx, N, 1)
            layer_norm(cx, N, 3, False)

        def sb_attention_sample(cx):
            N = NS
            P_ = NPG
            Eall = xacc[:].rearrange("p c n -> p (c n)")[:, 0:PAGE * HQ].rearrange("p (r q) -> p r q", q=HQ)
            Aall = Ssb[:].rearrange("p c n -> p (c n)")[:, 0:PAGE * HQ].rearrange("p (r q) -> p r q", q=HQ)
            Wall = xb[:].rearrange("p c n -> p (c n)")[:, 0:PAGE * HQ].rearrange("p (r q) -> p r q", q=HQ)
            BE, BA, BW = bf("Eall"), bf("Aall"), bf("Wall")
            allS = [bf("S%d" % h) for h in range(RH)]
            hflat = hbuf[:].rearrange("p c n -> p (c n)")
            Kb = hflat[:, 0:D]
            KT = hflat[:, D:2 * D].rearrange("p (h g) -> p h g", h=SH)
            Vb = hflat[:, 2 * D:3 * D]
            pflat = pool[:].rearrange("p c n -> p (c n)")
            gat = [pflat[:, 0:D], pflat[:, D:2 * D]]
            Bg = [bf("gat0"), bf("gat1")]

            def consq(fc, pb, pbb):
                S.op("act", lambda e: e.copy(out=qTs[:, fc, :], in_=pb[:, 0:N]), [pbb], [bf("qTs")])
            lin_fm([tab["q"][0] + i for i in range(8)], 2, KC, xb_rhs(cx, N), N, consq)
            wprefetch()
            S.dma("sp", pti[0:P_, :], pt[:, :], writes=[bf("pti")])
            S.op("dve", lambda e: e.tensor_copy(out=ptf[0:P_, :], in_=pti[0:P_, :]), [bf("pti")], [bf("ptf")])
            S.op("dve", lambda e: e.tensor_scalar(out=ptf[0:P_, :], in0=ptf[0:P_, :], scalar1=float(PAGE), scalar2=None, op0=ALU.mult), [bf("ptf")], [bf("ptf")])
            S.op("dve", lambda e: e.tensor_scalar(out=idxf[0:P_, :], in0=misc_t[0:P_, 896 + HQ:896 + HQ + PAGE], scalar1=ptf[0:P_, 0:1], scalar2=None, op0=ALU.add),
                 [bf("ptf"), bf("misc")], [bf("idxf")])
            S.op("dve", lambda e: e.tensor_copy(out=idxi[0:P_, :], in_=idxf[0:P_, :]), [bf("idxf")], [bf("pti")])
            S.op("dve", lambda e: e.tensor_copy(out=sbb64[:].rearrange("p (h q) -> p h q", h=SH), in_=sbb[:].unsqueeze(2).to_broadcast([128, SH, NS])),
                 [bf("sbb")], [bf("sbb64")])
            for r in range(PAGE):
                g = r % 2
                S.dma("pool", None, None, reads=[bf("pti")], writes=[Bg[g]] + (Bt if r < 2 else []),
                      fn=lambda e, g=g, r=r: e.indirect_dma_start(out=gat[g][0:P_, :], out_offset=None, in_=cache_k[:, :],
                                                                  in_offset=bass.IndirectOffsetOnAxis(ap=idxi[0:P_, r:r + 1], axis=0)))
                if r % 2 == 0:
                    S.op("act", lambda e, g=g: e.copy(out=Kb[0:P_, :], in_=gat[g][0:P_, :]), [Bg[g]], [bf("Kb")] + (Bh if r == 0 else []))
                else:
                    S.op("dve", lambda e, g=g: e.tensor_copy(out=Kb[0:P_, :], in_=gat[g][0:P_, :]), [Bg[g]], [bf("Kb")])
                z, zb = ring()
                for hh in range(2):
                    for hl in range(8):
                        h = hh * 8 + hl
                        S.op("pe", lambda e, hl=hl, h=h: e.transpose(tpb[:, hl * 128:hl * 128 + P_], Kb[0:P_, h * 128:(h + 1) * 128], identb[0:P_, 0:P_]),
                             [bf("Kb"), bf("identb")], [bf("tpbA"), bf("tpbB")])
                    S.op("act" if hh == 0 else "dve",
                         (lambda e, hh=hh: e.copy(out=KT[:, hh * 8:(hh + 1) * 8, 0:P_], in_=tpb[:, :].rearrange("p (h g) -> p h g", h=8)[:, :, 0:P_])) if hh == 0 else
                         (lambda e, hh=hh: e.tensor_copy(out=KT[:, hh * 8:(hh + 1) * 8, 0:P_], in_=tpb[:, :].rearrange("p (h g) -> p h g", h=8)[:, :, 0:P_])),
                         [bf("tpbA"), bf("tpbB")], [bf("KT%d" % hh)] + (Bh if r == 0 else []))
                    for hl in range(8):
                        h = hh * 8 + hl
                        S.op("pe", lambda e, h=h: e.matmul(z[0:P_, h * NS:(h + 1) * NS], lhsT=KT[:, h, 0:P_], rhs=qTs[:, h, :], start=True, stop=True),
                             [bf("KT%d" % hh), bf("qTs")], [zb])
                S.op("dve", lambda e, r=r: e.scalar_tensor_tensor(out=Eall[0:P_, r, :], in0=z[0:P_, 0:HQ], scalar=scale, in1=sbb64[0:P_, :], op0=ALU.mult, op1=ALU.add),
                     [zb, bf("sbb64")], [BE] + Bx)
            S.op("act", lambda e: e.activation(out=Eall[0:P_], in_=Eall[0:P_], func=AF.Exp), [BE], [BE])
            S.op("act", lambda e: e.activation(out=Aall[0:P_], in_=Eall[0:P_], func=AF.Ln, bias=1.0), [BE] + allS, [BA] + allS)
            S.op("dve", lambda e: e.tensor_reduce(out=Tt[0:P_, :], in_=Aall[0:P_].rearrange("p r q -> p q r"), axis=mybir.AxisListType.X, op=ALU.add),
                 [BA], [bf("Tt")])
            zn, znb = ring()
            for h in range(SH):
                S.op("pe", lambda e, h=h: e.matmul(zn[0:NS, h * NS:(h + 1) * NS], lhsT=ksn[:, h, :], rhs=qTs[:, h, :], start=True, stop=True),
                     [bf("ksn"), bf("qTs")], [znb])
            S.op("dve", lambda e: e.scalar_tensor_tensor(out=en[0:NS, :], in0=zn[0:NS, 0:HQ], scalar=scale, in1=sbb64[0:NS, :], op0=ALU.mult, op1=ALU.add),
                 [znb, bf("sbb64")], [bf("en")])
            S.op("act", lambda e: e.activation(out=en[0:NS, :], in_=en[0:NS, :], func=AF.Exp), [bf("en")], [bf("en")])
            S.op("dve", lambda e: e.tensor_tensor(out=en[0:NS, :], in0=en[0:NS, :], in1=misc_t[0:NS, 896:896 + HQ], op=ALU.mult), [bf("en"), bf("misc")], [bf("en")])
            S.op("act", lambda e: e.activation(out=spn[0:NS, :], in_=en[0:NS, :], func=AF.Ln, bias=1.0), [bf("en")], [bf("spn")])
            ct, ctb = ring()
            S.op("pe", lambda e: e.matmul(ct[0:P_, 0:HQ], lhsT=misc_t[0:P_, 256:256 + P_], rhs=Tt[0:P_, :], start=True, stop=False), [bf("misc"), bf("Tt")], [ctb])
            S.op("pe", lambda e: e.matmul(ct[0:P_, 0:HQ], lhsT=ones[0:NS, 0:P_], rhs=spn[0:NS, :], start=False, stop=True), [bf("ones"), bf("spn")], [ctb])
            S.op("dve", lambda e: e.tensor_tensor(out=Aall[0:P_, PAGE - 1, :], in0=Aall[0:P_, PAGE - 1, :], in1=ct[0:P_, 0:HQ], op=ALU.add), [BA, ctb], [BA])
            sh = 1
            while sh < PAGE:
                S.op("dve", lambda e, sh=sh: e.tensor_tensor(out=Aall[0:P_, 0:PAGE - sh, :], in0=Aall[0:P_, 0:PAGE - sh, :], in1=Aall[0:P_, sh:PAGE, :], op=ALU.add),
                     [BA], [BA])
                sh *= 2
            S.op("act", lambda e: e.activation(out=Aall[0:P_], in_=Aall[0:P_], func=AF.Exp, scale=-1.0), [BA], [BA])
            S.op("dve", lambda e: e.tensor_tensor(out=Wall[0:P_], in0=Eall[0:P_], in1=Aall[0:P_], op=ALU.mult), [BE, BA] + Bxb, [BW] + Bxb)
            an, anb = ring()
            S.op("pe", lambda e: e.matmul(an[0:NS, 0:HQ], lhsT=misc_t[0:NS, 128:128 + NS], rhs=spn[0:NS, :], start=True, stop=True), [bf("misc"), bf("spn")], [anb])
            S.op("act", lambda e: e.activation(out=un[0:NS, :], in_=an[0:NS, 0:HQ], func=AF.Exp, scale=-1.0), [anb], [bf("un")])
            S.op("dve", lambda e: e.tensor_tensor(out=wn[0:NS, :], in0=en[0:NS, :], in1=un[0:NS, :], op=ALU.mult), [bf("en"), bf("un")], [bf("wn")])
            for r in range(PAGE + 1):
                o_, ob_ = ring()
                if r < PAGE:
                    g = r % 2
                    S.dma("pool", None, None, reads=[bf("pti")], writes=[Bg[g]],
                          fn=lambda e, g=g, r=r: e.indirect_dma_start(out=gat[g][0:P_, :], out_offset=None, in_=cache_v[:, :],
                                                                      in_offset=bass.IndirectOffsetOnAxis(ap=idxi[0:P_, r:r + 1], axis=0)))
                    if r % 2 == 0:
                        S.op("act", lambda e, g=g: e.copy(out=Vb[0:P_, :], in_=gat[g][0:P_, :]), [Bg[g]], [bf("Vb")] + (Bh if r == 0 else []))
                    else:
                        S.op("dve", lambda e, g=g: e.tensor_copy(out=Vb[0:P_, :], in_=gat[g][0:P_, :]), [Bg[g]], [bf("Vb")])
                    for h in range(SH):
                        S.op("pe", lambda e, h=h, r=r: e.matmul(o_[:, h * NS:(h + 1) * NS], lhsT=Vb[0:P_, h * 128:(h + 1) * 128], rhs=Wall[0:P_, r, h * NS:(h + 1) * NS],
                                                                start=True, stop=True), [bf("Vb"), BW], [ob_])
                else:
                    for h in range(SH):
                        S.op("pe", lambda e, h=h: e.matmul(o_[:, h * NS:(h + 1) * NS], lhsT=vsn[0:NS, h * 128:(h + 1) * 128], rhs=wn[0:NS, h * NS:(h + 1) * NS],
                                                           start=True, stop=True), [bf("vsn"), bf("wn")], [ob_])
                if r == 0:
                    S.op("dve", lambda e: e.tensor_copy(out=oacc[:, :], in_=o_[:, 0:HQ]), [ob_], [bf("oacc")])
                else:
                    S.op("dve", lambda e: e.tensor_tensor(out=oacc[:, :], in0=oacc[:, :], in1=o_[:, 0:HQ], op=ALU.add), [ob_, bf("oacc")], [bf("oacc")])
            S.op("act", lambda e: e.copy(out=obs[:].rearrange("p h q -> p (h q)"), in_=oacc[:, :]), [bf("oacc")], [bf("obs")])

        @blk.sync
        def _(sync):
            S.dma("sp", dt_t[:].rearrange("p h i -> p (h i)"), dt_d[:, :], writes=[bf("dt")])
            S.dma("sp", dq_t[:].rearrange("p h i -> p (h i)"), dq_d[:, :], writes=[bf("dq")])
            S.dma("sp", dk_t[:], dk_d[:, :], writes=[bf("dk")])
            S.dma("sp", misc_t[:], misc_d[:, :], writes=[bf("misc")])
            S.dma("sp", lnp[:], lnp_d[:, :], writes=[bf("lnp")])
            S.dma("sp", sbb[:], sbb_d.partition_broadcast(128), writes=[bf("sbb")])
            S.dma("sp", pcf[:], pcf_d[:, :], writes=[bf("pcf")])
            S.dma("sp", pci[:], pci_d[:, :], writes=[bf("pci")])
            S.op("dve", lambda e: e.tensor_copy(out=identb[:], in_=misc_t[:, 0:128]), [bf("misc")], [bf("identb")])
            S.op("dve", lambda e: e.memset(ones[:], 1.0), [], [bf("ones")])
            S.op("dve", lambda e: e.tensor_scalar(out=lnpa[:], in0=lnp[:], scalar1=ALPHA, scalar2=None, op0=ALU.mult), [bf("lnp")], [bf("lnp")])

            def load_x(cx, src, N, c0):
                S.dma("sp", cx.xacc[:, :, 0:N], src[:, :, c0:c0 + N], writes=cx.Bx[0:1] if cx is ctxS else cx.Bx)
                S.dma("pool", cx.xb[:, :, 0:N], src[:, :, c0:c0 + N], writes=cx.Bxb[0:1] if cx is ctxS else cx.Bxb)
                for c in range(KC):
                    S.op("act", lambda e, c=c: e.mul(out=cx.xacc[:, c, 0:N], in_=cx.xacc[:, c, 0:N], mul=ALPHA), [cx.Bx[c]], [cx.Bx[c]])

            def layer0(cx, N, cs, nchunk, t, sample):
                retention(cx, N, cs, nchunk, (SEQ if sample else t * TT), sample)
                layer_norm(cx, N, 0, True)
                if stop_after == "ln0":
                    return
                mlp(cx, N, 0)
                layer_norm(cx, N, 1, True)
                if stop_after == "mlp0":
                    return
                kv_proj(cx, N, cs, nchunk, t, sample)

            for h in range(RH):
                S.op("dve", lambda e, h=h: e.memset(Ssb[:, 2 * h:2 * h + 2, :], 0.0), [], [bf("S%d" % h)])
            for t in range(NT):
                load_x(ctxP, xT, TT, t * TT)
                if stop_after == "ret0":
                    retention(ctxP, TT, 128, 4, 0, False)
                    S.dma("sp", yT[0], xacc[:].rearrange("p c n -> p (c n)"), reads=Bx, writes=[bf("o_y")])
                    S.finish(list(B.values()))
                    return
                layer0(ctxP, TT, 128, 4, t, False)
                if stop_after in ("t0", "ln0", "mlp0"):
                    S.dma("sp", yT[0], xacc[:].rearrange("p c n -> p (c n)"), reads=Bx, writes=[bf("o_y")])
                    S.finish(list(B.values()))
                    return
                S.dma("sp", X1s[t * 128:(t + 1) * 128, :], xacc[:].rearrange("p c n -> p (c n)"), reads=Bx, writes=[bf("X1s%d" % t)])
                S.dma("sp", XBs[t * 128:(t + 1) * 128, :], xb[:].rearrange("p c n -> p (c n)"), reads=Bxb, writes=[bf("XBs%d" % t)])
            for h in range(RH):
                S.dma("sp", retp[h].rearrange("(c p) v -> p c v", p=128), Ssb[:, 2 * h:2 * h + 2, :], reads=[bf("S%d" % h)], writes=[bf("o_retp")])
            if stop_after == "l0":
                S.finish(list(B.values()))
                return
            for j in range(NOWN):
                x1b = [bf("X1s%d" % t) for t in range(NT)]
                xbb = [bf("XBs%d" % t) for t in range(NT)]
                S.dma("pool", None, None, reads=x1b + [bf("pci")], writes=Bx,
                      fn=lambda e, j=j: e.indirect_dma_start(out=xacc[:].rearrange("p c n -> p (c n)"), out_offset=None, in_=X1s[:, :],
                                                             in_offset=bass.IndirectOffsetOnAxis(ap=pci[:, j:j + 1], axis=0)))
                S.dma("pool", None, None, reads=xbb + [bf("pci")], writes=Bxb,
                      fn=lambda e, j=j: e.indirect_dma_start(out=xb[:].rearrange("p c n -> p (c n)"), out_offset=None, in_=XBs[:, :],
                                                             in_offset=bass.IndirectOffsetOnAxis(ap=pci[:, j:j + 1], axis=0)))
                sb_attention_prompt(ctxP, j)
                layer1_tail(ctxP, TT, lambda kc: (hbuf[:, kc, 0:TT], [Bh[kc]]))
                S.dma("sp", yT[j], xacc[:].rearrange("p c n -> p (c n)"), reads=Bx, writes=[bf("o_y")])
            if stop_after is None:
                for h in range(RH):
                    S.dma("sp", Ssb[:, 2 * h:2 * h + 2, :], state[h].rearrange("(c p) v -> p c v", p=128), writes=[bf("S%d" % h)])
                load_x(ctxS, xsT, NS, 0)
                layer0(ctxS, NS, NS, 1, 0, True)
                for h in range(RH):
                    S.dma("sp", rets[h].rearrange("(c p) v -> p c v", p=128), Ssb[:, 2 * h:2 * h + 2, :], reads=[bf("S%d" % h)], writes=[bf("o_rets")])
                sb_attention_sample(ctxS)
                layer1_tail(ctxS, NS, lambda kc: (obs[:, kc, :], [bf("obs")]))
                S.dma("sp", ysT[:, :].rearrange("p (c n) -> p c n", c=KC), xsacc[:, :, :], reads=ctxS.Bx[0:1], writes=[bf("o_ys")])
            S.finish(list(B.values()))
        print("instructions:", S.n_inst, "waits:", S.n_wait, "pieces:", wpos[0], "/", len(plan), flush=True)
    return nc


_PROG_CACHE = {}
_LAST_RES = None


def _run(inputs, SEQ, NS, NPG, stop_after=None):
    x_prompt = np.asarray(inputs["x_prompt"], np.float32)
    x_sample = np.asarray(inputs["x_sample"], np.float32)
    state_ret = np.asarray(inputs["state_ret"], np.float32)
    cache_k = np.asarray(inputs["cache_k"], np.float32)
    cache_v = np.asarray(inputs["cache_v"], np.float32)
    page_table = np.asarray(inputs["page_table"], np.int32)
    NPOOL = cache_k.shape[0]
    BATCH = x_prompt.shape[0]
    NT = SEQ // TT
    NOWN = NT // 4
    key = (SEQ, NS, NPG, NPOOL, stop_after)
    if key not in _PROG_CACHE:
        _PROG_CACHE[key] = build_program(SEQ, NS, NPG, NPOOL, stop_after)
    nc = _PROG_CACHE[key]
    Wp = pack_weights(*[np.asarray(inputs[k], np.float32) for k in ("w_ret_in", "w_ret_out", "w_kv", "w_sb_q", "w_sb_o", "w_ff1", "w_ff2")])
    rope, dtab, dqtab, dktab, misc, _, _ = const_tables(SEQ, NS, NPG * PAGE)
    ln_g = np.asarray(inputs["ln_g"], np.float32).reshape(4, KC, 128)
    ln_b = np.asarray(inputs["ln_b"], np.float32).reshape(4, KC, 128)
    lnp = np.concatenate([ln_g.transpose(2, 0, 1).reshape(128, 4 * KC), ln_b.transpose(2, 0, 1).reshape(128, 4 * KC)], axis=1)
    sbb = np.asarray(inputs["sb_bias"], np.float32).reshape(1, SH)
    ck = cache_k.reshape(NPOOL * PAGE, D)
    cv = cache_v.reshape(NPOOL * PAGE, D)
    in_maps = []
    for c in range(NCORE):
        b, s = (c // 4) % BATCH, c % 4
        xT = np.ascontiguousarray(x_prompt[b].reshape(SEQ, KC, 128).transpose(2, 1, 0))
        xsT = np.ascontiguousarray(x_sample[c].reshape(NS, KC, 128).transpose(2, 1, 0))
        own = [s * NOWN + j for j in range(NOWN)]
        pcf = np.broadcast_to(np.array([o * TT for o in own], np.float32)[None], (128, NOWN)).copy()
        pci = (np.array(own, np.int32)[None, :] * 128 + np.arange(128, dtype=np.int32)[:, None]).astype(np.int32)
        in_maps.append(dict(xT=xT, xsT=xsT, state=np.ascontiguousarray(state_ret[0, c]), cache_k=ck, cache_v=cv,
                            pt=np.ascontiguousarray(page_table[c].reshape(NPG, 1)), W=Wp, rope=rope, dtab=dtab, dqtab=dqtab,
                            dktab=dktab, misc=misc, lnp=lnp, sbb=sbb, pcf=pcf, pci=pci))
    import os as _os
    _n = int(_os.environ.get("KNCORE", NCORE))
    if _os.environ.get("KTRACE"):
        res = run_bass_kernel_spmd(nc, in_maps[:_n], core_ids=list(range(_n)), trace=True)
        print("KTRACE exec_time_ns", res.exec_time_ns, flush=True)
        global _LAST_RES
        _LAST_RES = res
    else:
        res = run_bass_kernel_spmd(nc, in_maps[:_n], core_ids=list(range(_n)))
    R = list(res.results) + [res.results[0]] * (NCORE - _n)
    y_prompt = np.zeros((BATCH, SEQ, D), np.float32)
    for c in range(NCORE):
        b, s = (c // 4) % BATCH, c % 4
        for j in range(NOWN):
            t = s * NOWN + j
            y_prompt[b, t * TT:(t + 1) * TT] = R[c]["yT"][j].reshape(128, KC, TT).transpose(2, 1, 0).reshape(TT, D)
    y_sample = np.stack([R[c]["ysT"].reshape(128, KC, NS).transpose(2, 1, 0).reshape(NS, D) for c in range(NCORE)])
    ret_prompt = np.stack([R[4 * b]["retp"] for b in range(BATCH)])[None]
    k_prompt = np.stack([R[4 * b]["kTo"].transpose(2, 0, 1) for b in range(BATCH)])
    v_prompt = np.stack([R[4 * b]["vo"].reshape(SEQ, SH, 128) for b in range(BATCH)])
    ret_sample = np.stack([R[c]["rets"] for c in range(NCORE)])[None]
    k_sample = np.stack([R[c]["ksTo"].reshape(128, SH, NS).transpose(2, 1, 0) for c in range(NCORE)])
    v_sample = np.stack([R[c]["vso"].reshape(NS, SH, 128) for c in range(NCORE)])
    return (y_prompt, y_sample, np.ascontiguousarray(ret_prompt), np.ascontiguousarray(k_prompt), v_prompt,
            np.ascontiguousarray(ret_sample), np.ascontiguousarray(k_sample), v_sample)


def kernel(**inputs):
    SEQ = inputs["x_prompt"].shape[1]
    NS = inputs["x_sample"].shape[1]
    NPG = inputs["page_table"].shape[1]
    return _run(inputs, SEQ, NS, NPG)
```

```python
import contextlib
import math
import numpy as np
import concourse.bass as bass
import concourse.mybir as mybir
from concourse.bass_utils import run_bass_kernel_spmd

F32 = mybir.dt.float32
BF16 = mybir.dt.bfloat16
I32 = mybir.dt.int32
AF = mybir.ActivationFunctionType
ALU = mybir.AluOpType

D = 2048
KC = 16
RH = 8
SH = 16
DFF = 8192
TT = 512
LN_EPS = 1e-5
GN_EPS = 1e-6
DEPTH = 2
ALPHA = (2.0 * DEPTH) ** 0.25
ROPE_BASE = 10000.0
PAGE = 128
NCORE = 8
PIECE = 4096


class Buf:
    __slots__ = ("name", "w", "r")

    def __init__(self, name):
        self.name = name
        self.w = {}
        self.r = {}


class Sched:
    NDMA = 8

    def __init__(self, nc, stack):
        self.nc = nc
        self.eng = {"pe": nc.tensor, "act": nc.scalar, "dve": nc.vector, "pool": nc.gpsimd, "sp": nc.sync}
        self.sem = {}
        self.cnt = {}
        self.waited = {k: {} for k in self.eng}
        for k in ("pe", "act", "dve", "pool"):
            self.sem[k] = stack.enter_context(nc.semaphore("sem_" + k))
            self.cnt[k] = 0
        self.dsem = {}
        self.dn = {}
        for q in ("sp", "act", "pool"):
            self.dsem[q] = [stack.enter_context(nc.semaphore("dsem_%s_%d" % (q, i))) for i in range(self.NDMA)]
            self.dn[q] = 0
        self.n_inst = 0
        self.n_wait = 0

    def _wait(self, e, need):
        wd = self.waited[e]
        pes = self.sem["pe"]
        for sid, (s, v) in need.items():
            if wd.get(sid, 0) >= v:
                continue
            if e == "pe" and s is pes:
                continue
            self.eng[e].wait_ge(s, v)
            self.n_wait += 1
            wd[sid] = v

    @staticmethod
    def _merge(need, d):
        for sid, sv in d.items():
            o = need.get(sid)
            if o is None or o[1] < sv[1]:
                need[sid] = sv

    def _deps(self, reads, writes):
        need = {}
        for b in reads:
            self._merge(need, b.w)
        for b in writes:
            self._merge(need, b.w)
            self._merge(need, b.r)
        return need

    @staticmethod
    def _mark(ev, reads, writes):
        sid = id(ev[0])
        for b in reads:
            b.r[sid] = ev
        for b in writes:
            b.w[sid] = ev

    def op(self, e, fn, reads=(), writes=()):
        self._wait(e, self._deps(reads, writes))
        ins = fn(self.eng[e])
        self.cnt[e] += 1
        ev = (self.sem[e], self.cnt[e])
        ins.then_inc(ev[0], 1)
        self._mark(ev, reads, writes)
        self.n_inst += 1

    def dma(self, q, out, in_, reads=(), writes=(), fn=None):
        n = self.dn[q]
        K = self.NDMA
        s = self.dsem[q][n % K]
        prev = 16 * (n // K)
        need = self._deps(reads, writes)
        if prev > 0:
            self._merge(need, {id(s): (s, prev)})
        self._wait(q, need)
        if fn is None:
            ins = self.eng[q].dma_start(out=out, in_=in_)
        else:
            ins = fn(self.eng[q])
        ins.then_inc(s, 16)
        self.dn[q] = n + 1
        self._mark((s, prev + 16), reads, writes)
        self.n_inst += 1

    def finish(self, bufs):
        need = {}
        for b in bufs:
            self._merge(need, b.w)
            self._merge(need, b.r)
        self._wait("sp", need)


def piece_table():
    t = {}
    n = 0
    for name, cnt in (("qk", 2 * RH), ("g", 2 * RH), ("v", 2 * RH), ("out", 2 * RH),
                      ("ff1_0", 32), ("ff2_0", 32), ("k", 8), ("v2", 8), ("q", 8), ("o", 8),
                      ("ff1_1", 32), ("ff2_1", 32)):
        t[name] = (n, cnt)
        n += cnt
    return t, n


def tile_fm(Wm, fcs):
    K = Wm.shape[0]
    out = np.empty((len(fcs), 128, K // 128, 128), np.float32)
    for i, fc in enumerate(fcs):
        out[i] = Wm[:, fc * 128:(fc + 1) * 128].reshape(K // 128, 128, 128).transpose(1, 0, 2)
    return out


def pack_weights(w_ret_in, w_ret_out, w_kv, w_sb_q, w_sb_o, w_ff1, w_ff2):
    tab, n = piece_table()
    Wp = np.empty((n, 128, PIECE), np.float32)
    win = w_ret_in[0]
    HK = 2048
    for h in range(RH):
        Wp[tab["qk"][0] + 2 * h] = tile_fm(win, [(h * 256) // 128, (h * 256) // 128 + 1]).transpose(1, 0, 2, 3).reshape(128, PIECE)
        Wp[tab["qk"][0] + 2 * h + 1] = tile_fm(win, [(HK + h * 256) // 128, (HK + h * 256) // 128 + 1]).transpose(1, 0, 2, 3).reshape(128, PIECE)
        g0 = (2 * HK + 4096 + h * 512) // 128
        for j in range(2):
            Wp[tab["g"][0] + 2 * h + j] = tile_fm(win, [g0 + 2 * j, g0 + 2 * j + 1]).transpose(1, 0, 2, 3).reshape(128, PIECE)
        v0 = 2 * HK + h * 512
        for j in range(2):
            Wp[tab["v"][0] + 2 * h + j] = win[:, v0 + j * 256:v0 + (j + 1) * 256].reshape(KC, 128, 256).transpose(1, 0, 2).reshape(128, PIECE)
        wo = w_ret_out[0][h * 512:(h + 1) * 512]
        for j in range(2):
            Wp[tab["out"][0] + 2 * h + j] = tile_fm(wo, list(range(8 * j, 8 * j + 8))).transpose(1, 0, 2, 3).reshape(128, PIECE)
    for l in range(2):
        w1 = w_ff1[l]
        for i in range(32):
            Wp[tab["ff1_%d" % l][0] + i] = tile_fm(w1, [2 * i, 2 * i + 1]).transpose(1, 0, 2, 3).reshape(128, PIECE)
        w2 = w_ff2[l]
        for g in range(4):
            w2g = w2[g * 2048:(g + 1) * 2048]
            for i in range(8):
                Wp[tab["ff2_%d" % l][0] + g * 8 + i] = tile_fm(w2g, [2 * i, 2 * i + 1]).transpose(1, 0, 2, 3).reshape(128, PIECE)
    for i in range(8):
        Wp[tab["k"][0] + i] = tile_fm(w_kv, [2 * i, 2 * i + 1]).transpose(1, 0, 2, 3).reshape(128, PIECE)
        Wp[tab["v2"][0] + i] = w_kv[:, D + i * 256:D + (i + 1) * 256].reshape(KC, 128, 256).transpose(1, 0, 2).reshape(128, PIECE)
        Wp[tab["q"][0] + i] = tile_fm(w_sb_q[0], [2 * i, 2 * i + 1]).transpose(1, 0, 2, 3).reshape(128, PIECE)
        Wp[tab["o"][0] + i] = tile_fm(w_sb_o[0], [2 * i, 2 * i + 1]).transpose(1, 0, 2, 3).reshape(128, PIECE)
    return Wp


def const_tables(SEQ, NS, past_len):
    half = 128
    inv = ROPE_BASE ** (-np.arange(half, dtype=np.float32) / half)
    pos = np.concatenate([np.arange(SEQ), past_len + np.arange(NS)]).astype(np.float32)
    ang = inv[:, None].astype(np.float32) * pos[None, :].astype(np.float32)
    rope = np.stack([np.cos(ang), np.sin(ang)], axis=1).astype(np.float32)
    log_g = np.log1p(-np.power(2.0, -5.0 - np.arange(RH, dtype=np.float64)))
    idx = np.arange(128, dtype=np.float64)
    diff = idx[None, :] - idx[:, None]
    dt = np.where(diff[None] >= 0, np.exp(log_g[:, None, None] * np.maximum(diff, 0)[None]), 0.0) / 16.0
    dt = dt.transpose(1, 0, 2).astype(np.float32)
    dq = (np.exp(log_g[:, None] * (idx + 1.0)[None, :]) / 16.0).astype(np.float32)
    dq = np.broadcast_to(dq[None], (128, RH, 128)).copy()
    dk = np.zeros((128, 2 * RH), np.float32)
    dk[:, :RH] = np.exp(log_g[None, :] * (127.0 - idx)[:, None])
    dk[:NS, RH:] = np.exp(log_g[None, :] * (NS - 1.0 - idx[:NS])[:, None])
    ident = np.eye(128, dtype=np.float32)
    tri = (idx[:, None] >= idx[None, :]).astype(np.float32)
    ustr = (idx[:, None] > idx[None, :]).astype(np.float32)
    dmat = (np.arange(512, dtype=np.float32)[None, :] - np.arange(128, dtype=np.float32)[:, None])
    m4 = np.zeros((128, SH, NS), np.float32)
    for j in range(NS):
        for i in range(NS):
            if j < i:
                m4[j, :, i] = 1.0
    iota = np.broadcast_to(np.arange(128, dtype=np.float32)[None, :], (128, 128))
    misc = np.concatenate([ident, tri, ustr, dmat, m4.reshape(128, SH * NS), iota], axis=1).astype(np.float32)
    decay_c = [float(np.exp(log_g[h] * 128.0)) for h in range(RH)]
    decay_cs = [float(np.exp(log_g[h] * NS)) for h in range(RH)]
    return rope, dt.reshape(128, RH * 128), dq.reshape(128, RH * 128), dk, misc, decay_c, decay_cs


class Ctx:
    def __init__(self, xacc, xb, Bx, Bxb):
        self.xacc, self.xb, self.Bx, self.Bxb = xacc, xb, Bx, Bxb


def build_program(SEQ, NS, NPG, NPOOL, stop_after=None):
    NT = SEQ // TT
    assert NT % 4 == 0
    NOWN = NT // 4
    NKB = SEQ // 128
    HQ = SH * NS
    tab, NPIECE = piece_table()
    _, _, _, _, _, decay_c, decay_cs = const_tables(SEQ, NS, NPG * PAGE)
    scale = 128 ** -0.5
    MISCW = 128 * 3 + 512 + HQ + 128

    nc = bass.Bass("TRN2", target_bir_lowering=False)

    def din(name, shape, dt=F32):
        return nc.dram_tensor(name, list(shape), dt, kind="ExternalInput").ap()

    def dout(name, shape, dt=F32):
        return nc.dram_tensor(name, list(shape), dt, kind="ExternalOutput").ap()

    xT = din("xT", [128, KC, SEQ])
    xsT = din("xsT", [128, KC, NS])
    state = din("state", [RH, 256, 512])
    cache_k = din("cache_k", [NPOOL * PAGE, D])
    cache_v = din("cache_v", [NPOOL * PAGE, D])
    pt = din("pt", [NPG, 1], I32)
    Wd = din("W", [NPIECE, 128, PIECE])
    rope_d = din("rope", [128, 2, SEQ + NS])
    dt_d = din("dtab", [128, RH * 128])
    dq_d = din("dqtab", [128, RH * 128])
    dk_d = din("dktab", [128, 2 * RH])
    misc_d = din("misc", [128, MISCW])
    lnp_d = din("lnp", [128, 2 * 4 * KC])
    sbb_d = din("sbb", [1, SH])
    pcf_d = din("pcf", [128, NOWN])
    pci_d = din("pci", [128, NOWN], I32)

    yT = dout("yT", [NOWN, 128, KC * TT])
    ysT = dout("ysT", [128, KC * NS])
    retp = dout("retp", [RH, 256, 512])
    kTo = dout("kTo", [SH, 128, SEQ])
    vo = dout("vo", [SEQ, D])
    rets = dout("rets", [RH, 256, 512])
    ksTo = dout("ksTo", [128, HQ])
    vso = dout("vso", [NS, D])

    Ks = nc.dram_tensor("Ks", [SH, 128, SEQ], BF16).ap()
    Vs = nc.dram_tensor("Vs", [SH, 128, NKB * 128], BF16).ap()
    X1s = nc.dram_tensor("X1s", [NT * 128, KC * TT], F32).ap()
    XBs = nc.dram_tensor("XBs", [NT * 128, KC * TT], BF16).ap()
    Wbc = nc.dram_tensor("Wbc", [NPIECE, 128, PIECE], BF16).ap()

    with contextlib.ExitStack() as st:
        def sb(name, shape, dt):
            return st.enter_context(nc.sbuf_tensor("s_" + name, list(shape), dt))

        def psum(name, shape, dt):
            return st.enter_context(nc.psum_tensor("p_" + name, list(shape), dt))

        xacc = sb("xacc", [128, KC, TT], F32)
        xb = sb("xb", [128, KC, TT], BF16)
        NWB = 3
        wbuf = [sb("wb%d" % i, [128, PIECE], BF16) for i in range(NWB)]
        hbuf = sb("hbuf", [128, KC, TT], BF16)
        kvb = sb("kvb", [128, max(2 * NKB * 128, D + HQ)], BF16)
        Ssb = sb("S", [128, RH * 2, 512], F32)
        Sbf = sb("Sbf", [128, 2, 512], BF16)
        rope_t = sb("rope_t", [128, 2, TT], F32)
        rq = sb("rq", [128, 2, TT], BF16)
        qh = sb("qh", [128, 2, TT], BF16)
        rk = sb("rk", [128, 2, TT], BF16)
        vsb = sb("vsb", [128, 4, 512], BF16)
        sg = sb("sg", [128, 4, TT], F32)
        og = sb("og", [128, 4, TT], BF16)
        pool = sb("tpool", [128, 8, TT], F32)
        tmp = [pool[:, i, :] for i in range(8)]
        scb = sb("scb", [128, 128], BF16)
        kd = sb("kd", [128, 256], BF16)
        onb = sb("onb", [128, 512], BF16)
        stats = sb("stats", [128, 6], F32)
        mv = sb("mv", [128, 2], F32)
        rstd = sb("rstd", [128, 1], F32)
        dt_t = sb("dt_t", [128, RH, 128], F32)
        dq_t = sb("dq_t", [128, RH, 128], F32)
        dk_t = sb("dk_t", [128, 2 * RH], F32)
        misc_t = sb("misc_t", [128, MISCW], F32)
        identb = sb("identb", [128, 128], BF16)
        ones = sb("ones", [128, 128], F32)
        lnp = sb("lnp", [128, 2 * 4 * KC], F32)
        lnpa = sb("lnpa", [128, 2 * 4 * KC], F32)
        sbb = sb("sbb", [128, SH], F32)
        pcf = sb("pcf", [128, NOWN], F32)
        pci = sb("pci", [128, NOWN], I32)
        qT2 = sb("qT2", [128, 2, TT], BF16)
        w_t = [sb("w_t%d" % i, [128, TT], BF16) for i in range(2)]
        kb16 = [sb("kb16_%d" % i, [128, TT], BF16) for i in range(2)]
        vf = [sb("vf%d" % i, [128, 256], F32) for i in range(2)]
        vb16 = [sb("vb16_%d" % i, [128, 256], BF16) for i in range(2)]
        xsacc = sb("xsacc", [128, KC, NS], F32)
        xsb = sb("xsb", [128, KC, NS], BF16)
        qTs = sb("qTs", [128, SH, NS], BF16)
        pti = sb("pti", [128, 1], I32)
        ptf = sb("ptf", [128, 1], F32)
        idxi = sb("idxi", [128, PAGE], I32)
        rflat = rope_t[:].rearrange("p a n -> p (a n)")
        sbb64 = rflat[:, 0:HQ]
        Tt = rflat[:, HQ:2 * HQ]
        en = rflat[:, 2 * HQ:3 * HQ]
        spn = rflat[:, 3 * HQ:4 * HQ]
        un = rflat[:, 4 * HQ:5 * HQ]
        oacc = rflat[:, 5 * HQ:6 * HQ]
        idxf = rflat[:, 6 * HQ:6 * HQ + PAGE]
        wn = sb("wn", [128, HQ], BF16)
        obs = sb("obs", [128, SH, NS], BF16)
        ksn = kvb[:, D:D + HQ].rearrange("p (h q) -> p h q", h=SH)
        vsn = kvb[:, 0:D]

        NRING = 5
        pring = [psum("pr%d" % i, [128, 512], F32) for i in range(NRING)]
        accA = psum("accA", [128, 512], F32)
        accB = psum("accB", [128, 512], F32)
        tpb = psum("tpb", [128, 1024], BF16)

        blk = st.enter_context(nc.Block())
        S = Sched(nc, st)
        B = {}

        def bf(name):
            if name not in B:
                B[name] = Buf(name)
            return B[name]

        Bx = [bf("xacc%d" % c) for c in range(KC)]
        Bxb = [bf("xb%d" % c) for c in range(KC)]
        Bh = [bf("hbuf%d" % c) for c in range(KC)]
        Bw = [bf("wb%d" % i) for i in range(NWB)]
        Bring = [bf("pr%d" % i) for i in range(NRING)]
        Bt = [bf("tmp%d" % i) for i in range(8)]
        ring_pos = [0]
        ctxP = Ctx(xacc, xb, Bx, Bxb)
        ctxS = Ctx(xsacc, xsb, [bf("xsacc")] * KC, [bf("xsb")] * KC)

        def ring():
            i = ring_pos[0] % NRING
            ring_pos[0] += 1
            return pring[i], Bring[i]

        plan = []
        wpos = [0]
        wissued = [0]

        Bwc = {}

        def w_issue_upto(k):
            while wissued[0] < min(len(plan), k):
                i = wissued[0]
                pid = plan[i]
                if pid not in Bwc:
                    S.dma("pool", wbuf[i % NWB][:], Wd[pid], writes=[Bw[i % NWB]])
                    Bwc[pid] = Buf("wc%d" % pid)
                    S.dma("sp", Wbc[pid], wbuf[i % NWB][:], reads=[Bw[i % NWB]], writes=[Bwc[pid]])
                else:
                    S.dma("sp", wbuf[i % NWB][:], Wbc[pid], reads=[Bwc[pid]], writes=[Bw[i % NWB]])
                wissued[0] += 1

        def wget(pid):
            i = wpos[0]
            assert plan[i] == pid, (i, plan[i], pid)
            w_issue_upto(i + NWB)
            wpos[0] += 1
            return wbuf[i % NWB], Bw[i % NWB]

        def wprefetch():
            w_issue_upto(wpos[0] + NWB - 1)

        def plan_mlp(l):
            p = []
            for g in range(4):
                p += [tab["ff1_%d" % l][0] + g * 8 + i for i in range(8)]
                p += [tab["ff2_%d" % l][0] + g * 8 + i for i in range(8)]
            return p

        def plan_layer0():
            p = []
            for h in range(RH):
                p += [tab["qk"][0] + 2 * h, tab["qk"][0] + 2 * h + 1]
                p += [tab["v"][0] + 2 * h, tab["v"][0] + 2 * h + 1]
                p += [tab["g"][0] + 2 * h, tab["g"][0] + 2 * h + 1]
                p += [tab["out"][0] + 2 * h, tab["out"][0] + 2 * h + 1]
            p += plan_mlp(0)
            p += [tab["k"][0] + i for i in range(8)]
            p += [tab["v2"][0] + i for i in range(8)]
            return p

        def plan_layer1():
            p = [tab["q"][0] + i for i in range(8)]
            p += [tab["o"][0] + i for i in range(8)]
            p += plan_mlp(1)
            return p

        for t in range(NT):
            plan.extend(plan_layer0())
        for j in range(NOWN):
            plan.extend(plan_layer1())
        if stop_after is None:
            plan.extend(plan_layer0())
            plan.extend(plan_layer1())

        def lin_fm(pids, cpp, KCn, rhs_fn, N, consume):
            fc = 0
            for pid in pids:
                wap, wb_ = wget(pid)
                wv = wap[:].rearrange("p (j k m) -> p j k m", j=cpp, k=KCn)
                for j in range(cpp):
                    pb, pbb = ring()
                    for kc in range(KCn):
                        rap, rbufs = rhs_fn(kc)
                        S.op("pe", lambda e, pb=pb, l=wv[:, j, kc, :], r=rap, kc=kc: e.matmul(
                            pb[:, 0:N], lhsT=l, rhs=r, start=(kc == 0), stop=(kc == KCn - 1)),
                            [wb_] + rbufs, [pbb])
                    consume(fc, pb, pbb)
                    fc += 1

        def xb_rhs(cx, N):
            return lambda kc: (cx.xb[:, kc, 0:N], [cx.Bxb[kc]])

        def add_into_xacc(cx, N):
            def consume(fc, pb, pbb):
                S.op("dve", lambda e: e.tensor_tensor(out=cx.xacc[:, fc, 0:N], in0=pb[:, 0:N], in1=cx.xacc[:, fc, 0:N], op=ALU.add),
                     [pbb, cx.Bx[fc]], [cx.Bx[fc]])
            return consume

        def layer_norm(cx, N, li, oscale_alpha):
            xa, xbt = cx.xacc, cx.xb
            s1, b1 = ring()
            s2, b2 = ring()
            mean_t, rstd_t = tmp[3], tmp[4]
            xs_, qs_ = tmp[5], tmp[6]
            ubx = list({id(b): b for b in cx.Bx}.values())
            S.op("dve", lambda e: e.tensor_reduce(out=xs_[:, 0:N], in_=xa[:, :, 0:N].rearrange("p c n -> p n c"), axis=mybir.AxisListType.X, op=ALU.add),
                 ubx, [Bt[5]])
            for c in range(KC):
                sq = tmp[c % 3]
                if c == 0:
                    S.op("act", lambda e: e.activation(out=qs_[:, 0:N], in_=xa[:, 0, 0:N], func=AF.Square), [cx.Bx[0]], [Bt[6]])
                else:
                    S.op("act", lambda e, c=c, sq=sq: e.activation(out=sq[:, 0:N], in_=xa[:, c, 0:N], func=AF.Square), [cx.Bx[c]], [Bt[c % 3]])
                    S.op("dve", lambda e, sq=sq: e.tensor_tensor(out=qs_[:, 0:N], in0=qs_[:, 0:N], in1=sq[:, 0:N], op=ALU.add), [Bt[c % 3], Bt[6]], [Bt[6]])
            S.op("pe", lambda e: e.matmul(s1[:, 0:N], lhsT=ones[:], rhs=xs_[:, 0:N], start=True, stop=True), [bf("ones"), Bt[5]], [b1])
            S.op("pe", lambda e: e.matmul(s2[:, 0:N], lhsT=ones[:], rhs=qs_[:, 0:N], start=True, stop=True), [bf("ones"), Bt[6]], [b2])
            S.op("dve", lambda e: e.tensor_scalar(out=mean_t[:, 0:N], in0=s1[:, 0:N], scalar1=1.0 / D, scalar2=None, op0=ALU.mult),
                 [b1], [Bt[3]])
            S.op("dve", lambda e: e.tensor_tensor(out=tmp[0][:, 0:N], in0=mean_t[:, 0:N], in1=mean_t[:, 0:N], op=ALU.mult),
                 [Bt[3]], [Bt[0]])
            S.op("dve", lambda e: e.scalar_tensor_tensor(out=rstd_t[:, 0:N], in0=s2[:, 0:N], scalar=1.0 / D, in1=tmp[0][:, 0:N],
                                                         op0=ALU.mult, op1=ALU.subtract),
                 [b2, Bt[0]], [Bt[4]])
            S.op("act", lambda e: e.activation(out=rstd_t[:, 0:N], in_=rstd_t[:, 0:N], func=AF.Ln, bias=LN_EPS), [Bt[4]], [Bt[4]])
            S.op("act", lambda e: e.activation(out=rstd_t[:, 0:N], in_=rstd_t[:, 0:N], func=AF.Exp, scale=-0.5), [Bt[4]], [Bt[4]])
            gp = lnpa if oscale_alpha else lnp
            for c in range(KC):
                tm = tmp[c % 3]
                tb_ = Bt[c % 3]
                S.op("dve", lambda e, c=c, tm=tm: e.tensor_tensor(out=tm[:, 0:N], in0=xa[:, c, 0:N], in1=mean_t[:, 0:N], op=ALU.subtract),
                     [cx.Bx[c], Bt[3]], [tb_])
                S.op("dve", lambda e, c=c, tm=tm: e.tensor_tensor(out=tm[:, 0:N], in0=tm[:, 0:N], in1=rstd_t[:, 0:N], op=ALU.mult),
                     [tb_, Bt[4]], [tb_])
                gcol = li * KC + c
                bcol = 4 * KC + li * KC + c
                S.op("act", lambda e, c=c, tm=tm, gcol=gcol, bcol=bcol: e.activation(
                    out=xbt[:, c, 0:N], in_=tm[:, 0:N], func=AF.Identity, scale=lnp[:, gcol:gcol + 1], bias=lnp[:, bcol:bcol + 1]),
                    [tb_, bf("lnp")], [cx.Bxb[c]])
                S.op("act", lambda e, c=c, tm=tm, gcol=gcol, bcol=bcol: e.activation(
                    out=xa[:, c, 0:N], in_=tm[:, 0:N], func=AF.Identity, scale=gp[:, gcol:gcol + 1], bias=gp[:, bcol:bcol + 1]),
                    [tb_, bf("lnp")], [cx.Bx[c]])

        def mlp(cx, N, l):
            for g in range(4):
                def cons1(fc, pb, pbb):
                    tm = tmp[5 + fc % 3]
                    tb_ = Bt[5 + fc % 3]
                    S.op("act", lambda e: e.activation(out=tm[:, 0:N], in_=pb[:, 0:N], func=AF.Relu), [pbb], [tb_])
                    S.op("dve", lambda e: e.tensor_tensor(out=hbuf[:, fc, 0:N], in0=tm[:, 0:N], in1=tm[:, 0:N], op=ALU.mult),
                         [tb_], [Bh[fc]])
                lin_fm([tab["ff1_%d" % l][0] + g * 8 + i for i in range(8)], 2, KC, xb_rhs(cx, N), N, cons1)
                lin_fm([tab["ff2_%d" % l][0] + g * 8 + i for i in range(8)], 2, KC,
                       lambda kc: (hbuf[:, kc, 0:N], [Bh[kc]]), N, add_into_xacc(cx, N))

        def retention(cx, N, cs, nchunk, rope_col0, sample):
            xbt = cx.xb
            S.dma("sp", rope_t[:, :, 0:N], rope_d[:, :, rope_col0:rope_col0 + N], writes=[bf("rope")])
            dkoff = RH if sample else 0
            for h in range(RH):
                dcy = decay_cs[h] if sample else decay_c[h]
                S.op("act", lambda e, h=h: e.copy(out=Sbf[:, :, :], in_=Ssb[:, 2 * h:2 * h + 2, :]), [bf("S%d" % h)], [bf("Sbf")])
                for which, dst, dbuf in ((0, rq, "rq"), (1, rk, "rk")):
                    held = []

                    def cons(fc, pb, pbb, held=held):
                        held.append((pb, pbb))
                    lin_fm([tab["qk"][0] + 2 * h + which], 2, KC, xb_rhs(cx, N), N, cons)
                    (p1, b1), (p2, b2) = held
                    cosv = rope_t[:, 0, 0:N]
                    sinv = rope_t[:, 1, 0:N]
                    S.op("dve", lambda e: e.tensor_tensor(out=tmp[0][:, 0:N], in0=p1[:, 0:N], in1=cosv, op=ALU.mult), [b1, bf("rope")], [Bt[0]])
                    S.op("dve", lambda e: e.tensor_tensor(out=tmp[1][:, 0:N], in0=p2[:, 0:N], in1=sinv, op=ALU.mult), [b2, bf("rope")], [Bt[1]])
                    S.op("dve", lambda e: e.tensor_tensor(out=tmp[2][:, 0:N], in0=p1[:, 0:N], in1=sinv, op=ALU.mult), [b1, bf("rope")], [Bt[2]])
                    S.op("dve", lambda e: e.tensor_tensor(out=tmp[3][:, 0:N], in0=p2[:, 0:N], in1=cosv, op=ALU.mult), [b2, bf("rope")], [Bt[3]])
                    S.op("dve", lambda e, dst=dst: e.tensor_tensor(out=dst[:, 0, 0:N], in0=tmp[0][:, 0:N], in1=tmp[1][:, 0:N], op=ALU.subtract),
                         [Bt[0], Bt[1]], [bf(dbuf)])
                    S.op("dve", lambda e, dst=dst: e.tensor_tensor(out=dst[:, 1, 0:N], in0=tmp[2][:, 0:N], in1=tmp[3][:, 0:N], op=ALU.add),
                         [Bt[2], Bt[3]], [bf(dbuf)])
                for half in range(2):
                    S.op("dve", lambda e, half=half: e.tensor_tensor(
                        out=qh[:, half, 0:N].rearrange("p (c i) -> p c i", i=cs),
                        in0=rq[:, half, 0:N].rearrange("p (c i) -> p c i", i=cs),
                        in1=dq_t[:, h, 0:cs].unsqueeze(1).to_broadcast([128, nchunk, cs]), op=ALU.mult),
                        [bf("rq"), bf("dq")], [bf("qh")])
                for j in range(2):
                    wap, wb_ = wget(tab["v"][0] + 2 * h + j)
                    wv = wap[:].rearrange("p (k n) -> p k n", k=KC)
                    for c in range(nchunk):
                        pb, pbb = ring()
                        for kc in range(KC):
                            S.op("pe", lambda e, pb=pb, kc=kc, c=c: e.matmul(pb[0:cs, 0:256], lhsT=xbt[:, kc, c * 128:c * 128 + cs], rhs=wv[:, kc, :],
                                                                          start=(kc == 0), stop=(kc == KC - 1)), [wb_, cx.Bxb[kc]], [pbb])
                        S.op("act", lambda e, pb=pb, c=c, j=j: e.copy(out=vsb[0:cs, c, j * 256:(j + 1) * 256], in_=pb[0:cs, 0:256]), [pbb], [bf("vsb")])

                def consg(fc, pb, pbb):
                    S.op("act", lambda e: e.activation(out=sg[:, fc, 0:N], in_=pb[:, 0:N], func=AF.Silu), [pbb], [bf("sg")])
                lin_fm([tab["g"][0] + 2 * h, tab["g"][0] + 2 * h + 1], 2, KC, xb_rhs(cx, N), N, consg)
                wprefetch()
                for c in range(nchunk):
                    cr = slice(c * 128, c * 128 + cs)
                    pb, pbb = ring()
                    for half in range(2):
                        S.op("pe", lambda e, half=half: e.matmul(pb[0:cs, 0:cs], lhsT=rk[:, half, cr], rhs=rq[:, half, cr], start=(half == 0), stop=(half == 1)),
                             [bf("rk"), bf("rq")], [pbb])
                    S.op("dve", lambda e: e.tensor_tensor(out=scb[0:cs, 0:cs], in0=pb[0:cs, 0:cs], in1=dt_t[0:cs, h, 0:cs], op=ALU.mult),
                         [pbb, bf("dt")], [bf("scb")])
                    S.op("pe", lambda e: e.matmul(accA[0:cs, :], lhsT=scb[0:cs, 0:cs], rhs=vsb[0:cs, c, :], start=True, stop=False),
                         [bf("scb"), bf("vsb")], [bf("accA")])
                    for half in range(2):
                        S.op("pe", lambda e, half=half: e.matmul(accA[0:cs, :], lhsT=qh[:, half, cr], rhs=Sbf[:, half, :], start=False, stop=(half == 1)),
                             [bf("qh"), bf("Sbf")], [bf("accA")])
                    for half in range(2):
                        S.op("pe", lambda e, half=half: e.transpose(tpb[0:cs, half * 128:(half + 1) * 128], rk[:, half, cr], identb[:, :]),
                             [bf("rk"), bf("identb")], [bf("tpbA")])
                    S.op("act", lambda e: e.activation(out=kd[0:cs, :], in_=tpb[0:cs, 0:256], func=AF.Identity, scale=dk_t[0:cs, dkoff + h:dkoff + h + 1], bias=0.0),
                         [bf("tpbA"), bf("dk")], [bf("kd")])
                    for half in range(2):
                        pd, pdb = ring()
                        S.op("pe", lambda e, half=half, pd=pd: e.matmul(pd[:, :], lhsT=kd[0:cs, half * 128:(half + 1) * 128], rhs=vsb[0:cs, c, :], start=True, stop=True),
                             [bf("kd"), bf("vsb")], [pdb])
                        S.op("dve", lambda e, half=half, pd=pd: e.scalar_tensor_tensor(out=Ssb[:, 2 * h + half, :], in0=Ssb[:, 2 * h + half, :], scalar=dcy, in1=pd[:, :],
                                                                                      op0=ALU.mult, op1=ALU.add), [pdb, bf("S%d" % h)], [bf("S%d" % h)])
                    S.op("dve", lambda e: e.bn_stats(out=stats[0:cs, :], in_=accA[0:cs, :]), [bf("accA")], [bf("stats")])
                    S.op("dve", lambda e: e.bn_aggr(out=mv[0:cs, :], in_=stats[0:cs, :]), [bf("stats")], [bf("mv")])
                    S.op("act", lambda e: e.activation(out=rstd[0:cs, :], in_=mv[0:cs, 1:2], func=AF.Ln, bias=GN_EPS), [bf("mv")], [bf("rstd")])
                    S.op("act", lambda e: e.activation(out=rstd[0:cs, :], in_=rstd[0:cs, :], func=AF.Exp, scale=-0.5), [bf("rstd")], [bf("rstd")])
                    S.op("dve", lambda e: e.tensor_scalar(out=onb[0:cs, :], in0=accA[0:cs, :], scalar1=mv[0:cs, 0:1], scalar2=rstd[0:cs, 0:1],
                                                          op0=ALU.subtract, op1=ALU.mult), [bf("accA"), bf("mv"), bf("rstd")], [bf("onb")])
                    if c + 1 < nchunk:
                        S.op("act", lambda e: e.copy(out=Sbf[:, :, :], in_=Ssb[:, 2 * h:2 * h + 2, :]), [bf("S%d" % h)], [bf("Sbf")])
                    for vc in range(4):
                        S.op("pe", lambda e, vc=vc: e.transpose(tpb[:, 512 + vc * 128:512 + vc * 128 + cs], onb[0:cs, vc * 128:(vc + 1) * 128], identb[0:cs, 0:cs]),
                             [bf("onb"), bf("identb")], [bf("tpbB")])
                    S.op("dve", lambda e: e.tensor_tensor(out=og[:, :, cr], in0=tpb[:, 512:1024].rearrange("p (v i) -> p v i", v=4)[:, :, 0:cs],
                                                          in1=sg[:, :, cr], op=ALU.mult), [bf("tpbB"), bf("sg")], [bf("og")])
                for j in range(2):
                    wap, wb_ = wget(tab["out"][0] + 2 * h + j)
                    wv = wap[:].rearrange("p (f k m) -> p f k m", f=8, k=4)
                    for fl in range(8):
                        fc = 8 * j + fl
                        pb, pbb = ring()
                        for kc in range(4):
                            S.op("pe", lambda e, pb=pb, fl=fl, kc=kc: e.matmul(pb[:, 0:N], lhsT=wv[:, fl, kc, :], rhs=og[:, kc, 0:N], start=(kc == 0), stop=(kc == 3)),
                                 [wb_, bf("og")], [pbb])
                        add_into_xacc(cx, N)(fc, pb, pbb)

        def kv_proj(cx, N, cs, nchunk, t, sample):
            xbt = cx.xb

            def consk(fc, pb, pbb):
                i = fc % 2
                S.op("act", lambda e: e.copy(out=tmp[i][:, 0:N], in_=pb[:, 0:N]), [pbb], [Bt[i]])
                if sample:
                    S.op("dve", lambda e: e.tensor_copy(out=ksn[:, fc, :], in_=tmp[i][:, 0:N]), [Bt[i]], [bf("ksn")])
                    S.dma("sp", ksTo[:, fc * NS:(fc + 1) * NS], tmp[i][:, 0:N], reads=[Bt[i]], writes=[bf("o_ksT")])
                else:
                    S.op("dve", lambda e: e.tensor_copy(out=kb16[i][:, 0:N], in_=tmp[i][:, 0:N]), [Bt[i]], [bf("kb16_%d" % i)])
                    S.dma("sp", kTo[fc, :, t * TT:t * TT + N], tmp[i][:, 0:N], reads=[Bt[i]], writes=[bf("o_kT")])
                    S.dma("sp", Ks[fc, :, t * TT:t * TT + N], kb16[i][:, 0:N], reads=[bf("kb16_%d" % i)], writes=[bf("Ks%d" % t)])
            lin_fm([tab["k"][0] + i for i in range(8)], 2, KC, xb_rhs(cx, N), N, consk)
            cnt = 0
            for j in range(8):
                wap, wb_ = wget(tab["v2"][0] + j)
                wv = wap[:].rearrange("p (k n) -> p k n", k=KC)
                for c in range(nchunk):
                    pb, pbb = ring()
                    for kc in range(KC):
                        S.op("pe", lambda e, pb=pb, kc=kc, c=c: e.matmul(pb[0:cs, 0:256], lhsT=xbt[:, kc, c * 128:c * 128 + cs], rhs=wv[:, kc, :],
                                                                      start=(kc == 0), stop=(kc == KC - 1)), [wb_, cx.Bxb[kc]], [pbb])
                    i = cnt % 2
                    cnt += 1
                    S.op("act", lambda e, pb=pb, i=i: e.copy(out=vf[i][0:cs, :], in_=pb[0:cs, 0:256]), [pbb], [bf("vf%d" % i)])
                    if sample:
                        S.op("dve", lambda e, i=i, j=j: e.tensor_copy(out=vsn[0:cs, j * 256:(j + 1) * 256], in_=vf[i][0:cs, :]), [bf("vf%d" % i)], [bf("vsn")])
                        S.dma("sp", vso[0:cs, j * 256:(j + 1) * 256], vf[i][0:cs, :], reads=[bf("vf%d" % i)], writes=[bf("o_vs")])
                    else:
                        S.op("dve", lambda e, i=i: e.tensor_copy(out=vb16[i][0:cs, :], in_=vf[i][0:cs, :]), [bf("vf%d" % i)], [bf("vb16_%d" % i)])
                        r0 = t * TT + c * 128
                        S.dma("sp", vo[r0:r0 + cs, j * 256:(j + 1) * 256], vf[i][0:cs, :], reads=[bf("vf%d" % i)], writes=[bf("o_v")])
                        kb = t * 4 + c
                        for hh in range(2):
                            S.dma("sp", Vs[2 * j + hh, 0:cs, kb * 128:(kb + 1) * 128],
                                  vb16[i][0:cs, hh * 128:(hh + 1) * 128], reads=[bf("vb16_%d" % i)], writes=[bf("Vs%d" % t)])

        def sb_attention_prompt(cx, j):
            N = TT
            KTh = kvb[:, 0:SEQ]
            Vh = kvb[:, NKB * 128:2 * NKB * 128]
            ksb = [bf("Ks%d" % t) for t in range(NT)]
            vsb_ = [bf("Vs%d" % t) for t in range(NT)]
            e_t = [tmp[0], tmp[1], tmp[2]]
            sp_t = [tmp[3], tmp[4], tmp[5]]
            Be = [Bt[0], Bt[1], Bt[2]]
            Bsp = [Bt[3], Bt[4], Bt[5]]
            dqm, lacc = sg[:, 0, :], sg[:, 1, :]
            u_t = [sg[:, 2, :], sg[:, 3, :]]
            Bdqm, Blacc, Bu = bf("sg"), bf("lacc"), [bf("u0"), bf("u1")]
            S.op("dve", lambda e: e.tensor_scalar(out=dqm, in0=misc_t[:, 384:896], scalar1=pcf[:, j:j + 1], scalar2=None, op0=ALU.add),
                 [bf("misc"), bf("pcf")], [Bdqm])
            kbs = list(reversed(range(NKB)))
            for hp in range(8):
                def consq(fc, pb, pbb):
                    S.op("act", lambda e: e.copy(out=qT2[:, fc, 0:N], in_=pb[:, 0:N]), [pbb], [bf("qT2")])
                lin_fm([tab["q"][0] + hp], 2, KC, xb_rhs(cx, N), N, consq)
                wprefetch()
                for jj in range(2):
                    h = 2 * hp + jj
                    S.dma("sp", KTh, Ks[h], reads=ksb, writes=[bf("KTh")])
                    S.dma("sp", Vh, Vs[h], reads=vsb_, writes=[bf("Vh")])
                    S.op("dve", lambda e: e.memset(lacc, 0.0), [], [Blacc])

                    def stage1(n):
                        kb = kbs[n]
                        i3 = n % 3
                        z, zb = ring()
                        S.op("pe", lambda e: e.matmul(z[:, :], lhsT=KTh[:, kb * 128:(kb + 1) * 128], rhs=qT2[:, jj, :], start=True, stop=True),
                             [bf("KTh"), bf("qT2")], [zb])
                        S.op("act", lambda e: e.activation(out=e_t[i3], in_=z[:, :], func=AF.Exp, scale=scale, bias=sbb[:, h:h + 1]),
                             [zb, bf("sbb")], [Be[i3]])
                        S.op("dve", lambda e: e.scalar_tensor_tensor(out=e_t[i3], in0=dqm, scalar=float(kb * 128), in1=e_t[i3],
                                                                     op0=ALU.is_gt, op1=ALU.mult), [Bdqm, Be[i3]], [Be[i3]])
                        S.op("act", lambda e: e.activation(out=sp_t[i3], in_=e_t[i3], func=AF.Ln, bias=1.0), [Be[i3]], [Bsp[i3]])

                    def stage2(n):
                        i3, i2 = n % 3, n % 2
                        a, ab = ring()
                        S.op("pe", lambda e: e.matmul(a[:, :], lhsT=misc_t[:, 128:256], rhs=sp_t[i3], start=True, stop=False),
                             [bf("misc"), Bsp[i3]], [ab])
                        S.op("pe", lambda e: e.matmul(a[:, :], lhsT=ones[:, :], rhs=lacc, start=False, stop=True), [bf("ones"), Blacc], [ab])
                        S.op("act", lambda e: e.activation(out=u_t[i2], in_=a[:, :], func=AF.Exp, scale=-1.0), [ab], [Bu[i2]])
                        S.op("dve", lambda e: e.tensor_tensor(out=w_t[i2][:, :], in0=e_t[i3], in1=u_t[i2], op=ALU.mult),
                             [Be[i3], Bu[i2]], [bf("w%d" % i2)])
                        S.op("dve", lambda e: e.tensor_tensor(out=lacc, in0=lacc, in1=sp_t[i3], op=ALU.add),
                             [Blacc, Bsp[i3]], [Blacc])

                    def stage3(n):
                        kb = kbs[n]
                        i2 = n % 2
                        S.op("pe", lambda e: e.matmul(accB[:, :], lhsT=Vh[:, kb * 128:(kb + 1) * 128], rhs=w_t[i2][:, :], start=(n == 0), stop=(n == NKB - 1)),
                             [bf("Vh"), bf("w%d" % i2)], [bf("accB")])

                    for n in range(NKB + 2):
                        if n < NKB:
                            stage1(n)
                        if 0 <= n - 1 < NKB:
                            stage2(n - 1)
                        if 0 <= n - 2 < NKB:
                            stage3(n - 2)
                    S.op("act", lambda e: e.copy(out=hbuf[:, h, :], in_=accB[:, :]), [bf("accB")], [Bh[h]])

        def layer1_tail(cx, N, ob_rhs):
            lin_fm([tab["o"][0] + i for i in range(8)], 2, KC, ob_rhs, N, add_into_xacc(cx, N))
            layer_norm(cx, N, 2, True)
            mlp(cx, N, 1)
            layer_norm(cx, N, 3, False)

        def sb_attention_sample(cx):
            N = NS
            P_ = NPG
            Eall = xacc[:].rearrange("p c n -> p (c n)")[:, 0:PAGE * HQ].rearrange("p (r q) -> p r q", q=HQ)
            Aall = Ssb[:].rearrange("p c n -> p (c n)")[:, 0:PAGE * HQ].rearrange("p (r q) -> p r q", q=HQ)
            Wall = xb[:].rearrange("p c n -> p (c n)")[:, 0:PAGE * HQ].rearrange("p (r q) -> p r q", q=HQ)
            BE, BA, BW = bf("Eall"), bf("Aall"), bf("Wall")
            allS = [bf("S%d" % h) for h in range(RH)]
            hflat = hbuf[:].rearrange("p c n -> p (c n)")
            Kb = hflat[:, 0:D]
            KT = hflat[:, D:2 * D].rearrange("p (h g) -> p h g", h=SH)
            Vb = hflat[:, 2 * D:3 * D]
            pflat = pool[:].rearrange("p c n -> p (c n)")
            gat = [pflat[:, 0:D], pflat[:, D:2 * D]]
            Bg = [bf("gat0"), bf("gat1")]

            def consq(fc, pb, pbb):
                S.op("act", lambda e: e.copy(out=qTs[:, fc, :], in_=pb[:, 0:N]), [pbb], [bf("qTs")])
            lin_fm([tab["q"][0] + i for i in range(8)], 2, KC, xb_rhs(cx, N), N, consq)
            wprefetch()
            S.dma("sp", pti[0:P_, :], pt[:, :], writes=[bf("pti")])
            S.op("dve", lambda e: e.tensor_copy(out=ptf[0:P_, :], in_=pti[0:P_, :]), [bf("pti")], [bf("ptf")])
            S.op("dve", lambda e: e.tensor_scalar(out=ptf[0:P_, :], in0=ptf[0:P_, :], scalar1=float(PAGE), scalar2=None, op0=ALU.mult), [bf("ptf")], [bf("ptf")])
            S.op("dve", lambda e: e.tensor_scalar(out=idxf[0:P_, :], in0=misc_t[0:P_, 896 + HQ:896 + HQ + PAGE], scalar1=ptf[0:P_, 0:1], scalar2=None, op0=ALU.add),
                 [bf("ptf"), bf("misc")], [bf("idxf")])
            S.op("dve", lambda e: e.tensor_copy(out=idxi[0:P_, :], in_=idxf[0:P_, :]), [bf("idxf")], [bf("pti")])
            S.op("dve", lambda e: e.tensor_copy(out=sbb64[:].rearrange("p (h q) -> p h q", h=SH), in_=sbb[:].unsqueeze(2).to_broadcast([128, SH, NS])),
                 [bf("sbb")], [bf("sbb64")])
            for r in range(PAGE):
                g = r % 2
                S.dma("pool", None, None, reads=[bf("pti")], writes=[Bg[g]] + (Bt if r < 2 else []),
                      fn=lambda e, g=g, r=r: e.indirect_dma_start(out=gat[g][0:P_, :], out_offset=None, in_=cache_k[:, :],
                                                                  in_offset=bass.IndirectOffsetOnAxis(ap=idxi[0:P_, r:r + 1], axis=0)))
                if r % 2 == 0:
                    S.op("act", lambda e, g=g: e.copy(out=Kb[0:P_, :], in_=gat[g][0:P_, :]), [Bg[g]], [bf("Kb")] + (Bh if r == 0 else []))
                else:
                    S.op("dve", lambda e, g=g: e.tensor_copy(out=Kb[0:P_, :], in_=gat[g][0:P_, :]), [Bg[g]], [bf("Kb")])
                z, zb = ring()
                for hh in range(2):
                    for hl in range(8):
                        h = hh * 8 + hl
                        S.op("pe", lambda e, hl=hl, h=h: e.transpose(tpb[:, hl * 128:hl * 128 + P_], Kb[0:P_, h * 128:(h + 1) * 128], identb[0:P_, 0:P_]),
                             [bf("Kb"), bf("identb")], [bf("tpbA"), bf("tpbB")])
                    S.op("act" if hh == 0 else "dve",
                         (lambda e, hh=hh: e.copy(out=KT[:, hh * 8:(hh + 1) * 8, 0:P_], in_=tpb[:, :].rearrange("p (h g) -> p h g", h=8)[:, :, 0:P_])) if hh == 0 else
                         (lambda e, hh=hh: e.tensor_copy(out=KT[:, hh * 8:(hh + 1) * 8, 0:P_], in_=tpb[:, :].rearrange("p (h g) -> p h g", h=8)[:, :, 0:P_])),
                         [bf("tpbA"), bf("tpbB")], [bf("KT%d" % hh)] + (Bh if r == 0 else []))
                    for hl in range(8):
                        h = hh * 8 + hl
                        S.op("pe", lambda e, h=h: e.matmul(z[0:P_, h * NS:(h + 1) * NS], lhsT=KT[:, h, 0:P_], rhs=qTs[:, h, :], start=True, stop=True),
                             [bf("KT%d" % hh), bf("qTs")], [zb])
                S.op("dve", lambda e, r=r: e.scalar_tensor_tensor(out=Eall[0:P_, r, :], in0=z[0:P_, 0:HQ], scalar=scale, in1=sbb64[0:P_, :], op0=ALU.mult, op1=ALU.add),
                     [zb, bf("sbb64")], [BE] + Bx)
            S.op("act", lambda e: e.activation(out=Eall[0:P_], in_=Eall[0:P_], func=AF.Exp), [BE], [BE])
            S.op("act", lambda e: e.activation(out=Aall[0:P_], in_=Eall[0:P_], func=AF.Ln, bias=1.0), [BE] + allS, [BA] + allS)
            S.op("dve", lambda e: e.tensor_reduce(out=Tt[0:P_, :], in_=Aall[0:P_].rearrange("p r q -> p q r"), axis=mybir.AxisListType.X, op=ALU.add),
                 [BA], [bf("Tt")])
            zn, znb = ring()
            for h in range(SH):
                S.op("pe", lambda e, h=h: e.matmul(zn[0:NS, h * NS:(h + 1) * NS], lhsT=ksn[:, h, :], rhs=qTs[:, h, :], start=True, stop=True),
                     [bf("ksn"), bf("qTs")], [znb])
            S.op("dve", lambda e: e.scalar_tensor_tensor(out=en[0:NS, :], in0=zn[0:NS, 0:HQ], scalar=scale, in1=sbb64[0:NS, :], op0=ALU.mult, op1=ALU.add),
                 [znb, bf("sbb64")], [bf("en")])
            S.op("act", lambda e: e.activation(out=en[0:NS, :], in_=en[0:NS, :], func=AF.Exp), [bf("en")], [bf("en")])
            S.op("dve", lambda e: e.tensor_tensor(out=en[0:NS, :], in0=en[0:NS, :], in1=misc_t[0:NS, 896:896 + HQ], op=ALU.mult), [bf("en"), bf("misc")], [bf("en")])
            S.op("act", lambda e: e.activation(out=spn[0:NS, :], in_=en[0:NS, :], func=AF.Ln, bias=1.0), [bf("en")], [bf("spn")])
            ct, ctb = ring()
            S.op("pe", lambda e: e.matmul(ct[0:P_, 0:HQ], lhsT=misc_t[0:P_, 256:256 + P_], rhs=Tt[0:P_, :], start=True, stop=False), [bf("misc"), bf("Tt")], [ctb])
            S.op("pe", lambda e: e.matmul(ct[0:P_, 0:HQ], lhsT=ones[0:NS, 0:P_], rhs=spn[0:NS, :], start=False, stop=True), [bf("ones"), bf("spn")], [ctb])
            S.op("dve", lambda e: e.tensor_tensor(out=Aall[0:P_, PAGE - 1, :], in0=Aall[0:P_, PAGE - 1, :], in1=ct[0:P_, 0:HQ], op=ALU.add), [BA, ctb], [BA])
            sh = 1
            while sh < PAGE:
                S.op("dve", lambda e, sh=sh: e.tensor_tensor(out=Aall[0:P_, 0:PAGE - sh, :], in0=Aall[0:P_, 0:PAGE - sh, :], in1=Aall[0:P_, sh:PAGE, :], op=ALU.add),
                     [BA], [BA])
                sh *= 2
            S.op("act", lambda e: e.activation(out=Aall[0:P_], in_=Aall[0:P_], func=AF.Exp, scale=-1.0), [BA], [BA])
            S.op("dve", lambda e: e.tensor_tensor(out=Wall[0:P_], in0=Eall[0:P_], in1=Aall[0:P_], op=ALU.mult), [BE, BA] + Bxb, [BW] + Bxb)
            an, anb = ring()
            S.op("pe", lambda e: e.matmul(an[0:NS, 0:HQ], lhsT=misc_t[0:NS, 128:128 + NS], rhs=spn[0:NS, :], start=True, stop=True), [bf("misc"), bf("spn")], [anb])
            S.op("act", lambda e: e.activation(out=un[0:NS, :], in_=an[0:NS, 0:HQ], func=AF.Exp, scale=-1.0), [anb], [bf("un")])
            S.op("dve", lambda e: e.tensor_tensor(out=wn[0:NS, :], in0=en[0:NS, :], in1=un[0:NS, :], op=ALU.mult), [bf("en"), bf("un")], [bf("wn")])
            for r in range(PAGE + 1):
                o_, ob_ = ring()
                if r < PAGE:
                    g = r % 2
                    S.dma("pool", None, None, reads=[bf("pti")], writes=[Bg[g]],
                          fn=lambda e, g=g, r=r: e.indirect_dma_start(out=gat[g][0:P_, :], out_offset=None, in_=cache_v[:, :],
                                                                      in_offset=bass.IndirectOffsetOnAxis(ap=idxi[0:P_, r:r + 1], axis=0)))
                    if r % 2 == 0:
                        S.op("act", lambda e, g=g: e.copy(out=Vb[0:P_, :], in_=gat[g][0:P_, :]), [Bg[g]], [bf("Vb")] + (Bh if r == 0 else []))
                    else:
                        S.op("dve", lambda e, g=g: e.tensor_copy(out=Vb[0:P_, :], in_=gat[g][0:P_, :]), [Bg[g]], [bf("Vb")])
                    for h in range(SH):
                        S.op("pe", lambda e, h=h, r=r: e.matmul(o_[:, h * NS:(h + 1) * NS], lhsT=Vb[0:P_, h * 128:(h + 1) * 128], rhs=Wall[0:P_, r, h * NS:(h + 1) * NS],
                                                                start=True, stop=True), [bf("Vb"), BW], [ob_])
                else:
                    for h in range(SH):
                        S.op("pe", lambda e, h=h: e.matmul(o_[:, h * NS:(h + 1) * NS], lhsT=vsn[0:NS, h * 128:(h + 1) * 128], rhs=wn[0:NS, h * NS:(h + 1) * NS],
                                                           start=True, stop=True), [bf("vsn"), bf("wn")], [ob_])
                if r == 0:
                    S.op("dve", lambda e: e.tensor_copy(out=oacc[:, :], in_=o_[:, 0:HQ]), [ob_], [bf("oacc")])
                else:
                    S.op("dve", lambda e: e.tensor_tensor(out=oacc[:, :], in0=oacc[:, :], in1=o_[:, 0:HQ], op=ALU.add), [ob_, bf("oacc")], [bf("oacc")])
            S.op("act", lambda e: e.copy(out=obs[:].rearrange("p h q -> p (h q)"), in_=oacc[:, :]), [bf("oacc")], [bf("obs")])

        @blk.sync
        def _(sync):
            S.dma("sp", dt_t[:].rearrange("p h i -> p (h i)"), dt_d[:, :], writes=[bf("dt")])
            S.dma("sp", dq_t[:].rearrange("p h i -> p (h i)"), dq_d[:, :], writes=[bf("dq")])
            S.dma("sp", dk_t[:], dk_d[:, :], writes=[bf("dk")])
            S.dma("sp", misc_t[:], misc_d[:, :], writes=[bf("misc")])
            S.dma("sp", lnp[:], lnp_d[:, :], writes=[bf("lnp")])
            S.dma("sp", sbb[:], sbb_d.partition_broadcast(128), writes=[bf("sbb")])
            S.dma("sp", pcf[:], pcf_d[:, :], writes=[bf("pcf")])
            S.dma("sp", pci[:], pci_d[:, :], writes=[bf("pci")])
            S.op("dve", lambda e: e.tensor_copy(out=identb[:], in_=misc_t[:, 0:128]), [bf("misc")], [bf("identb")])
            S.op("dve", lambda e: e.memset(ones[:], 1.0), [], [bf("ones")])
            S.op("dve", lambda e: e.tensor_scalar(out=lnpa[:], in0=lnp[:], scalar1=ALPHA, scalar2=None, op0=ALU.mult), [bf("lnp")], [bf("lnp")])

            def load_x(cx, src, N, c0):
                S.dma("sp", cx.xacc[:, :, 0:N], src[:, :, c0:c0 + N], writes=cx.Bx[0:1] if cx is ctxS else cx.Bx)
                S.dma("pool", cx.xb[:, :, 0:N], src[:, :, c0:c0 + N], writes=cx.Bxb[0:1] if cx is ctxS else cx.Bxb)
                for c in range(KC):
                    S.op("act", lambda e, c=c: e.mul(out=cx.xacc[:, c, 0:N], in_=cx.xacc[:, c, 0:N], mul=ALPHA), [cx.Bx[c]], [cx.Bx[c]])

            def layer0(cx, N, cs, nchunk, t, sample):
                retention(cx, N, cs, nchunk, (SEQ if sample else t * TT), sample)
                layer_norm(cx, N, 0, True)
                if stop_after == "ln0":
                    return
                mlp(cx, N, 0)
                layer_norm(cx, N, 1, True)
                if stop_after == "mlp0":
                    return
                kv_proj(cx, N, cs, nchunk, t, sample)

            for h in range(RH):
                S.op("dve", lambda e, h=h: e.memset(Ssb[:, 2 * h:2 * h + 2, :], 0.0), [], [bf("S%d" % h)])
            for t in range(NT):
                load_x(ctxP, xT, TT, t * TT)
                if stop_after == "ret0":
                    retention(ctxP, TT, 128, 4, 0, False)
                    S.dma("sp", yT[0], xacc[:].rearrange("p c n -> p (c n)"), reads=Bx, writes=[bf("o_y")])
                    S.finish(list(B.values()))
                    return
                layer0(ctxP, TT, 128, 4, t, False)
                if stop_after in ("t0", "ln0", "mlp0"):
                    S.dma("sp", yT[0], xacc[:].rearrange("p c n -> p (c n)"), reads=Bx, writes=[bf("o_y")])
                    S.finish(list(B.values()))
                    return
                S.dma("sp", X1s[t * 128:(t + 1) * 128, :], xacc[:].rearrange("p c n -> p (c n)"), reads=Bx, writes=[bf("X1s%d" % t)])
                S.dma("sp", XBs[t * 128:(t + 1) * 128, :], xb[:].rearrange("p c n -> p (c n)"), reads=Bxb, writes=[bf("XBs%d" % t)])
            for h in range(RH):
                S.dma("sp", retp[h].rearrange("(c p) v -> p c v", p=128), Ssb[:, 2 * h:2 * h + 2, :], reads=[bf("S%d" % h)], writes=[bf("o_retp")])
            if stop_after == "l0":
                S.finish(list(B.values()))
                return
            for j in range(NOWN):
                x1b = [bf("X1s%d" % t) for t in range(NT)]
                xbb = [bf("XBs%d" % t) for t in range(NT)]
                S.dma("pool", None, None, reads=x1b + [bf("pci")], writes=Bx,
                      fn=lambda e, j=j: e.indirect_dma_start(out=xacc[:].rearrange("p c n -> p (c n)"), out_offset=None, in_=X1s[:, :],
                                                             in_offset=bass.IndirectOffsetOnAxis(ap=pci[:, j:j + 1], axis=0)))
                S.dma("pool", None, None, reads=xbb + [bf("pci")], writes=Bxb,
                      fn=lambda e, j=j: e.indirect_dma_start(out=xb[:].rearrange("p c n -> p (c n)"), out_offset=None, in_=XBs[:, :],
                                                             in_offset=bass.IndirectOffsetOnAxis(ap=pci[:, j:j + 1], axis=0)))
                sb_attention_prompt(ctxP, j)
                layer1_tail(ctxP, TT, lambda kc: (hbuf[:, kc, 0:TT], [Bh[kc]]))
                S.dma("sp", yT[j], xacc[:].rearrange("p c n -> p (c n)"), reads=Bx, writes=[bf("o_y")])
            if stop_after is None:
                for h in range(RH):
                    S.dma("sp", Ssb[:, 2 * h:2 * h + 2, :], state[h].rearrange("(c p) v -> p c v", p=128), writes=[bf("S%d" % h)])
                load_x(ctxS, xsT, NS, 0)
                layer0(ctxS, NS, NS, 1, 0, True)
                for h in range(RH):
                    S.dma("sp", rets[h].rearrange("(c p) v -> p c v", p=128), Ssb[:, 2 * h:2 * h + 2, :], reads=[bf("S%d" % h)], writes=[bf("o_rets")])
                sb_attention_sample(ctxS)
                layer1_tail(ctxS, NS, lambda kc: (obs[:, kc, :], [bf("obs")]))
                S.dma("sp", ysT[:, :].rearrange("p (c n) -> p c n", c=KC), xsacc[:, :, :], reads=ctxS.Bx[0:1], writes=[bf("o_ys")])
            S.finish(list(B.values()))
        print("instructions:", S.n_inst, "waits:", S.n_wait, "pieces:", wpos[0], "/", len(plan), flush=True)
    return nc


_PROG_CACHE = {}
_LAST_RES = None


def _run(inputs, SEQ, NS, NPG, stop_after=None):
    x_prompt = np.asarray(inputs["x_prompt"], np.float32)
    x_sample = np.asarray(inputs["x_sample"], np.float32)
    state_ret = np.asarray(inputs["state_ret"], np.float32)
    cache_k = np.asarray(inputs["cache_k"], np.float32)
    cache_v = np.asarray(inputs["cache_v"], np.float32)
    page_table = np.asarray(inputs["page_table"], np.int32)
    NPOOL = cache_k.shape[0]
    BATCH = x_prompt.shape[0]
    NT = SEQ // TT
    NOWN = NT // 4
    key = (SEQ, NS, NPG, NPOOL, stop_after)
    if key not in _PROG_CACHE:
        _PROG_CACHE[key] = build_program(SEQ, NS, NPG, NPOOL, stop_after)
    nc = _PROG_CACHE[key]
    Wp = pack_weights(*[np.asarray(inputs[k], np.float32) for k in ("w_ret_in", "w_ret_out", "w_kv", "w_sb_q", "w_sb_o", "w_ff1", "w_ff2")])
    rope, dtab, dqtab, dktab, misc, _, _ = const_tables(SEQ, NS, NPG * PAGE)
    ln_g = np.asarray(inputs["ln_g"], np.float32).reshape(4, KC, 128)
    ln_b = np.asarray(inputs["ln_b"], np.float32).reshape(4, KC, 128)
    lnp = np.concatenate([ln_g.transpose(2, 0, 1).reshape(128, 4 * KC), ln_b.transpose(2, 0, 1).reshape(128, 4 * KC)], axis=1)
    sbb = np.asarray(inputs["sb_bias"], np.float32).reshape(1, SH)
    ck = cache_k.reshape(NPOOL * PAGE, D)
    cv = cache_v.reshape(NPOOL * PAGE, D)
    in_maps = []
    for c in range(NCORE):
        b, s = (c // 4) % BATCH, c % 4
        xT = np.ascontiguousarray(x_prompt[b].reshape(SEQ, KC, 128).transpose(2, 1, 0))
        xsT = np.ascontiguousarray(x_sample[c].reshape(NS, KC, 128).transpose(2, 1, 0))
        own = [s * NOWN + j for j in range(NOWN)]
        pcf = np.broadcast_to(np.array([o * TT for o in own], np.float32)[None], (128, NOWN)).copy()
        pci = (np.array(own, np.int32)[None, :] * 128 + np.arange(128, dtype=np.int32)[:, None]).astype(np.int32)
        in_maps.append(dict(xT=xT, xsT=xsT, state=np.ascontiguousarray(state_ret[0, c]), cache_k=ck, cache_v=cv,
                            pt=np.ascontiguousarray(page_table[c].reshape(NPG, 1)), W=Wp, rope=rope, dtab=dtab, dqtab=dqtab,
                            dktab=dktab, misc=misc, lnp=lnp, sbb=sbb, pcf=pcf, pci=pci))
    import os as _os
    _n = int(_os.environ.get("KNCORE", NCORE))
    if _os.environ.get("KTRACE"):
        res = run_bass_kernel_spmd(nc, in_maps[:_n], core_ids=list(range(_n)), trace=True)
        print("KTRACE exec_time_ns", res.exec_time_ns, flush=True)
        global _LAST_RES
        _LAST_RES = res
    else:
        res = run_bass_kernel_spmd(nc, in_maps[:_n], core_ids=list(range(_n)))
    R = list(res.results) + [res.results[0]] * (NCORE - _n)
    y_prompt = np.zeros((BATCH, SEQ, D), np.float32)
    for c in range(NCORE):
        b, s = (c // 4) % BATCH, c % 4
        for j in range(NOWN):
            t = s * NOWN + j
            y_prompt[b, t * TT:(t + 1) * TT] = R[c]["yT"][j].reshape(128, KC, TT).transpose(2, 1, 0).reshape(TT, D)
    y_sample = np.stack([R[c]["ysT"].reshape(128, KC, NS).transpose(2, 1, 0).reshape(NS, D) for c in range(NCORE)])
    ret_prompt = np.stack([R[4 * b]["retp"] for b in range(BATCH)])[None]
    k_prompt = np.stack([R[4 * b]["kTo"].transpose(2, 0, 1) for b in range(BATCH)])
    v_prompt = np.stack([R[4 * b]["vo"].reshape(SEQ, SH, 128) for b in range(BATCH)])
    ret_sample = np.stack([R[c]["rets"] for c in range(NCORE)])[None]
    k_sample = np.stack([R[c]["ksTo"].reshape(128, SH, NS).transpose(2, 1, 0) for c in range(NCORE)])
    v_sample = np.stack([R[c]["vso"].reshape(NS, SH, 128) for c in range(NCORE)])
    return (y_prompt, y_sample, np.ascontiguousarray(ret_prompt), np.ascontiguousarray(k_prompt), v_prompt,
            np.ascontiguousarray(ret_sample), np.ascontiguousarray(k_sample), v_sample)


def kernel(**inputs):
    SEQ = inputs["x_prompt"].shape[1]
    NS = inputs["x_sample"].shape[1]
    NPG = inputs["page_table"].shape[1]
    return _run(inputs, SEQ, NS, NPG)
```

```python
import contextlib
import math
import numpy as np
import concourse.bass as bass
import concourse.mybir as mybir
from concourse.bass_utils import run_bass_kernel_spmd

F32 = mybir.dt.float32
BF16 = mybir.dt.bfloat16
I32 = mybir.dt.int32
AF = mybir.ActivationFunctionType
ALU = mybir.AluOpType

D = 2048
KC = 16
RH = 8
SH = 16
DFF = 8192
TT = 512
LN_EPS = 1e-5
GN_EPS = 1e-6
DEPTH = 2
ALPHA = (2.0 * DEPTH) ** 0.25
ROPE_BASE = 10000.0
PAGE = 128
NCORE = 8
PIECE = 4096


class Buf:
    __slots__ = ("name", "w", "r")

    def __init__(self, name):
        self.name = name
        self.w = {}
        self.r = {}


class Sched:
    NDMA = 8

    def __init__(self, nc, stack):
        self.nc = nc
        self.eng = {"pe": nc.tensor, "act": nc.scalar, "dve": nc.vector, "pool": nc.gpsimd, "sp": nc.sync}
        self.sem = {}
        self.cnt = {}
        self.waited = {k: {} for k in self.eng}
        for k in ("pe", "act", "dve", "pool"):
            self.sem[k] = stack.enter_context(nc.semaphore("sem_" + k))
            self.cnt[k] = 0
        self.dsem = {}
        self.dn = {}
        for q in ("sp", "act", "pool"):
            self.dsem[q] = [stack.enter_context(nc.semaphore("dsem_%s_%d" % (q, i))) for i in range(self.NDMA)]
            self.dn[q] = 0
        self.n_inst = 0
        self.n_wait = 0

    def _wait(self, e, need):
        wd = self.waited[e]
        pes = self.sem["pe"]
        for sid, (s, v) in need.items():
            if wd.get(sid, 0) >= v:
                continue
            if e == "pe" and s is pes:
                continue
            self.eng[e].wait_ge(s, v)
            self.n_wait += 1
            wd[sid] = v

    @staticmethod
    def _merge(need, d):
        for sid, sv in d.items():
            o = need.get(sid)
            if o is None or o[1] < sv[1]:
                need[sid] = sv

    def _deps(self, reads, writes):
        need = {}
        for b in reads:
            self._merge(need, b.w)
        for b in writes:
            self._merge(need, b.w)
            self._merge(need, b.r)
        return need

    @staticmethod
    def _mark(ev, reads, writes):
        sid = id(ev[0])
        for b in reads:
            b.r[sid] = ev
        for b in writes:
            b.w[sid] = ev

    def op(self, e, fn, reads=(), writes=()):
        self._wait(e, self._deps(reads, writes))
        ins = fn(self.eng[e])
        self.cnt[e] += 1
        ev = (self.sem[e], self.cnt[e])
        ins.then_inc(ev[0], 1)
        self._mark(ev, reads, writes)
        self.n_inst += 1

    def dma(self, q, out, in_, reads=(), writes=(), fn=None):
        n = self.dn[q]
        K = self.NDMA
        s = self.dsem[q][n % K]
        prev = 16 * (n // K)
        need = self._deps(reads, writes)
        if prev > 0:
            self._merge(need, {id(s): (s, prev)})
        self._wait(q, need)
        if fn is None:
            ins = self.eng[q].dma_start(out=out, in_=in_)
        else:
            ins = fn(self.eng[q])
        ins.then_inc(s, 16)
        self.dn[q] = n + 1
        self._mark((s, prev + 16), reads, writes)
        self.n_inst += 1

    def finish(self, bufs):
        need = {}
        for b in bufs:
            self._merge(need, b.w)
            self._merge(need, b.r)
        self._wait("sp", need)


def piece_table():
    t = {}
    n = 0
    for name, cnt in (("qk", 2 * RH), ("g", 2 * RH), ("v", 2 * RH), ("out", 2 * RH),
                      ("ff1_0", 32), ("ff2_0", 32), ("k", 8), ("v2", 8), ("q", 8), ("o", 8),
                      ("ff1_1", 32), ("ff2_1", 32)):
        t[name] = (n, cnt)
        n += cnt
    return t, n


def tile_fm(Wm, fcs):
    K = Wm.shape[0]
    out = np.empty((len(fcs), 128, K // 128, 128), np.float32)
    for i, fc in enumerate(fcs):
        out[i] = Wm[:, fc * 128:(fc + 1) * 128].reshape(K // 128, 128, 128).transpose(1, 0, 2)
    return out


def pack_weights(w_ret_in, w_ret_out, w_kv, w_sb_q, w_sb_o, w_ff1, w_ff2):
    tab, n = piece_table()
    Wp = np.empty((n, 128, PIECE), np.float32)
    win = w_ret_in[0]
    HK = 2048
    for h in range(RH):
        Wp[tab["qk"][0] + 2 * h] = tile_fm(win, [(h * 256) // 128, (h * 256) // 128 + 1]).transpose(1, 0, 2, 3).reshape(128, PIECE)
        Wp[tab["qk"][0] + 2 * h + 1] = tile_fm(win, [(HK + h * 256) // 128, (HK + h * 256) // 128 + 1]).transpose(1, 0, 2, 3).reshape(128, PIECE)
        g0 = (2 * HK + 4096 + h * 512) // 128
        for j in range(2):
            Wp[tab["g"][0] + 2 * h + j] = tile_fm(win, [g0 + 2 * j, g0 + 2 * j + 1]).transpose(1, 0, 2, 3).reshape(128, PIECE)
        v0 = 2 * HK + h * 512
        for j in range(2):
            Wp[tab["v"][0] + 2 * h + j] = win[:, v0 + j * 256:v0 + (j + 1) * 256].reshape(KC, 128, 256).transpose(1, 0, 2).reshape(128, PIECE)
        wo = w_ret_out[0][h * 512:(h + 1) * 512]
        for j in range(2):
            Wp[tab["out"][0] + 2 * h + j] = tile_fm(wo, list(range(8 * j, 8 * j + 8))).transpose(1, 0, 2, 3).reshape(128, PIECE)
    for l in range(2):
        w1 = w_ff1[l]
        for i in range(32):
            Wp[tab["ff1_%d" % l][0] + i] = tile_fm(w1, [2 * i, 2 * i + 1]).transpose(1, 0, 2, 3).reshape(128, PIECE)
        w2 = w_ff2[l]
        for g in range(4):
            w2g = w2[g * 2048:(g + 1) * 2048]
            for i in range(8):
                Wp[tab["ff2_%d" % l][0] + g * 8 + i] = tile_fm(w2g, [2 * i, 2 * i + 1]).transpose(1, 0, 2, 3).reshape(128, PIECE)
    for i in range(8):
        Wp[tab["k"][0] + i] = tile_fm(w_kv, [2 * i, 2 * i + 1]).transpose(1, 0, 2, 3).reshape(128, PIECE)
        Wp[tab["v2"][0] + i] = w_kv[:, D + i * 256:D + (i + 1) * 256].reshape(KC, 128, 256).transpose(1, 0, 2).reshape(128, PIECE)
        Wp[tab["q"][0] + i] = tile_fm(w_sb_q[0], [2 * i, 2 * i + 1]).transpose(1, 0, 2, 3).reshape(128, PIECE)
        Wp[tab["o"][0] + i] = tile_fm(w_sb_o[0], [2 * i, 2 * i + 1]).transpose(1, 0, 2, 3).reshape(128, PIECE)
    return Wp


def const_tables(SEQ, NS, past_len):
    half = 128
    inv = ROPE_BASE ** (-np.arange(half, dtype=np.float32) / half)
    pos = np.concatenate([np.arange(SEQ), past_len + np.arange(NS)]).astype(np.float32)
    ang = inv[:, None].astype(np.float32) * pos[None, :].astype(np.float32)
    rope = np.stack([np.cos(ang), np.sin(ang)], axis=1).astype(np.float32)
    log_g = np.log1p(-np.power(2.0, -5.0 - np.arange(RH, dtype=np.float64)))
    idx = np.arange(128, dtype=np.float64)
    diff = idx[None, :] - idx[:, None]
    dt = np.where(diff[None] >= 0, np.exp(log_g[:, None, None] * np.maximum(diff, 0)[None]), 0.0) / 16.0
    dt = dt.transpose(1, 0, 2).astype(np.float32)
    dq = (np.exp(log_g[:, None] * (idx + 1.0)[None, :]) / 16.0).astype(np.float32)
    dq = np.broadcast_to(dq[None], (128, RH, 128)).copy()
    dk = np.zeros((128, 2 * RH), np.float32)
    dk[:, :RH] = np.exp(log_g[None, :] * (127.0 - idx)[:, None])
    dk[:NS, RH:] = np.exp(log_g[None, :] * (NS - 1.0 - idx[:NS])[:, None])
    ident = np.eye(128, dtype=np.float32)
    tri = (idx[:, None] >= idx[None, :]).astype(np.float32)
    ustr = (idx[:, None] > idx[None, :]).astype(np.float32)
    dmat = (np.arange(512, dtype=np.float32)[None, :] - np.arange(128, dtype=np.float32)[:, None])
    m4 = np.zeros((128, SH, NS), np.float32)
    for j in range(NS):
        for i in range(NS):
            if j < i:
                m4[j, :, i] = 1.0
    iota = np.broadcast_to(np.arange(128, dtype=np.float32)[None, :], (128, 128))
    misc = np.concatenate([ident, tri, ustr, dmat, m4.reshape(128, SH * NS), iota], axis=1).astype(np.float32)
    decay_c = [float(np.exp(log_g[h] * 128.0)) for h in range(RH)]
    decay_cs = [float(np.exp(log_g[h] * NS)) for h in range(RH)]
    return rope, dt.reshape(128, RH * 128), dq.reshape(128, RH * 128), dk, misc, decay_c, decay_cs


class Ctx:
    def __init__(self, xacc, xb, Bx, Bxb):
        self.xacc, self.xb, self.Bx, self.Bxb = xacc, xb, Bx, Bxb


def build_program(SEQ, NS, NPG, NPOOL, stop_after=None):
    NT = SEQ // TT
    assert NT % 4 == 0
    NOWN = NT // 4
    NKB = SEQ // 128
    HQ = SH * NS
    tab, NPIECE = piece_table()
    _, _, _, _, _, decay_c, decay_cs = const_tables(SEQ, NS, NPG * PAGE)
    scale = 128 ** -0.5
    MISCW = 128 * 3 + 512 + HQ + 128

    nc = bass.Bass("TRN2", target_bir_lowering=False)

    def din(name, shape, dt=F32):
        return nc.dram_tensor(name, list(shape), dt, kind="ExternalInput").ap()

    def dout(name, shape, dt=F32):
        return nc.dram_tensor(name, list(shape), dt, kind="ExternalOutput").ap()

    xT = din("xT", [128, KC, SEQ])
    xsT = din("xsT", [128, KC, NS])
    state = din("state", [RH, 256, 512])
    cache_k = din("cache_k", [NPOOL * PAGE, D])
    cache_v = din("cache_v", [NPOOL * PAGE, D])
    pt = din("pt", [NPG, 1], I32)
    Wd = din("W", [NPIECE, 128, PIECE])
    rope_d = din("rope", [128, 2, SEQ + NS])
    dt_d = din("dtab", [128, RH * 128])
    dq_d = din("dqtab", [128, RH * 128])
    dk_d = din("dktab", [128, 2 * RH])
    misc_d = din("misc", [128, MISCW])
    lnp_d = din("lnp", [128, 2 * 4 * KC])
    sbb_d = din("sbb", [1, SH])
    pcf_d = din("pcf", [128, NOWN])
    pci_d = din("pci", [128, NOWN], I32)

    yT = dout("yT", [NOWN, 128, KC * TT])
    ysT = dout("ysT", [128, KC * NS])
    retp = dout("retp", [RH, 256, 512])
    kTo = dout("kTo", [SH, 128, SEQ])
    vo = dout("vo", [SEQ, D])
    rets = dout("rets", [RH, 256, 512])
    ksTo = dout("ksTo", [128, HQ])
    vso = dout("vso", [NS, D])

    Ks = nc.dram_tensor("Ks", [SH, 128, SEQ], BF16).ap()
    Vs = nc.dram_tensor("Vs", [SH, 128, NKB * 128], BF16).ap()
    X1s = nc.dram_tensor("X1s", [NT * 128, KC * TT], F32).ap()
    XBs = nc.dram_tensor("XBs", [NT * 128, KC * TT], BF16).ap()
    Wbc = nc.dram_tensor("Wbc", [NPIECE, 128, PIECE], BF16).ap()

    with contextlib.ExitStack() as st:
        def sb(name, shape, dt):
            return st.enter_context(nc.sbuf_tensor("s_" + name, list(shape), dt))

        def psum(name, shape, dt):
            return st.enter_context(nc.psum_tensor("p_" + name, list(shape), dt))

        xacc = sb("xacc", [128, KC, TT], F32)
        xb = sb("xb", [128, KC, TT], BF16)
        NWB = 3
        wbuf = [sb("wb%d" % i, [128, PIECE], BF16) for i in range(NWB)]
        hbuf = sb("hbuf", [128, KC, TT], BF16)
        kvb = sb("kvb", [128, max(2 * NKB * 128, D + HQ)], BF16)
        Ssb = sb("S", [128, RH * 2, 512], F32)
        Sbf = sb("Sbf", [128, 2, 512], BF16)
        rope_t = sb("rope_t", [128, 2, TT], F32)
        rq = sb("rq", [128, 2, TT], BF16)
        qh = sb("qh", [128, 2, TT], BF16)
        rk = sb("rk", [128, 2, TT], BF16)
        vsb = sb("vsb", [128, 4, 512], BF16)
        sg = sb("sg", [128, 4, TT], F32)
        og2 = [sb("og_a", [128, 4, TT], BF16), sb("og_b", [128, 4, TT], BF16)]
        pool = sb("tpool", [128, 8, TT], F32)
        tmp = [pool[:, i, :] for i in range(8)]
        scb = sb("scb", [128, 128], BF16)
        kd = sb("kd", [128, 256], BF16)
        onb = sb("onb", [128, 512], BF16)
        stats = sb("stats", [128, 6], F32)
        mv = sb("mv", [128, 2], F32)
        rstd = sb("rstd", [128, 1], F32)
        dt_t = sb("dt_t", [128, RH, 128], F32)
        dq_t = sb("dq_t", [128, RH, 128], F32)
        dk_t = sb("dk_t", [128, 2 * RH], F32)
        misc_t = sb("misc_t", [128, MISCW], F32)
        identb = sb("identb", [128, 128], BF16)
        ones = sb("ones", [128, 128], F32)
        lnp = sb("lnp", [128, 2 * 4 * KC], F32)
        lnpa = sb("lnpa", [128, 2 * 4 * KC], F32)
        sbb = sb("sbb", [128, SH], F32)
        pcf = sb("pcf", [128, NOWN], F32)
        pci = sb("pci", [128, NOWN], I32)
        qT2 = sb("qT2", [128, 2, TT], BF16)
        w_t = [sb("w_t%d" % i, [128, TT], BF16) for i in range(2)]
        kb16 = [sb("kb16_0", [128, TT], BF16)] * 2
        vf = [sb("vf0", [128, 256], F32)] * 2
        vb16 = [sb("vb16_0", [128, 256], BF16)] * 2
        xsacc = sb("xsacc", [128, KC, NS], F32)
        xsb = sb("xsb", [128, KC, NS], BF16)
        qTs = sb("qTs", [128, SH, NS], BF16)
        pti = sb("pti", [128, 1], I32)
        ptf = sb("ptf", [128, 1], F32)
        idxi = sb("idxi", [128, PAGE], I32)
        rflat = rope_t[:].rearrange("p a n -> p (a n)")
        sbb64 = rflat[:, 0:HQ]
        Tt = rflat[:, HQ:2 * HQ]
        en = rflat[:, 2 * HQ:3 * HQ]
        spn = rflat[:, 3 * HQ:4 * HQ]
        un = rflat[:, 4 * HQ:5 * HQ]
        oacc = rflat[:, 5 * HQ:6 * HQ]
        idxf = rflat[:, 6 * HQ:6 * HQ + PAGE]
        wn = sb("wn", [128, HQ], BF16)
        obs = sb("obs", [128, SH, NS], BF16)
        ksn = kvb[:, D:D + HQ].rearrange("p (h q) -> p h q", h=SH)
        vsn = kvb[:, 0:D]

        NRING = 5
        pring = [psum("pr%d" % i, [128, 512], F32) for i in range(NRING)]
        accA = psum("accA", [128, 512], F32)
        accB = psum("accB", [128, 512], F32)
        tpb = psum("tpb", [128, 1024], BF16)

        blk = st.enter_context(nc.Block())
        S = Sched(nc, st)
        B = {}

        def bf(name):
            if name not in B:
                B[name] = Buf(name)
            return B[name]

        Bx = [bf("xacc%d" % c) for c in range(KC)]
        Bxb = [bf("xb%d" % c) for c in range(KC)]
        Bh = [bf("hbuf%d" % c) for c in range(KC)]
        Bw = [bf("wb%d" % i) for i in range(NWB)]
        Bring = [bf("pr%d" % i) for i in range(NRING)]
        Bt = [bf("tmp%d" % i) for i in range(8)]
        ring_pos = [0]
        ctxP = Ctx(xacc, xb, Bx, Bxb)
        ctxS = Ctx(xsacc, xsb, [bf("xsacc")] * KC, [bf("xsb")] * KC)

        def ring():
            i = ring_pos[0] % NRING
            ring_pos[0] += 1
            return pring[i], Bring[i]

        plan = []
        wpos = [0]
        wissued = [0]

        Bwc = {}

        def w_issue_upto(k):
            while wissued[0] < min(len(plan), k):
                i = wissued[0]
                pid = plan[i]
                if pid not in Bwc:
                    S.dma("pool", wbuf[i % NWB][:], Wd[pid], writes=[Bw[i % NWB]])
                    Bwc[pid] = Buf("wc%d" % pid)
                    S.dma("sp", Wbc[pid], wbuf[i % NWB][:], reads=[Bw[i % NWB]], writes=[Bwc[pid]])
                else:
                    S.dma("sp", wbuf[i % NWB][:], Wbc[pid], reads=[Bwc[pid]], writes=[Bw[i % NWB]])
                wissued[0] += 1

        def wget(pid):
            i = wpos[0]
            assert plan[i] == pid, (i, plan[i], pid)
            w_issue_upto(i + NWB)
            wpos[0] += 1
            return wbuf[i % NWB], Bw[i % NWB]

        def wprefetch():
            w_issue_upto(wpos[0] + NWB - 1)

        def plan_mlp(l):
            p = []
            for g in range(4):
                p += [tab["ff1_%d" % l][0] + g * 8 + i for i in range(8)]
                p += [tab["ff2_%d" % l][0] + g * 8 + i for i in range(8)]
            return p

        def plan_layer0():
            p = []
            for h in range(RH):
                p += [tab["qk"][0] + 2 * h, tab["qk"][0] + 2 * h + 1]
                p += [tab["v"][0] + 2 * h, tab["v"][0] + 2 * h + 1]
                p += [tab["g"][0] + 2 * h, tab["g"][0] + 2 * h + 1]
                if h > 0:
                    p += [tab["out"][0] + 2 * (h - 1), tab["out"][0] + 2 * (h - 1) + 1]
            p += [tab["out"][0] + 2 * (RH - 1), tab["out"][0] + 2 * (RH - 1) + 1]
            p += plan_mlp(0)
            p += [tab["k"][0] + i for i in range(8)]
            p += [tab["v2"][0] + i for i in range(8)]
            return p

        def plan_layer1():
            p = [tab["q"][0] + i for i in range(8)]
            p += [tab["o"][0] + i for i in range(8)]
            p += plan_mlp(1)
            return p

        for t in range(NT):
            plan.extend(plan_layer0())
        for j in range(NOWN):
            plan.extend(plan_layer1())
        if stop_after is None:
            plan.extend(plan_layer0())
            plan.extend(plan_layer1())

        def lin_fm(pids, cpp, KCn, rhs_fn, N, consume):
            fc = 0
            for pid in pids:
                wap, wb_ = wget(pid)
                wv = wap[:].rearrange("p (j k m) -> p j k m", j=cpp, k=KCn)
                for j in range(cpp):
                    pb, pbb = ring()
                    for kc in range(KCn):
                        rap, rbufs = rhs_fn(kc)
                        S.op("pe", lambda e, pb=pb, l=wv[:, j, kc, :], r=rap, kc=kc: e.matmul(
                            pb[:, 0:N], lhsT=l, rhs=r, start=(kc == 0), stop=(kc == KCn - 1)),
                            [wb_] + rbufs, [pbb])
                    consume(fc, pb, pbb)
                    fc += 1

        def xb_rhs(cx, N):
            return lambda kc: (cx.xb[:, kc, 0:N], [cx.Bxb[kc]])

        def add_into_xacc(cx, N):
            def consume(fc, pb, pbb):
                S.op("dve", lambda e: e.tensor_tensor(out=cx.xacc[:, fc, 0:N], in0=pb[:, 0:N], in1=cx.xacc[:, fc, 0:N], op=ALU.add),
                     [pbb, cx.Bx[fc]], [cx.Bx[fc]])
            return consume

        def layer_norm(cx, N, li, oscale_alpha):
            xa, xbt = cx.xacc, cx.xb
            s1, b1 = ring()
            s2, b2 = ring()
            mean_t, rstd_t = tmp[3], tmp[4]
            for c in range(KC):
                S.op("pe", lambda e, c=c: e.matmul(s1[:, 0:N], lhsT=ones[:], rhs=xa[:, c, 0:N], start=(c == 0), stop=(c == KC - 1)),
                     [bf("ones"), cx.Bx[c]], [b1])
            for c in range(KC):
                sq = tmp[c % 3]
                S.op("act", lambda e, c=c, sq=sq: e.activation(out=sq[:, 0:N], in_=xa[:, c, 0:N], func=AF.Square),
                     [cx.Bx[c]], [Bt[c % 3]])
                S.op("pe", lambda e, c=c, sq=sq: e.matmul(s2[:, 0:N], lhsT=ones[:], rhs=sq[:, 0:N], start=(c == 0), stop=(c == KC - 1)),
                     [bf("ones"), Bt[c % 3]], [b2])
            S.op("dve", lambda e: e.tensor_scalar(out=mean_t[:, 0:N], in0=s1[:, 0:N], scalar1=1.0 / D, scalar2=None, op0=ALU.mult),
                 [b1], [Bt[3]])
            S.op("dve", lambda e: e.tensor_tensor(out=tmp[0][:, 0:N], in0=mean_t[:, 0:N], in1=mean_t[:, 0:N], op=ALU.mult),
                 [Bt[3]], [Bt[0]])
            S.op("dve", lambda e: e.scalar_tensor_tensor(out=rstd_t[:, 0:N], in0=s2[:, 0:N], scalar=1.0 / D, in1=tmp[0][:, 0:N],
                                                         op0=ALU.mult, op1=ALU.subtract),
                 [b2, Bt[0]], [Bt[4]])
            S.op("act", lambda e: e.activation(out=rstd_t[:, 0:N], in_=rstd_t[:, 0:N], func=AF.Ln, bias=LN_EPS), [Bt[4]], [Bt[4]])
            S.op("act", lambda e: e.activation(out=rstd_t[:, 0:N], in_=rstd_t[:, 0:N], func=AF.Exp, scale=-0.5), [Bt[4]], [Bt[4]])
            gp = lnpa if oscale_alpha else lnp
            for c in range(KC):
                tm = tmp[c % 3]
                tb_ = Bt[c % 3]
                S.op("dve", lambda e, c=c, tm=tm: e.tensor_tensor(out=tm[:, 0:N], in0=xa[:, c, 0:N], in1=mean_t[:, 0:N], op=ALU.subtract),
                     [cx.Bx[c], Bt[3]], [tb_])
                S.op("dve", lambda e, c=c, tm=tm: e.tensor_tensor(out=tm[:, 0:N], in0=tm[:, 0:N], in1=rstd_t[:, 0:N], op=ALU.mult),
                     [tb_, Bt[4]], [tb_])
                gcol = li * KC + c
                bcol = 4 * KC + li * KC + c
                S.op("act", lambda e, c=c, tm=tm, gcol=gcol, bcol=bcol: e.activation(
                    out=xbt[:, c, 0:N], in_=tm[:, 0:N], func=AF.Identity, scale=lnp[:, gcol:gcol + 1], bias=lnp[:, bcol:bcol + 1]),
                    [tb_, bf("lnp")], [cx.Bxb[c]])
                S.op("act", lambda e, c=c, tm=tm, gcol=gcol, bcol=bcol: e.activation(
                    out=xa[:, c, 0:N], in_=tm[:, 0:N], func=AF.Identity, scale=gp[:, gcol:gcol + 1], bias=gp[:, bcol:bcol + 1]),
                    [tb_, bf("lnp")], [cx.Bx[c]])

        def mlp(cx, N, l):
            for g in range(4):
                def cons1(fc, pb, pbb):
                    tm = tmp[5 + fc % 3]
                    tb_ = Bt[5 + fc % 3]
                    S.op("act", lambda e: e.activation(out=tm[:, 0:N], in_=pb[:, 0:N], func=AF.Relu), [pbb], [tb_])
                    S.op("dve", lambda e: e.tensor_tensor(out=hbuf[:, fc, 0:N], in0=tm[:, 0:N], in1=tm[:, 0:N], op=ALU.mult),
                         [tb_], [Bh[fc]])
                lin_fm([tab["ff1_%d" % l][0] + g * 8 + i for i in range(8)], 2, KC, xb_rhs(cx, N), N, cons1)
                lin_fm([tab["ff2_%d" % l][0] + g * 8 + i for i in range(8)], 2, KC,
                       lambda kc: (hbuf[:, kc, 0:N], [Bh[kc]]), N, add_into_xacc(cx, N))

        def retention(cx, N, cs, nchunk, rope_col0, sample):
            xbt = cx.xb
            S.dma("sp", rope_t[:, :, 0:N], rope_d[:, :, rope_col0:rope_col0 + N], writes=[bf("rope")])
            dkoff = RH if sample else 0
            ow = {}

            def outproj_part(hh, parts):
                ogh = og2[hh % 2]
                for part in parts:
                    j = part // 2
                    if (hh, j) not in ow:
                        wap, wb_ = wget(tab["out"][0] + 2 * hh + j)
                        ow[(hh, j)] = (wap[:].rearrange("p (f k m) -> p f k m", f=8, k=4), wb_)
                    wv, wb_ = ow[(hh, j)]
                    for fl in range(4 * (part % 2), 4 * (part % 2) + 4):
                        fc = 8 * j + fl
                        pb, pbb = ring()
                        for kc in range(4):
                            S.op("pe", lambda e, pb=pb, fl=fl, kc=kc: e.matmul(pb[:, 0:N], lhsT=wv[:, fl, kc, :], rhs=ogh[:, kc, 0:N], start=(kc == 0), stop=(kc == 3)),
                                 [wb_, bf("og%d" % (hh % 2))], [pbb])
                        add_into_xacc(cx, N)(fc, pb, pbb)

            for h in range(RH):
                dcy = decay_cs[h] if sample else decay_c[h]
                og = og2[h % 2]
                ogb = bf("og%d" % (h % 2))
                S.op("act", lambda e, h=h: e.copy(out=Sbf[:, :, :], in_=Ssb[:, 2 * h:2 * h + 2, :]), [bf("S%d" % h)], [bf("Sbf")])
                for which, dst, dbuf in ((0, rq, "rq"), (1, rk, "rk")):
                    held = []

                    def cons(fc, pb, pbb, held=held):
                        held.append((pb, pbb))
                    lin_fm([tab["qk"][0] + 2 * h + which], 2, KC, xb_rhs(cx, N), N, cons)
                    (p1, b1), (p2, b2) = held
                    cosv = rope_t[:, 0, 0:N]
                    sinv = rope_t[:, 1, 0:N]
                    S.op("dve", lambda e: e.tensor_tensor(out=tmp[0][:, 0:N], in0=p1[:, 0:N], in1=cosv, op=ALU.mult), [b1, bf("rope")], [Bt[0]])
                    S.op("dve", lambda e: e.tensor_tensor(out=tmp[1][:, 0:N], in0=p2[:, 0:N], in1=sinv, op=ALU.mult), [b2, bf("rope")], [Bt[1]])
                    S.op("dve", lambda e: e.tensor_tensor(out=tmp[2][:, 0:N], in0=p1[:, 0:N], in1=sinv, op=ALU.mult), [b1, bf("rope")], [Bt[2]])
                    S.op("dve", lambda e: e.tensor_tensor(out=tmp[3][:, 0:N], in0=p2[:, 0:N], in1=cosv, op=ALU.mult), [b2, bf("rope")], [Bt[3]])
                    S.op("dve", lambda e, dst=dst: e.tensor_tensor(out=dst[:, 0, 0:N], in0=tmp[0][:, 0:N], in1=tmp[1][:, 0:N], op=ALU.subtract),
                         [Bt[0], Bt[1]], [bf(dbuf)])
                    S.op("dve", lambda e, dst=dst: e.tensor_tensor(out=dst[:, 1, 0:N], in0=tmp[2][:, 0:N], in1=tmp[3][:, 0:N], op=ALU.add),
                         [Bt[2], Bt[3]], [bf(dbuf)])
                for half in range(2):
                    S.op("dve", lambda e, half=half: e.tensor_tensor(
                        out=qh[:, half, 0:N].rearrange("p (c i) -> p c i", i=cs),
                        in0=rq[:, half, 0:N].rearrange("p (c i) -> p c i", i=cs),
                        in1=dq_t[:, h, 0:cs].unsqueeze(1).to_broadcast([128, nchunk, cs]), op=ALU.mult),
                        [bf("rq"), bf("dq")], [bf("qh")])
                for j in range(2):
                    wap, wb_ = wget(tab["v"][0] + 2 * h + j)
                    wv = wap[:].rearrange("p (k n) -> p k n", k=KC)
                    for c in range(nchunk):
                        pb, pbb = ring()
                        for kc in range(KC):
                            S.op("pe", lambda e, pb=pb, kc=kc, c=c: e.matmul(pb[0:cs, 0:256], lhsT=xbt[:, kc, c * 128:c * 128 + cs], rhs=wv[:, kc, :],
                                                                          start=(kc == 0), stop=(kc == KC - 1)), [wb_, cx.Bxb[kc]], [pbb])
                        S.op("act", lambda e, pb=pb, c=c, j=j: e.copy(out=vsb[0:cs, c, j * 256:(j + 1) * 256], in_=pb[0:cs, 0:256]), [pbb], [bf("vsb")])

                def consg(fc, pb, pbb):
                    S.op("act", lambda e: e.activation(out=sg[:, fc, 0:N], in_=pb[:, 0:N], func=AF.Silu), [pbb], [bf("sg")])
                lin_fm([tab["g"][0] + 2 * h, tab["g"][0] + 2 * h + 1], 2, KC, xb_rhs(cx, N), N, consg)
                wprefetch()
                for c in range(nchunk):
                    cr = slice(c * 128, c * 128 + cs)
                    pb, pbb = ring()
                    for half in range(2):
                        S.op("pe", lambda e, half=half: e.matmul(pb[0:cs, 0:cs], lhsT=rk[:, half, cr], rhs=rq[:, half, cr], start=(half == 0), stop=(half == 1)),
                             [bf("rk"), bf("rq")], [pbb])
                    S.op("dve", lambda e: e.tensor_tensor(out=scb[0:cs, 0:cs], in0=pb[0:cs, 0:cs], in1=dt_t[0:cs, h, 0:cs], op=ALU.mult),
                         [pbb, bf("dt")], [bf("scb")])
                    S.op("pe", lambda e: e.matmul(accA[0:cs, :], lhsT=scb[0:cs, 0:cs], rhs=vsb[0:cs, c, :], start=True, stop=False),
                         [bf("scb"), bf("vsb")], [bf("accA")])
                    for half in range(2):
                        S.op("pe", lambda e, half=half: e.matmul(accA[0:cs, :], lhsT=qh[:, half, cr], rhs=Sbf[:, half, :], start=False, stop=(half == 1)),
                             [bf("qh"), bf("Sbf")], [bf("accA")])
                    for half in range(2):
                        S.op("pe", lambda e, half=half: e.transpose(tpb[0:cs, half * 128:(half + 1) * 128], rk[:, half, cr], identb[:, :]),
                             [bf("rk"), bf("identb")], [bf("tpbA")])
                    S.op("act", lambda e: e.activation(out=kd[0:cs, :], in_=tpb[0:cs, 0:256], func=AF.Identity, scale=dk_t[0:cs, dkoff + h:dkoff + h + 1], bias=0.0),
                         [bf("tpbA"), bf("dk")], [bf("kd")])
                    for half in range(2):
                        pd, pdb = ring()
                        S.op("pe", lambda e, half=half, pd=pd: e.matmul(pd[:, :], lhsT=kd[0:cs, half * 128:(half + 1) * 128], rhs=vsb[0:cs, c, :], start=True, stop=True),
                             [bf("kd"), bf("vsb")], [pdb])
                        S.op("dve", lambda e, half=half, pd=pd: e.scalar_tensor_tensor(out=Ssb[:, 2 * h + half, :], in0=Ssb[:, 2 * h + half, :], scalar=dcy, in1=pd[:, :],
                                                                                      op0=ALU.mult, op1=ALU.add), [pdb, bf("S%d" % h)], [bf("S%d" % h)])
                    S.op("dve", lambda e: e.bn_stats(out=stats[0:cs, :], in_=accA[0:cs, :]), [bf("accA")], [bf("stats")])
                    S.op("dve", lambda e: e.bn_aggr(out=mv[0:cs, :], in_=stats[0:cs, :]), [bf("stats")], [bf("mv")])
                    S.op("act", lambda e: e.activation(out=rstd[0:cs, :], in_=mv[0:cs, 1:2], func=AF.Ln, bias=GN_EPS), [bf("mv")], [bf("rstd")])
                    S.op("act", lambda e: e.activation(out=rstd[0:cs, :], in_=rstd[0:cs, :], func=AF.Exp, scale=-0.5), [bf("rstd")], [bf("rstd")])
                    S.op("dve", lambda e: e.tensor_scalar(out=onb[0:cs, :], in0=accA[0:cs, :], scalar1=mv[0:cs, 0:1], scalar2=rstd[0:cs, 0:1],
                                                          op0=ALU.subtract, op1=ALU.mult), [bf("accA"), bf("mv"), bf("rstd")], [bf("onb")])
                    if c + 1 < nchunk:
                        S.op("act", lambda e: e.copy(out=Sbf[:, :, :], in_=Ssb[:, 2 * h:2 * h + 2, :]), [bf("S%d" % h)], [bf("Sbf")])
                    for vc in range(4):
                        S.op("pe", lambda e, vc=vc: e.transpose(tpb[:, 512 + vc * 128:512 + vc * 128 + cs], onb[0:cs, vc * 128:(vc + 1) * 128], identb[0:cs, 0:cs]),
                             [bf("onb"), bf("identb")], [bf("tpbB")])
                    S.op("dve", lambda e: e.tensor_tensor(out=og[:, :, cr], in0=tpb[:, 512:1024].rearrange("p (v i) -> p v i", v=4)[:, :, 0:cs],
                                                          in1=sg[:, :, cr], op=ALU.mult), [bf("tpbB"), bf("sg")], [ogb])
                    if h > 0:
                        if nchunk == 4:
                            outproj_part(h - 1, [c])
                        elif c == nchunk - 1:
                            outproj_part(h - 1, list(range(4)))
            outproj_part(RH - 1, list(range(4)))

        def kv_proj(cx, N, cs, nchunk, t, sample):
            xbt = cx.xb

            def consk(fc, pb, pbb):
                i = fc % 2
                S.op("act", lambda e: e.copy(out=tmp[i][:, 0:N], in_=pb[:, 0:N]), [pbb], [Bt[i]])
                if sample:
                    S.op("dve", lambda e: e.tensor_copy(out=ksn[:, fc, :], in_=tmp[i][:, 0:N]), [Bt[i]], [bf("ksn")])
                    S.dma("sp", ksTo[:, fc * NS:(fc + 1) * NS], tmp[i][:, 0:N], reads=[Bt[i]], writes=[bf("o_ksT")])
                else:
                    S.op("dve", lambda e: e.tensor_copy(out=kb16[i][:, 0:N], in_=tmp[i][:, 0:N]), [Bt[i]], [bf("kb16_0")])
                    S.dma("sp", kTo[fc, :, t * TT:t * TT + N], tmp[i][:, 0:N], reads=[Bt[i]], writes=[bf("o_kT")])
                    S.dma("sp", Ks[fc, :, t * TT:t * TT + N], kb16[i][:, 0:N], reads=[bf("kb16_0")], writes=[bf("Ks%d" % t)])
            lin_fm([tab["k"][0] + i for i in range(8)], 2, KC, xb_rhs(cx, N), N, consk)
            cnt = 0
            for j in range(8):
                wap, wb_ = wget(tab["v2"][0] + j)
                wv = wap[:].rearrange("p (k n) -> p k n", k=KC)
                for c in range(nchunk):
                    pb, pbb = ring()
                    for kc in range(KC):
                        S.op("pe", lambda e, pb=pb, kc=kc, c=c: e.matmul(pb[0:cs, 0:256], lhsT=xbt[:, kc, c * 128:c * 128 + cs], rhs=wv[:, kc, :],
                                                                      start=(kc == 0), stop=(kc == KC - 1)), [wb_, cx.Bxb[kc]], [pbb])
                    i = cnt % 2
                    cnt += 1
                    S.op("act", lambda e, pb=pb, i=i: e.copy(out=vf[i][0:cs, :], in_=pb[0:cs, 0:256]), [pbb], [bf("vf0")])
                    if sample:
                        S.op("dve", lambda e, i=i, j=j: e.tensor_copy(out=vsn[0:cs, j * 256:(j + 1) * 256], in_=vf[i][0:cs, :]), [bf("vf0")], [bf("vsn")])
                        S.dma("sp", vso[0:cs, j * 256:(j + 1) * 256], vf[i][0:cs, :], reads=[bf("vf0")], writes=[bf("o_vs")])
                    else:
                        S.op("dve", lambda e, i=i: e.tensor_copy(out=vb16[i][0:cs, :], in_=vf[i][0:cs, :]), [bf("vf0")], [bf("vb16_0")])
                        r0 = t * TT + c * 128
                        S.dma("sp", vo[r0:r0 + cs, j * 256:(j + 1) * 256], vf[i][0:cs, :], reads=[bf("vf0")], writes=[bf("o_v")])
                        kb = t * 4 + c
                        for hh in range(2):
                            S.dma("sp", Vs[2 * j + hh, 0:cs, kb * 128:(kb + 1) * 128],
                                  vb16[i][0:cs, hh * 128:(hh + 1) * 128], reads=[bf("vb16_0")], writes=[bf("Vs%d" % t)])

        def sb_attention_prompt(cx, j):
            N = TT
            KTh = kvb[:, 0:SEQ]
            Vh = kvb[:, NKB * 128:2 * NKB * 128]
            ksb = [bf("Ks%d" % t) for t in range(NT)]
            vsb_ = [bf("Vs%d" % t) for t in range(NT)]
            e_t = [tmp[0], tmp[1], tmp[2]]
            sp_t = [tmp[3], tmp[4], tmp[5]]
            Be = [Bt[0], Bt[1], Bt[2]]
            Bsp = [Bt[3], Bt[4], Bt[5]]
            dqm, lacc = sg[:, 0, :], sg[:, 1, :]
            u_t = [sg[:, 2, :], sg[:, 3, :]]
            Bdqm, Blacc, Bu = bf("sg"), bf("lacc"), [bf("u0"), bf("u1")]
            S.op("dve", lambda e: e.tensor_scalar(out=dqm, in0=misc_t[:, 384:896], scalar1=pcf[:, j:j + 1], scalar2=None, op0=ALU.add),
                 [bf("misc"), bf("pcf")], [Bdqm])
            kbs = list(reversed(range(NKB)))
            for hp in range(8):
                def consq(fc, pb, pbb):
                    S.op("act", lambda e: e.copy(out=qT2[:, fc, 0:N], in_=pb[:, 0:N]), [pbb], [bf("qT2")])
                lin_fm([tab["q"][0] + hp], 2, KC, xb_rhs(cx, N), N, consq)
                wprefetch()
                for jj in range(2):
                    h = 2 * hp + jj
                    S.dma("sp", KTh, Ks[h], reads=ksb, writes=[bf("KTh")])
                    S.dma("sp", Vh, Vs[h], reads=vsb_, writes=[bf("Vh")])
                    S.op("dve", lambda e: e.memset(lacc, 0.0), [], [Blacc])

                    def stage1(n):
                        kb = kbs[n]
                        i3 = n % 3
                        z, zb = ring()
                        S.op("pe", lambda e: e.matmul(z[:, :], lhsT=KTh[:, kb * 128:(kb + 1) * 128], rhs=qT2[:, jj, :], start=True, stop=True),
                             [bf("KTh"), bf("qT2")], [zb])
                        S.op("act", lambda e: e.activation(out=e_t[i3], in_=z[:, :], func=AF.Exp, scale=scale, bias=sbb[:, h:h + 1]),
                             [zb, bf("sbb")], [Be[i3]])
                        S.op("dve", lambda e: e.scalar_tensor_tensor(out=e_t[i3], in0=dqm, scalar=float(kb * 128), in1=e_t[i3],
                                                                     op0=ALU.is_gt, op1=ALU.mult), [Bdqm, Be[i3]], [Be[i3]])
                        S.op("act", lambda e: e.activation(out=sp_t[i3], in_=e_t[i3], func=AF.Ln, bias=1.0), [Be[i3]], [Bsp[i3]])

                    def stage2(n):
                        i3, i2 = n % 3, n % 2
                        a, ab = ring()
                        S.op("pe", lambda e: e.matmul(a[:, :], lhsT=misc_t[:, 128:256], rhs=sp_t[i3], start=True, stop=False),
                             [bf("misc"), Bsp[i3]], [ab])
                        S.op("pe", lambda e: e.matmul(a[:, :], lhsT=ones[:, :], rhs=lacc, start=False, stop=True), [bf("ones"), Blacc], [ab])
                        S.op("act", lambda e: e.activation(out=u_t[i2], in_=a[:, :], func=AF.Exp, scale=-1.0), [ab], [Bu[i2]])
                        S.op("dve", lambda e: e.tensor_tensor(out=w_t[i2][:, :], in0=e_t[i3], in1=u_t[i2], op=ALU.mult),
                             [Be[i3], Bu[i2]], [bf("w%d" % i2)])
                        S.op("dve", lambda e: e.tensor_tensor(out=lacc, in0=lacc, in1=sp_t[i3], op=ALU.add),
                             [Blacc, Bsp[i3]], [Blacc])

                    def stage3(n):
                        kb = kbs[n]
                        i2 = n % 2
                        S.op("pe", lambda e: e.matmul(accB[:, :], lhsT=Vh[:, kb * 128:(kb + 1) * 128], rhs=w_t[i2][:, :], start=(n == 0), stop=(n == NKB - 1)),
                             [bf("Vh"), bf("w%d" % i2)], [bf("accB")])

                    for n in range(NKB + 2):
                        if n < NKB:
                            stage1(n)
                        if 0 <= n - 1 < NKB:
                            stage2(n - 1)
                        if 0 <= n - 2 < NKB:
                            stage3(n - 2)
                    S.op("act", lambda e: e.copy(out=hbuf[:, h, :], in_=accB[:, :]), [bf("accB")], [Bh[h]])

        def layer1_tail(cx, N, ob_rhs):
            lin_fm([tab["o"][0] + i for i in range(8)], 2, KC, ob_rhs, N, add_into_xacc(cx, N))
            layer_norm(cx, N, 2, True)
            mlp(cx, N, 1)
            layer_norm(cx, N, 3, False)

        def sb_attention_sample(cx):
            N = NS
            P_ = NPG
            Eall = xacc[:].rearrange("p c n -> p (c n)")[:, 0:PAGE * HQ].rearrange("p (r q) -> p r q", q=HQ)
            Aall = Ssb[:].rearrange("p c n -> p (c n)")[:, 0:PAGE * HQ].rearrange("p (r q) -> p r q", q=HQ)
            Wall = xb[:].rearrange("p c n -> p (c n)")[:, 0:PAGE * HQ].rearrange("p (r q) -> p r q", q=HQ)
            BE, BA, BW = bf("Eall"), bf("Aall"), bf("Wall")
            allS = [bf("S%d" % h) for h in range(RH)]
            hflat = hbuf[:].rearrange("p c n -> p (c n)")
            Kb = hflat[:, 0:D]
            KT = hflat[:, D:2 * D].rearrange("p (h g) -> p h g", h=SH)
            Vb = hflat[:, 2 * D:3 * D]
            pflat = pool[:].rearrange("p c n -> p (c n)")
            gat = [pflat[:, 0:D], pflat[:, D:2 * D]]
            Bg = [bf("gat0"), bf("gat1")]

            def consq(fc, pb, pbb):
                S.op("act", lambda e: e.copy(out=qTs[:, fc, :], in_=pb[:, 0:N]), [pbb], [bf("qTs")])
            lin_fm([tab["q"][0] + i for i in range(8)], 2, KC, xb_rhs(cx, N), N, consq)
            wprefetch()
            S.dma("sp", pti[0:P_, :], pt[:, :], writes=[bf("pti")])
            S.op("dve", lambda e: e.tensor_copy(out=ptf[0:P_, :], in_=pti[0:P_, :]), [bf("pti")], [bf("ptf")])
            S.op("dve", lambda e: e.tensor_scalar(out=ptf[0:P_, :], in0=ptf[0:P_, :], scalar1=float(PAGE), scalar2=None, op0=ALU.mult), [bf("ptf")], [bf("ptf")])
            S.op("dve", lambda e: e.tensor_scalar(out=idxf[0:P_, :], in0=misc_t[0:P_, 896 + HQ:896 + HQ + PAGE], scalar1=ptf[0:P_, 0:1], scalar2=None, op0=ALU.add),
                 [bf("ptf"), bf("misc")], [bf("idxf")])
            S.op("dve", lambda e: e.tensor_copy(out=idxi[0:P_, :], in_=idxf[0:P_, :]), [bf("idxf")], [bf("pti")])
            S.op("dve", lambda e: e.tensor_copy(out=sbb64[:].rearrange("p (h q) -> p h q", h=SH), in_=sbb[:].unsqueeze(2).to_broadcast([128, SH, NS])),
                 [bf("sbb")], [bf("sbb64")])
            for r in range(PAGE):
                g = r % 2
                S.dma("pool", None, None, reads=[bf("pti")], writes=[Bg[g]] + (Bt if r < 2 else []),
                      fn=lambda e, g=g, r=r: e.indirect_dma_start(out=gat[g][0:P_, :], out_offset=None, in_=cache_k[:, :],
                                                                  in_offset=bass.IndirectOffsetOnAxis(ap=idxi[0:P_, r:r + 1], axis=0)))
                if r % 2 == 0:
                    S.op("act", lambda e, g=g: e.copy(out=Kb[0:P_, :], in_=gat[g][0:P_, :]), [Bg[g]], [bf("Kb")] + (Bh if r == 0 else []))
                else:
                    S.op("dve", lambda e, g=g: e.tensor_copy(out=Kb[0:P_, :], in_=gat[g][0:P_, :]), [Bg[g]], [bf("Kb")])
                z, zb = ring()
                for hh in range(2):
                    for hl in range(8):
                        h = hh * 8 + hl
                        S.op("pe", lambda e, hl=hl, h=h: e.transpose(tpb[:, hl * 128:hl * 128 + P_], Kb[0:P_, h * 128:(h + 1) * 128], identb[0:P_, 0:P_]),
                             [bf("Kb"), bf("identb")], [bf("tpbA"), bf("tpbB")])
                    S.op("act" if hh == 0 else "dve",
                         (lambda e, hh=hh: e.copy(out=KT[:, hh * 8:(hh + 1) * 8, 0:P_], in_=tpb[:, :].rearrange("p (h g) -> p h g", h=8)[:, :, 0:P_])) if hh == 0 else
                         (lambda e, hh=hh: e.tensor_copy(out=KT[:, hh * 8:(hh + 1) * 8, 0:P_], in_=tpb[:, :].rearrange("p (h g) -> p h g", h=8)[:, :, 0:P_])),
                         [bf("tpbA"), bf("tpbB")], [bf("KT%d" % hh)] + (Bh if r == 0 else []))
                    for hl in range(8):
                        h = hh * 8 + hl
                        S.op("pe", lambda e, h=h: e.matmul(z[0:P_, h * NS:(h + 1) * NS], lhsT=KT[:, h, 0:P_], rhs=qTs[:, h, :], start=True, stop=True),
                             [bf("KT%d" % hh), bf("qTs")], [zb])
                S.op("dve", lambda e, r=r: e.scalar_tensor_tensor(out=Eall[0:P_, r, :], in0=z[0:P_, 0:HQ], scalar=scale, in1=sbb64[0:P_, :], op0=ALU.mult, op1=ALU.add),
                     [zb, bf("sbb64")], [BE] + Bx)
            S.op("act", lambda e: e.activation(out=Eall[0:P_], in_=Eall[0:P_], func=AF.Exp), [BE], [BE])
            S.op("act", lambda e: e.activation(out=Aall[0:P_], in_=Eall[0:P_], func=AF.Ln, bias=1.0), [BE] + allS, [BA] + allS)
            S.op("dve", lambda e: e.tensor_reduce(out=Tt[0:P_, :], in_=Aall[0:P_].rearrange("p r q -> p q r"), axis=mybir.AxisListType.X, op=ALU.add),
                 [BA], [bf("Tt")])
            zn, znb = ring()
            for h in range(SH):
                S.op("pe", lambda e, h=h: e.matmul(zn[0:NS, h * NS:(h + 1) * NS], lhsT=ksn[:, h, :], rhs=qTs[:, h, :], start=True, stop=True),
                     [bf("ksn"), bf("qTs")], [znb])
            S.op("dve", lambda e: e.scalar_tensor_tensor(out=en[0:NS, :], in0=zn[0:NS, 0:HQ], scalar=scale, in1=sbb64[0:NS, :], op0=ALU.mult, op1=ALU.add),
                 [znb, bf("sbb64")], [bf("en")])
            S.op("act", lambda e: e.activation(out=en[0:NS, :], in_=en[0:NS, :], func=AF.Exp), [bf("en")], [bf("en")])
            S.op("dve", lambda e: e.tensor_tensor(out=en[0:NS, :], in0=en[0:NS, :], in1=misc_t[0:NS, 896:896 + HQ], op=ALU.mult), [bf("en"), bf("misc")], [bf("en")])
            S.op("act", lambda e: e.activation(out=spn[0:NS, :], in_=en[0:NS, :], func=AF.Ln, bias=1.0), [bf("en")], [bf("spn")])
            ct, ctb = ring()
            S.op("pe", lambda e: e.matmul(ct[0:P_, 0:HQ], lhsT=misc_t[0:P_, 256:256 + P_], rhs=Tt[0:P_, :], start=True, stop=False), [bf("misc"), bf("Tt")], [ctb])
            S.op("pe", lambda e: e.matmul(ct[0:P_, 0:HQ], lhsT=ones[0:NS, 0:P_], rhs=spn[0:NS, :], start=False, stop=True), [bf("ones"), bf("spn")], [ctb])
            S.op("dve", lambda e: e.tensor_tensor(out=Aall[0:P_, PAGE - 1, :], in0=Aall[0:P_, PAGE - 1, :], in1=ct[0:P_, 0:HQ], op=ALU.add), [BA, ctb], [BA])
            sh = 1
            while sh < PAGE:
                S.op("dve", lambda e, sh=sh: e.tensor_tensor(out=Aall[0:P_, 0:PAGE - sh, :], in0=Aall[0:P_, 0:PAGE - sh, :], in1=Aall[0:P_, sh:PAGE, :], op=ALU.add),
                     [BA], [BA])
                sh *= 2
            S.op("act", lambda e: e.activation(out=Aall[0:P_], in_=Aall[0:P_], func=AF.Exp, scale=-1.0), [BA], [BA])
            S.op("dve", lambda e: e.tensor_tensor(out=Wall[0:P_], in0=Eall[0:P_], in1=Aall[0:P_], op=ALU.mult), [BE, BA] + Bxb, [BW] + Bxb)
            an, anb = ring()
            S.op("pe", lambda e: e.matmul(an[0:NS, 0:HQ], lhsT=misc_t[0:NS, 128:128 + NS], rhs=spn[0:NS, :], start=True, stop=True), [bf("misc"), bf("spn")], [anb])
            S.op("act", lambda e: e.activation(out=un[0:NS, :], in_=an[0:NS, 0:HQ], func=AF.Exp, scale=-1.0), [anb], [bf("un")])
            S.op("dve", lambda e: e.tensor_tensor(out=wn[0:NS, :], in0=en[0:NS, :], in1=un[0:NS, :], op=ALU.mult), [bf("en"), bf("un")], [bf("wn")])
            for r in range(PAGE + 1):
                o_, ob_ = ring()
                if r < PAGE:
                    g = r % 2
                    S.dma("pool", None, None, reads=[bf("pti")], writes=[Bg[g]],
                          fn=lambda e, g=g, r=r: e.indirect_dma_start(out=gat[g][0:P_, :], out_offset=None, in_=cache_v[:, :],
                                                                      in_offset=bass.IndirectOffsetOnAxis(ap=idxi[0:P_, r:r + 1], axis=0)))
                    if r % 2 == 0:
                        S.op("act", lambda e, g=g: e.copy(out=Vb[0:P_, :], in_=gat[g][0:P_, :]), [Bg[g]], [bf("Vb")] + (Bh if r == 0 else []))
                    else:
                        S.op("dve", lambda e, g=g: e.tensor_copy(out=Vb[0:P_, :], in_=gat[g][0:P_, :]), [Bg[g]], [bf("Vb")])
                    for h in range(SH):
                        S.op("pe", lambda e, h=h, r=r: e.matmul(o_[:, h * NS:(h + 1) * NS], lhsT=Vb[0:P_, h * 128:(h + 1) * 128], rhs=Wall[0:P_, r, h * NS:(h + 1) * NS],
                                                                start=True, stop=True), [bf("Vb"), BW], [ob_])
                else:
                    for h in range(SH):
                        S.op("pe", lambda e, h=h: e.matmul(o_[:, h * NS:(h + 1) * NS], lhsT=vsn[0:NS, h * 128:(h + 1) * 128], rhs=wn[0:NS, h * NS:(h + 1) * NS],
                                                           start=True, stop=True), [bf("vsn"), bf("wn")], [ob_])
                if r == 0:
                    S.op("dve", lambda e: e.tensor_copy(out=oacc[:, :], in_=o_[:, 0:HQ]), [ob_], [bf("oacc")])
                else:
                    S.op("dve", lambda e: e.tensor_tensor(out=oacc[:, :], in0=oacc[:, :], in1=o_[:, 0:HQ], op=ALU.add), [ob_, bf("oacc")], [bf("oacc")])
            S.op("act", lambda e: e.copy(out=obs[:].rearrange("p h q -> p (h q)"), in_=oacc[:, :]), [bf("oacc")], [bf("obs")])

        @blk.sync
        def _(sync):
            S.dma("sp", dt_t[:].rearrange("p h i -> p (h i)"), dt_d[:, :], writes=[bf("dt")])
            S.dma("sp", dq_t[:].rearrange("p h i -> p (h i)"), dq_d[:, :], writes=[bf("dq")])
            S.dma("sp", dk_t[:], dk_d[:, :], writes=[bf("dk")])
            S.dma("sp", misc_t[:], misc_d[:, :], writes=[bf("misc")])
            S.dma("sp", lnp[:], lnp_d[:, :], writes=[bf("lnp")])
            S.dma("sp", sbb[:], sbb_d.partition_broadcast(128), writes=[bf("sbb")])
            S.dma("sp", pcf[:], pcf_d[:, :], writes=[bf("pcf")])
            S.dma("sp", pci[:], pci_d[:, :], writes=[bf("pci")])
            S.op("dve", lambda e: e.tensor_copy(out=identb[:], in_=misc_t[:, 0:128]), [bf("misc")], [bf("identb")])
            S.op("dve", lambda e: e.memset(ones[:], 1.0), [], [bf("ones")])
            S.op("dve", lambda e: e.tensor_scalar(out=lnpa[:], in0=lnp[:], scalar1=ALPHA, scalar2=None, op0=ALU.mult), [bf("lnp")], [bf("lnp")])

            def load_x(cx, src, N, c0):
                S.dma("sp", cx.xacc[:, :, 0:N], src[:, :, c0:c0 + N], writes=cx.Bx[0:1] if cx is ctxS else cx.Bx)
                S.dma("pool", cx.xb[:, :, 0:N], src[:, :, c0:c0 + N], writes=cx.Bxb[0:1] if cx is ctxS else cx.Bxb)
                for c in range(KC):
                    S.op("act", lambda e, c=c: e.mul(out=cx.xacc[:, c, 0:N], in_=cx.xacc[:, c, 0:N], mul=ALPHA), [cx.Bx[c]], [cx.Bx[c]])

            def layer0(cx, N, cs, nchunk, t, sample):
                retention(cx, N, cs, nchunk, (SEQ if sample else t * TT), sample)
                layer_norm(cx, N, 0, True)
                if stop_after == "ln0":
                    return
                mlp(cx, N, 0)
                layer_norm(cx, N, 1, True)
                if stop_after == "mlp0":
                    return
                kv_proj(cx, N, cs, nchunk, t, sample)

            for h in range(RH):
                S.op("dve", lambda e, h=h: e.memset(Ssb[:, 2 * h:2 * h + 2, :], 0.0), [], [bf("S%d" % h)])
            for t in range(NT):
                load_x(ctxP, xT, TT, t * TT)
                if stop_after == "ret0":
                    retention(ctxP, TT, 128, 4, 0, False)
                    S.dma("sp", yT[0], xacc[:].rearrange("p c n -> p (c n)"), reads=Bx, writes=[bf("o_y")])
                    S.finish(list(B.values()))
                    return
                layer0(ctxP, TT, 128, 4, t, False)
                if stop_after in ("t0", "ln0", "mlp0"):
                    S.dma("sp", yT[0], xacc[:].rearrange("p c n -> p (c n)"), reads=Bx, writes=[bf("o_y")])
                    S.finish(list(B.values()))
                    return
                S.dma("sp", X1s[t * 128:(t + 1) * 128, :], xacc[:].rearrange("p c n -> p (c n)"), reads=Bx, writes=[bf("X1s%d" % t)])
                S.dma("sp", XBs[t * 128:(t + 1) * 128, :], xb[:].rearrange("p c n -> p (c n)"), reads=Bxb, writes=[bf("XBs%d" % t)])
            for h in range(RH):
                S.dma("sp", retp[h].rearrange("(c p) v -> p c v", p=128), Ssb[:, 2 * h:2 * h + 2, :], reads=[bf("S%d" % h)], writes=[bf("o_retp")])
            if stop_after == "l0":
                S.finish(list(B.values()))
                return
            for j in range(NOWN):
                x1b = [bf("X1s%d" % t) for t in range(NT)]
                xbb = [bf("XBs%d" % t) for t in range(NT)]
                S.dma("pool", None, None, reads=x1b + [bf("pci")], writes=Bx,
                      fn=lambda e, j=j: e.indirect_dma_start(out=xacc[:].rearrange("p c n -> p (c n)"), out_offset=None, in_=X1s[:, :],
                                                             in_offset=bass.IndirectOffsetOnAxis(ap=pci[:, j:j + 1], axis=0)))
                S.dma("pool", None, None, reads=xbb + [bf("pci")], writes=Bxb,
                      fn=lambda e, j=j: e.indirect_dma_start(out=xb[:].rearrange("p c n -> p (c n)"), out_offset=None, in_=XBs[:, :],
                                                             in_offset=bass.IndirectOffsetOnAxis(ap=pci[:, j:j + 1], axis=0)))
                sb_attention_prompt(ctxP, j)
                layer1_tail(ctxP, TT, lambda kc: (hbuf[:, kc, 0:TT], [Bh[kc]]))
                S.dma("sp", yT[j], xacc[:].rearrange("p c n -> p (c n)"), reads=Bx, writes=[bf("o_y")])
            if stop_after is None:
                for h in range(RH):
                    S.dma("sp", Ssb[:, 2 * h:2 * h + 2, :], state[h].rearrange("(c p) v -> p c v", p=128), writes=[bf("S%d" % h)])
                load_x(ctxS, xsT, NS, 0)
                layer0(ctxS, NS, NS, 1, 0, True)
                for h in range(RH):
                    S.dma("sp", rets[h].rearrange("(c p) v -> p c v", p=128), Ssb[:, 2 * h:2 * h + 2, :], reads=[bf("S%d" % h)], writes=[bf("o_rets")])
                sb_attention_sample(ctxS)
                layer1_tail(ctxS, NS, lambda kc: (obs[:, kc, :], [bf("obs")]))
                S.dma("sp", ysT[:, :].rearrange("p (c n) -> p c n", c=KC), xsacc[:, :, :], reads=ctxS.Bx[0:1], writes=[bf("o_ys")])
            S.finish(list(B.values()))
        print("instructions:", S.n_inst, "waits:", S.n_wait, "pieces:", wpos[0], "/", len(plan), flush=True)
    return nc


_PROG_CACHE = {}
_LAST_RES = None


def _run(inputs, SEQ, NS, NPG, stop_after=None):
    x_prompt = np.asarray(inputs["x_prompt"], np.float32)
    x_sample = np.asarray(inputs["x_sample"], np.float32)
    state_ret = np.asarray(inputs["state_ret"], np.float32)
    cache_k = np.asarray(inputs["cache_k"], np.float32)
    cache_v = np.asarray(inputs["cache_v"], np.float32)
    page_table = np.asarray(inputs["page_table"], np.int32)
    NPOOL = cache_k.shape[0]
    BATCH = x_prompt.shape[0]
    NT = SEQ // TT
    NOWN = NT // 4
    key = (SEQ, NS, NPG, NPOOL, stop_after)
    if key not in _PROG_CACHE:
        _PROG_CACHE[key] = build_program(SEQ, NS, NPG, NPOOL, stop_after)
    nc = _PROG_CACHE[key]
    Wp = pack_weights(*[np.asarray(inputs[k], np.float32) for k in ("w_ret_in", "w_ret_out", "w_kv", "w_sb_q", "w_sb_o", "w_ff1", "w_ff2")])
    rope, dtab, dqtab, dktab, misc, _, _ = const_tables(SEQ, NS, NPG * PAGE)
    ln_g = np.asarray(inputs["ln_g"], np.float32).reshape(4, KC, 128)
    ln_b = np.asarray(inputs["ln_b"], np.float32).reshape(4, KC, 128)
    lnp = np.concatenate([ln_g.transpose(2, 0, 1).reshape(128, 4 * KC), ln_b.transpose(2, 0, 1).reshape(128, 4 * KC)], axis=1)
    sbb = np.asarray(inputs["sb_bias"], np.float32).reshape(1, SH)
    ck = cache_k.reshape(NPOOL * PAGE, D)
    cv = cache_v.reshape(NPOOL * PAGE, D)
    in_maps = []
    for c in range(NCORE):
        b, s = (c // 4) % BATCH, c % 4
        xT = np.ascontiguousarray(x_prompt[b].reshape(SEQ, KC, 128).transpose(2, 1, 0))
        xsT = np.ascontiguousarray(x_sample[c].reshape(NS, KC, 128).transpose(2, 1, 0))
        own = [s * NOWN + j for j in range(NOWN)]
        pcf = np.broadcast_to(np.array([o * TT for o in own], np.float32)[None], (128, NOWN)).copy()
        pci = (np.array(own, np.int32)[None, :] * 128 + np.arange(128, dtype=np.int32)[:, None]).astype(np.int32)
        in_maps.append(dict(xT=xT, xsT=xsT, state=np.ascontiguousarray(state_ret[0, c]), cache_k=ck, cache_v=cv,
                            pt=np.ascontiguousarray(page_table[c].reshape(NPG, 1)), W=Wp, rope=rope, dtab=dtab, dqtab=dqtab,
                            dktab=dktab, misc=misc, lnp=lnp, sbb=sbb, pcf=pcf, pci=pci))
    import os as _os
    _n = int(_os.environ.get("KNCORE", NCORE))
    if _os.environ.get("KTRACE"):
        res = run_bass_kernel_spmd(nc, in_maps[:_n], core_ids=list(range(_n)), trace=True)
        print("KTRACE exec_time_ns", res.exec_time_ns, flush=True)
        global _LAST_RES
        _LAST_RES = res
    else:
        res = run_bass_kernel_spmd(nc, in_maps[:_n], core_ids=list(range(_n)))
    R = list(res.results) + [res.results[0]] * (NCORE - _n)
    y_prompt = np.zeros((BATCH, SEQ, D), np.float32)
    for c in range(NCORE):
        b, s = (c // 4) % BATCH, c % 4
        for j in range(NOWN):
            t = s * NOWN + j
            y_prompt[b, t * TT:(t + 1) * TT] = R[c]["yT"][j].reshape(128, KC, TT).transpose(2, 1, 0).reshape(TT, D)
    y_sample = np.stack([R[c]["ysT"].reshape(128, KC, NS).transpose(2, 1, 0).reshape(NS, D) for c in range(NCORE)])
    ret_prompt = np.stack([R[4 * b]["retp"] for b in range(BATCH)])[None]
    k_prompt = np.stack([R[4 * b]["kTo"].transpose(2, 0, 1) for b in range(BATCH)])
    v_prompt = np.stack([R[4 * b]["vo"].reshape(SEQ, SH, 128) for b in range(BATCH)])
    ret_sample = np.stack([R[c]["rets"] for c in range(NCORE)])[None]
    k_sample = np.stack([R[c]["ksTo"].reshape(128, SH, NS).transpose(2, 1, 0) for c in range(NCORE)])
    v_sample = np.stack([R[c]["vso"].reshape(NS, SH, 128) for c in range(NCORE)])
    return (y_prompt, y_sample, np.ascontiguousarray(ret_prompt), np.ascontiguousarray(k_prompt), v_prompt,
            np.ascontiguousarray(ret_sample), np.ascontiguousarray(k_sample), v_sample)


def kernel(**inputs):
    SEQ = inputs["x_prompt"].shape[1]
    NS = inputs["x_sample"].shape[1]
    NPG = inputs["page_table"].shape[1]
    return _run(inputs, SEQ, NS, NPG)
```

```python
import contextlib
import math
import numpy as np
import concourse.bass as bass
import concourse.mybir as mybir
from concourse.bass_utils import run_bass_kernel_spmd

F32 = mybir.dt.float32
BF16 = mybir.dt.bfloat16
I32 = mybir.dt.int32
AF = mybir.ActivationFunctionType
ALU = mybir.AluOpType

D = 2048
KC = 16
RH = 8
SH = 16
DFF = 8192
TT = 512
LN_EPS = 1e-5
GN_EPS = 1e-6
DEPTH = 2
ALPHA = (2.0 * DEPTH) ** 0.25
ROPE_BASE = 10000.0
PAGE = 128
NCORE = 8
PIECE = 4096


class Buf:
    __slots__ = ("name", "w", "r")

    def __init__(self, name):
        self.name = name
        self.w = {}
        self.r = {}


class Sched:
    NDMA = 8

    def __init__(self, nc, stack):
        self.nc = nc
        self.eng = {"pe": nc.tensor, "act": nc.scalar, "dve": nc.vector, "pool": nc.gpsimd, "sp": nc.sync}
        self.sem = {}
        self.cnt = {}
        self.waited = {k: {} for k in self.eng}
        for k in ("pe", "act", "dve", "pool"):
            self.sem[k] = stack.enter_context(nc.semaphore("sem_" + k))
            self.cnt[k] = 0
        self.dsem = {}
        self.dn = {}
        for q in ("sp", "act", "pool"):
            self.dsem[q] = [stack.enter_context(nc.semaphore("dsem_%s_%d" % (q, i))) for i in range(self.NDMA)]
            self.dn[q] = 0
        self.n_inst = 0
        self.n_wait = 0

    def _wait(self, e, need):
        wd = self.waited[e]
        pes = self.sem["pe"]
        for sid, (s, v) in need.items():
            if wd.get(sid, 0) >= v:
                continue
            if e == "pe" and s is pes:
                continue
            self.eng[e].wait_ge(s, v)
            self.n_wait += 1
            wd[sid] = v

    @staticmethod
    def _merge(need, d):
        for sid, sv in d.items():
            o = need.get(sid)
            if o is None or o[1] < sv[1]:
                need[sid] = sv

    def _deps(self, reads, writes):
        need = {}
        for b in reads:
            self._merge(need, b.w)
        for b in writes:
            self._merge(need, b.w)
            self._merge(need, b.r)
        return need

    @staticmethod
    def _mark(ev, reads, writes):
        sid = id(ev[0])
        for b in reads:
            b.r[sid] = ev
        for b in writes:
            b.w[sid] = ev

    def op(self, e, fn, reads=(), writes=()):
        self._wait(e, self._deps(reads, writes))
        ins = fn(self.eng[e])
        self.cnt[e] += 1
        ev = (self.sem[e], self.cnt[e])
        ins.then_inc(ev[0], 1)
        self._mark(ev, reads, writes)
        self.n_inst += 1

    def dma(self, q, out, in_, reads=(), writes=(), fn=None):
        n = self.dn[q]
        K = self.NDMA
        s = self.dsem[q][n % K]
        prev = 16 * (n // K)
        need = self._deps(reads, writes)
        if prev > 0:
            self._merge(need, {id(s): (s, prev)})
        self._wait(q, need)
        if fn is None:
            ins = self.eng[q].dma_start(out=out, in_=in_)
        else:
            ins = fn(self.eng[q])
        ins.then_inc(s, 16)
        self.dn[q] = n + 1
        self._mark((s, prev + 16), reads, writes)
        self.n_inst += 1

    def finish(self, bufs):
        need = {}
        for b in bufs:
            self._merge(need, b.w)
            self._merge(need, b.r)
        self._wait("sp", need)


def piece_table():
    t = {}
    n = 0
    for name, cnt in (("qk", 2 * RH), ("g", 2 * RH), ("v", 2 * RH), ("out", 2 * RH),
                      ("ff1_0", 32), ("ff2_0", 32), ("k", 8), ("v2", 8), ("q", 8), ("o", 8),
                      ("ff1_1", 32), ("ff2_1", 32)):
        t[name] = (n, cnt)
        n += cnt
    return t, n


def tile_fm(Wm, fcs):
    K = Wm.shape[0]
    out = np.empty((len(fcs), 128, K // 128, 128), np.float32)
    for i, fc in enumerate(fcs):
        out[i] = Wm[:, fc * 128:(fc + 1) * 128].reshape(K // 128, 128, 128).transpose(1, 0, 2)
    return out


def pack_weights(w_ret_in, w_ret_out, w_kv, w_sb_q, w_sb_o, w_ff1, w_ff2):
    tab, n = piece_table()
    Wp = np.empty((n, 128, PIECE), np.float32)
    win = w_ret_in[0]
    HK = 2048
    for h in range(RH):
        Wp[tab["qk"][0] + 2 * h] = tile_fm(win, [(h * 256) // 128, (h * 256) // 128 + 1]).transpose(1, 0, 2, 3).reshape(128, PIECE)
        Wp[tab["qk"][0] + 2 * h + 1] = tile_fm(win, [(HK + h * 256) // 128, (HK + h * 256) // 128 + 1]).transpose(1, 0, 2, 3).reshape(128, PIECE)
        g0 = (2 * HK + 4096 + h * 512) // 128
        for j in range(2):
            Wp[tab["g"][0] + 2 * h + j] = tile_fm(win, [g0 + 2 * j, g0 + 2 * j + 1]).transpose(1, 0, 2, 3).reshape(128, PIECE)
        v0 = 2 * HK + h * 512
        for j in range(2):
            Wp[tab["v"][0] + 2 * h + j] = win[:, v0 + j * 256:v0 + (j + 1) * 256].reshape(KC, 128, 256).transpose(1, 0, 2).reshape(128, PIECE)
        wo = w_ret_out[0][h * 512:(h + 1) * 512]
        for j in range(2):
            Wp[tab["out"][0] + 2 * h + j] = tile_fm(wo, list(range(8 * j, 8 * j + 8))).transpose(1, 0, 2, 3).reshape(128, PIECE)
    for l in range(2):
        w1 = w_ff1[l]
        for i in range(32):
            Wp[tab["ff1_%d" % l][0] + i] = tile_fm(w1, [2 * i, 2 * i + 1]).transpose(1, 0, 2, 3).reshape(128, PIECE)
        w2 = w_ff2[l]
        for g in range(4):
            w2g = w2[g * 2048:(g + 1) * 2048]
            for i in range(8):
                Wp[tab["ff2_%d" % l][0] + g * 8 + i] = tile_fm(w2g, [2 * i, 2 * i + 1]).transpose(1, 0, 2, 3).reshape(128, PIECE)
    for i in range(8):
        Wp[tab["k"][0] + i] = tile_fm(w_kv, [2 * i, 2 * i + 1]).transpose(1, 0, 2, 3).reshape(128, PIECE)
        Wp[tab["v2"][0] + i] = w_kv[:, D + i * 256:D + (i + 1) * 256].reshape(KC, 128, 256).transpose(1, 0, 2).reshape(128, PIECE)
        Wp[tab["q"][0] + i] = tile_fm(w_sb_q[0], [2 * i, 2 * i + 1]).transpose(1, 0, 2, 3).reshape(128, PIECE)
        Wp[tab["o"][0] + i] = tile_fm(w_sb_o[0], [2 * i, 2 * i + 1]).transpose(1, 0, 2, 3).reshape(128, PIECE)
    return Wp


def const_tables(SEQ, NS, past_len):
    half = 128
    inv = ROPE_BASE ** (-np.arange(half, dtype=np.float32) / half)
    pos = np.concatenate([np.arange(SEQ), past_len + np.arange(NS)]).astype(np.float32)
    ang = inv[:, None].astype(np.float32) * pos[None, :].astype(np.float32)
    rope = np.stack([np.cos(ang), np.sin(ang)], axis=1).astype(np.float32)
    log_g = np.log1p(-np.power(2.0, -5.0 - np.arange(RH, dtype=np.float64)))
    idx = np.arange(128, dtype=np.float64)
    diff = idx[None, :] - idx[:, None]
    dt = np.where(diff[None] >= 0, np.exp(log_g[:, None, None] * np.maximum(diff, 0)[None]), 0.0) / 16.0
    dt = dt.transpose(1, 0, 2).astype(np.float32)
    dq = (np.exp(log_g[:, None] * (idx + 1.0)[None, :]) / 16.0).astype(np.float32)
    dq = np.broadcast_to(dq[None], (128, RH, 128)).copy()
    dk = np.zeros((128, 2 * RH), np.float32)
    dk[:, :RH] = np.exp(log_g[None, :] * (127.0 - idx)[:, None])
    dk[:NS, RH:] = np.exp(log_g[None, :] * (NS - 1.0 - idx[:NS])[:, None])
    ident = np.eye(128, dtype=np.float32)
    tri = (idx[:, None] >= idx[None, :]).astype(np.float32)
    ustr = (idx[:, None] > idx[None, :]).astype(np.float32)
    dmat = (np.arange(512, dtype=np.float32)[None, :] - np.arange(128, dtype=np.float32)[:, None])
    m4 = np.zeros((128, SH, NS), np.float32)
    for j in range(NS):
        for i in range(NS):
            if j < i:
                m4[j, :, i] = 1.0
    iota = np.broadcast_to(np.arange(128, dtype=np.float32)[None, :], (128, 128))
    misc = np.concatenate([ident, tri, ustr, dmat, m4.reshape(128, SH * NS), iota], axis=1).astype(np.float32)
    decay_c = [float(np.exp(log_g[h] * 128.0)) for h in range(RH)]
    decay_cs = [float(np.exp(log_g[h] * NS)) for h in range(RH)]
    return rope, dt.reshape(128, RH * 128), dq.reshape(128, RH * 128), dk, misc, decay_c, decay_cs


class Ctx:
    def __init__(self, xacc, xb, Bx, Bxb):
        self.xacc, self.xb, self.Bx, self.Bxb = xacc, xb, Bx, Bxb


def build_program(SEQ, NS, NPG, NPOOL, stop_after=None):
    NT = SEQ // TT
    assert NT % 4 == 0
    NOWN = NT // 4
    NKB = SEQ // 128
    HQ = SH * NS
    tab, NPIECE = piece_table()
    _, _, _, _, _, decay_c, decay_cs = const_tables(SEQ, NS, NPG * PAGE)
    scale = 128 ** -0.5
    MISCW = 128 * 3 + 512 + HQ + 128

    nc = bass.Bass("TRN2", target_bir_lowering=False)

    def din(name, shape, dt=F32):
        return nc.dram_tensor(name, list(shape), dt, kind="ExternalInput").ap()

    def dout(name, shape, dt=F32):
        return nc.dram_tensor(name, list(shape), dt, kind="ExternalOutput").ap()

    xT = din("xT", [128, KC, SEQ])
    xsT = din("xsT", [128, KC, NS])
    state = din("state", [RH, 256, 512])
    cache_k = din("cache_k", [NPOOL * PAGE, D])
    cache_v = din("cache_v", [NPOOL * PAGE, D])
    pt = din("pt", [NPG, 1], I32)
    Wd = din("W", [NPIECE, 128, PIECE])
    rope_d = din("rope", [128, 2, SEQ + NS])
    dt_d = din("dtab", [128, RH * 128])
    dq_d = din("dqtab", [128, RH * 128])
    dk_d = din("dktab", [128, 2 * RH])
    misc_d = din("misc", [128, MISCW])
    lnp_d = din("lnp", [128, 2 * 4 * KC])
    sbb_d = din("sbb", [1, SH])
    pcf_d = din("pcf", [128, NOWN])
    pci_d = din("pci", [128, NOWN], I32)

    yT = dout("yT", [NOWN, 128, KC * TT])
    ysT = dout("ysT", [128, KC * NS])
    retp = dout("retp", [RH, 256, 512])
    kTo = dout("kTo", [SH, 128, SEQ])
    vo = dout("vo", [SEQ, D])
    rets = dout("rets", [RH, 256, 512])
    ksTo = dout("ksTo", [128, HQ])
    vso = dout("vso", [NS, D])

    Ks = nc.dram_tensor("Ks", [SH, 128, SEQ], BF16).ap()
    Vs = nc.dram_tensor("Vs", [SH, 128, NKB * 128], BF16).ap()
    X1s = nc.dram_tensor("X1s", [NT * 128, KC * TT], F32).ap()
    XBs = nc.dram_tensor("XBs", [NT * 128, KC * TT], BF16).ap()
    Wbc = nc.dram_tensor("Wbc", [NPIECE, 128, PIECE], BF16).ap()

    with contextlib.ExitStack() as st:
        def sb(name, shape, dt):
            return st.enter_context(nc.sbuf_tensor("s_" + name, list(shape), dt))

        def psum(name, shape, dt):
            return st.enter_context(nc.psum_tensor("p_" + name, list(shape), dt))

        xacc = sb("xacc", [128, KC, TT], F32)
        xb = sb("xb", [128, KC, TT], BF16)
        NWB = 3
        wbuf = [sb("wb%d" % i, [128, PIECE], BF16) for i in range(NWB)]
        hbuf = sb("hbuf", [128, KC, TT], BF16)
        kvb = sb("kvb", [128, max(2 * NKB * 128, D + HQ)], BF16)
        Ssb = sb("S", [128, RH * 2, 512], F32)
        Sbf = sb("Sbf", [128, 2, 512], BF16)
        rope_t = sb("rope_t", [128, 2, TT], F32)
        rq = sb("rq", [128, 2, TT], BF16)
        qh = sb("qh", [128, 2, TT], BF16)
        rk = sb("rk", [128, 2, TT], BF16)
        vsb = sb("vsb", [128, 4, 512], BF16)
        sg = sb("sg", [128, 4, TT], F32)
        og = sb("og", [128, 4, TT], BF16)
        pool = sb("tpool", [128, 8, TT], F32)
        tmp = [pool[:, i, :] for i in range(8)]
        scb = sb("scb", [128, 128], BF16)
        kd = sb("kd", [128, 256], BF16)
        onb = sb("onb", [128, 512], BF16)
        stats = sb("stats", [128, 6], F32)
        mv = sb("mv", [128, 2], F32)
        rstd = sb("rstd", [128, 1], F32)
        dt_t = sb("dt_t", [128, RH, 128], F32)
        dq_t = sb("dq_t", [128, RH, 128], F32)
        dk_t = sb("dk_t", [128, 2 * RH], F32)
        misc_t = sb("misc_t", [128, MISCW], F32)
        identb = sb("identb", [128, 128], BF16)
        ones = sb("ones", [128, 128], F32)
        lnp = sb("lnp", [128, 2 * 4 * KC], F32)
        lnpa = sb("lnpa", [128, 2 * 4 * KC], F32)
        sbb = sb("sbb", [128, SH], F32)
        pcf = sb("pcf", [128, NOWN], F32)
        pci = sb("pci", [128, NOWN], I32)
        qT2 = sb("qT2", [128, 2, TT], BF16)
        w_t = [sb("w_t%d" % i, [128, TT], BF16) for i in range(2)]
        kb16 = [sb("kb16_%d" % i, [128, TT], BF16) for i in range(2)]
        vf = [sb("vf%d" % i, [128, 256], F32) for i in range(2)]
        vb16 = [sb("vb16_%d" % i, [128, 256], BF16) for i in range(2)]
        xsacc = sb("xsacc", [128, KC, NS], F32)
        xsb = sb("xsb", [128, KC, NS], BF16)
        qTs = sb("qTs", [128, SH, NS], BF16)
        pti = sb("pti", [128, 1], I32)
        ptf = sb("ptf", [128, 1], F32)
        idxi = sb("idxi", [128, PAGE], I32)
        rflat = rope_t[:].rearrange("p a n -> p (a n)")
        sbb64 = rflat[:, 0:HQ]
        Tt = rflat[:, HQ:2 * HQ]
        en = rflat[:, 2 * HQ:3 * HQ]
        spn = rflat[:, 3 * HQ:4 * HQ]
        un = rflat[:, 4 * HQ:5 * HQ]
        oacc = rflat[:, 5 * HQ:6 * HQ]
        idxf = rflat[:, 6 * HQ:6 * HQ + PAGE]
        wn = sb("wn", [128, HQ], BF16)
        obs = sb("obs", [128, SH, NS], BF16)
        ksn = kvb[:, D:D + HQ].rearrange("p (h q) -> p h q", h=SH)
        vsn = kvb[:, 0:D]

        NRING = 5
        pring = [psum("pr%d" % i, [128, 512], F32) for i in range(NRING)]
        accA = psum("accA", [128, 512], F32)
        accB = psum("accB", [128, 512], F32)
        tpb = psum("tpb", [128, 1024], BF16)

        blk = st.enter_context(nc.Block())
        S = Sched(nc, st)
        B = {}

        def bf(name):
            if name not in B:
                B[name] = Buf(name)
            return B[name]

        Bx = [bf("xacc%d" % c) for c in range(KC)]
        Bxb = [bf("xb%d" % c) for c in range(KC)]
        Bh = [bf("hbuf%d" % c) for c in range(KC)]
        Bw = [bf("wb%d" % i) for i in range(NWB)]
        Bring = [bf("pr%d" % i) for i in range(NRING)]
        Bt = [bf("tmp%d" % i) for i in range(8)]
        ring_pos = [0]
        ctxP = Ctx(xacc, xb, Bx, Bxb)
        ctxS = Ctx(xsacc, xsb, [bf("xsacc")] * KC, [bf("xsb")] * KC)

        def ring():
            i = ring_pos[0] % NRING
            ring_pos[0] += 1
            return pring[i], Bring[i]

        plan = []
        wpos = [0]
        wissued = [0]

        Bwc = {}

        def w_issue_upto(k):
            while wissued[0] < min(len(plan), k):
                i = wissued[0]
                pid = plan[i]
                if pid not in Bwc:
                    S.dma("pool", wbuf[i % NWB][:], Wd[pid], writes=[Bw[i % NWB]])
                    Bwc[pid] = Buf("wc%d" % pid)
                    S.dma("sp", Wbc[pid], wbuf[i % NWB][:], reads=[Bw[i % NWB]], writes=[Bwc[pid]])
                else:
                    S.dma("sp", wbuf[i % NWB][:], Wbc[pid], reads=[Bwc[pid]], writes=[Bw[i % NWB]])
                wissued[0] += 1

        def wget(pid):
            i = wpos[0]
            assert plan[i] == pid, (i, plan[i], pid)
            w_issue_upto(i + NWB)
            wpos[0] += 1
            return wbuf[i % NWB], Bw[i % NWB]

        def wprefetch():
            w_issue_upto(wpos[0] + NWB - 1)

        def plan_mlp(l):
            p = []
            for g in range(4):
                p += [tab["ff1_%d" % l][0] + g * 8 + i for i in range(8)]
                p += [tab["ff2_%d" % l][0] + g * 8 + i for i in range(8)]
            return p

        def plan_layer0():
            p = []
            for h in range(RH):
                p += [tab["qk"][0] + 2 * h, tab["qk"][0] + 2 * h + 1]
                p += [tab["v"][0] + 2 * h, tab["v"][0] + 2 * h + 1]
                p += [tab["g"][0] + 2 * h, tab["g"][0] + 2 * h + 1]
                p += [tab["out"][0] + 2 * h, tab["out"][0] + 2 * h + 1]
            p += plan_mlp(0)
            p += [tab["k"][0] + i for i in range(8)]
            p += [tab["v2"][0] + i for i in range(8)]
            return p

        def plan_layer1():
            p = [tab["q"][0] + i for i in range(8)]
            p += [tab["o"][0] + i for i in range(8)]
            p += plan_mlp(1)
            return p

        for t in range(NT):
            plan.extend(plan_layer0())
        for j in range(NOWN):
            plan.extend(plan_layer1())
        if stop_after is None:
            plan.extend(plan_layer0())
            plan.extend(plan_layer1())

        def lin_fm(pids, cpp, KCn, rhs_fn, N, consume):
            fc = 0
            for pid in pids:
                wap, wb_ = wget(pid)
                wv = wap[:].rearrange("p (j k m) -> p j k m", j=cpp, k=KCn)
                for j in range(cpp):
                    pb, pbb = ring()
                    for kc in range(KCn):
                        rap, rbufs = rhs_fn(kc)
                        S.op("pe", lambda e, pb=pb, l=wv[:, j, kc, :], r=rap, kc=kc: e.matmul(
                            pb[:, 0:N], lhsT=l, rhs=r, start=(kc == 0), stop=(kc == KCn - 1)),
                            [wb_] + rbufs, [pbb])
                    consume(fc, pb, pbb)
                    fc += 1

        def xb_rhs(cx, N):
            return lambda kc: (cx.xb[:, kc, 0:N], [cx.Bxb[kc]])

        def add_into_xacc(cx, N):
            def consume(fc, pb, pbb):
                S.op("dve", lambda e: e.tensor_tensor(out=cx.xacc[:, fc, 0:N], in0=pb[:, 0:N], in1=cx.xacc[:, fc, 0:N], op=ALU.add),
                     [pbb, cx.Bx[fc]], [cx.Bx[fc]])
            return consume

        def layer_norm(cx, N, li, oscale_alpha):
            xa, xbt = cx.xacc, cx.xb
            s1, b1 = ring()
            s2, b2 = ring()
            mean_t, rstd_t = tmp[3], tmp[4]
            for c in range(KC):
                S.op("pe", lambda e, c=c: e.matmul(s1[:, 0:N], lhsT=ones[:], rhs=xa[:, c, 0:N], start=(c == 0), stop=(c == KC - 1)),
                     [bf("ones"), cx.Bx[c]], [b1])
            for c in range(KC):
                sq = tmp[c % 3]
                S.op("act", lambda e, c=c, sq=sq: e.activation(out=sq[:, 0:N], in_=xa[:, c, 0:N], func=AF.Square),
                     [cx.Bx[c]], [Bt[c % 3]])
                S.op("pe", lambda e, c=c, sq=sq: e.matmul(s2[:, 0:N], lhsT=ones[:], rhs=sq[:, 0:N], start=(c == 0), stop=(c == KC - 1)),
                     [bf("ones"), Bt[c % 3]], [b2])
            S.op("dve", lambda e: e.tensor_scalar(out=mean_t[:, 0:N], in0=s1[:, 0:N], scalar1=1.0 / D, scalar2=None, op0=ALU.mult),
                 [b1], [Bt[3]])
            S.op("dve", lambda e: e.tensor_tensor(out=tmp[0][:, 0:N], in0=mean_t[:, 0:N], in1=mean_t[:, 0:N], op=ALU.mult),
                 [Bt[3]], [Bt[0]])
            S.op("dve", lambda e: e.scalar_tensor_tensor(out=rstd_t[:, 0:N], in0=s2[:, 0:N], scalar=1.0 / D, in1=tmp[0][:, 0:N],
                                                         op0=ALU.mult, op1=ALU.subtract),
                 [b2, Bt[0]], [Bt[4]])
            S.op("act", lambda e: e.activation(out=rstd_t[:, 0:N], in_=rstd_t[:, 0:N], func=AF.Ln, bias=LN_EPS), [Bt[4]], [Bt[4]])
            S.op("act", lambda e: e.activation(out=rstd_t[:, 0:N], in_=rstd_t[:, 0:N], func=AF.Exp, scale=-0.5), [Bt[4]], [Bt[4]])
            gp = lnpa if oscale_alpha else lnp
            for c in range(KC):
                tm = tmp[c % 3]
                tb_ = Bt[c % 3]
                S.op("dve", lambda e, c=c, tm=tm: e.tensor_tensor(out=tm[:, 0:N], in0=xa[:, c, 0:N], in1=mean_t[:, 0:N], op=ALU.subtract),
                     [cx.Bx[c], Bt[3]], [tb_])
                S.op("dve", lambda e, c=c, tm=tm: e.tensor_tensor(out=tm[:, 0:N], in0=tm[:, 0:N], in1=rstd_t[:, 0:N], op=ALU.mult),
                     [tb_, Bt[4]], [tb_])
                gcol = li * KC + c
                bcol = 4 * KC + li * KC + c
                S.op("act", lambda e, c=c, tm=tm, gcol=gcol, bcol=bcol: e.activation(
                    out=xbt[:, c, 0:N], in_=tm[:, 0:N], func=AF.Identity, scale=lnp[:, gcol:gcol + 1], bias=lnp[:, bcol:bcol + 1]),
                    [tb_, bf("lnp")], [cx.Bxb[c]])
                S.op("act", lambda e, c=c, tm=tm, gcol=gcol, bcol=bcol: e.activation(
                    out=xa[:, c, 0:N], in_=tm[:, 0:N], func=AF.Identity, scale=gp[:, gcol:gcol + 1], bias=gp[:, bcol:bcol + 1]),
                    [tb_, bf("lnp")], [cx.Bx[c]])

        def mlp(cx, N, l):
            for g in range(4):
                def cons1(fc, pb, pbb):
                    tm = tmp[5 + fc % 3]
                    tb_ = Bt[5 + fc % 3]
                    S.op("act", lambda e: e.activation(out=tm[:, 0:N], in_=pb[:, 0:N], func=AF.Relu), [pbb], [tb_])
                    S.op("dve", lambda e: e.tensor_tensor(out=hbuf[:, fc, 0:N], in0=tm[:, 0:N], in1=tm[:, 0:N], op=ALU.mult),
                         [tb_], [Bh[fc]])
                lin_fm([tab["ff1_%d" % l][0] + g * 8 + i for i in range(8)], 2, KC, xb_rhs(cx, N), N, cons1)
                lin_fm([tab["ff2_%d" % l][0] + g * 8 + i for i in range(8)], 2, KC,
                       lambda kc: (hbuf[:, kc, 0:N], [Bh[kc]]), N, add_into_xacc(cx, N))

        def retention(cx, N, cs, nchunk, rope_col0, sample):
            xbt = cx.xb
            S.dma("sp", rope_t[:, :, 0:N], rope_d[:, :, rope_col0:rope_col0 + N], writes=[bf("rope")])
            dkoff = RH if sample else 0
            for h in range(RH):
                dcy = decay_cs[h] if sample else decay_c[h]
                S.op("act", lambda e, h=h: e.copy(out=Sbf[:, :, :], in_=Ssb[:, 2 * h:2 * h + 2, :]), [bf("S%d" % h)], [bf("Sbf")])
                for which, dst, dbuf in ((0, rq, "rq"), (1, rk, "rk")):
                    held = []

                    def cons(fc, pb, pbb, held=held):
                        held.append((pb, pbb))
                    lin_fm([tab["qk"][0] + 2 * h + which], 2, KC, xb_rhs(cx, N), N, cons)
                    (p1, b1), (p2, b2) = held
                    cosv = rope_t[:, 0, 0:N]
                    sinv = rope_t[:, 1, 0:N]
                    S.op("dve", lambda e: e.tensor_tensor(out=tmp[0][:, 0:N], in0=p1[:, 0:N], in1=cosv, op=ALU.mult), [b1, bf("rope")], [Bt[0]])
                    S.op("dve", lambda e: e.tensor_tensor(out=tmp[1][:, 0:N], in0=p2[:, 0:N], in1=sinv, op=ALU.mult), [b2, bf("rope")], [Bt[1]])
                    S.op("dve", lambda e: e.tensor_tensor(out=tmp[2][:, 0:N], in0=p1[:, 0:N], in1=sinv, op=ALU.mult), [b1, bf("rope")], [Bt[2]])
                    S.op("dve", lambda e: e.tensor_tensor(out=tmp[3][:, 0:N], in0=p2[:, 0:N], in1=cosv, op=ALU.mult), [b2, bf("rope")], [Bt[3]])
                    S.op("dve", lambda e, dst=dst: e.tensor_tensor(out=dst[:, 0, 0:N], in0=tmp[0][:, 0:N], in1=tmp[1][:, 0:N], op=ALU.subtract),
                         [Bt[0], Bt[1]], [bf(dbuf)])
                    S.op("dve", lambda e, dst=dst: e.tensor_tensor(out=dst[:, 1, 0:N], in0=tmp[2][:, 0:N], in1=tmp[3][:, 0:N], op=ALU.add),
                         [Bt[2], Bt[3]], [bf(dbuf)])
                for half in range(2):
                    S.op("dve", lambda e, half=half: e.tensor_tensor(
                        out=qh[:, half, 0:N].rearrange("p (c i) -> p c i", i=cs),
                        in0=rq[:, half, 0:N].rearrange("p (c i) -> p c i", i=cs),
                        in1=dq_t[:, h, 0:cs].unsqueeze(1).to_broadcast([128, nchunk, cs]), op=ALU.mult),
                        [bf("rq"), bf("dq")], [bf("qh")])
                for j in range(2):
                    wap, wb_ = wget(tab["v"][0] + 2 * h + j)
                    wv = wap[:].rearrange("p (k n) -> p k n", k=KC)
                    for c in range(nchunk):
                        pb, pbb = ring()
                        for kc in range(KC):
                            S.op("pe", lambda e, pb=pb, kc=kc, c=c: e.matmul(pb[0:cs, 0:256], lhsT=xbt[:, kc, c * 128:c * 128 + cs], rhs=wv[:, kc, :],
                                                                          start=(kc == 0), stop=(kc == KC - 1)), [wb_, cx.Bxb[kc]], [pbb])
                        S.op("act", lambda e, pb=pb, c=c, j=j: e.copy(out=vsb[0:cs, c, j * 256:(j + 1) * 256], in_=pb[0:cs, 0:256]), [pbb], [bf("vsb")])

                def consg(fc, pb, pbb):
                    S.op("act", lambda e: e.activation(out=sg[:, fc, 0:N], in_=pb[:, 0:N], func=AF.Silu), [pbb], [bf("sg")])
                lin_fm([tab["g"][0] + 2 * h, tab["g"][0] + 2 * h + 1], 2, KC, xb_rhs(cx, N), N, consg)
                wprefetch()
                for c in range(nchunk):
                    cr = slice(c * 128, c * 128 + cs)
                    pb, pbb = ring()
                    for half in range(2):
                        S.op("pe", lambda e, half=half: e.matmul(pb[0:cs, 0:cs], lhsT=rk[:, half, cr], rhs=rq[:, half, cr], start=(half == 0), stop=(half == 1)),
                             [bf("rk"), bf("rq")], [pbb])
                    S.op("dve", lambda e: e.tensor_tensor(out=scb[0:cs, 0:cs], in0=pb[0:cs, 0:cs], in1=dt_t[0:cs, h, 0:cs], op=ALU.mult),
                         [pbb, bf("dt")], [bf("scb")])
                    S.op("pe", lambda e: e.matmul(accA[0:cs, :], lhsT=scb[0:cs, 0:cs], rhs=vsb[0:cs, c, :], start=True, stop=False),
                         [bf("scb"), bf("vsb")], [bf("accA")])
                    for half in range(2):
                        S.op("pe", lambda e, half=half: e.matmul(accA[0:cs, :], lhsT=qh[:, half, cr], rhs=Sbf[:, half, :], start=False, stop=(half == 1)),
                             [bf("qh"), bf("Sbf")], [bf("accA")])
                    for half in range(2):
                        S.op("pe", lambda e, half=half: e.transpose(tpb[0:cs, half * 128:(half + 1) * 128], rk[:, half, cr], identb[:, :]),
                             [bf("rk"), bf("identb")], [bf("tpbA")])
                    S.op("act", lambda e: e.activation(out=kd[0:cs, :], in_=tpb[0:cs, 0:256], func=AF.Identity, scale=dk_t[0:cs, dkoff + h:dkoff + h + 1], bias=0.0),
                         [bf("tpbA"), bf("dk")], [bf("kd")])
                    for half in range(2):
                        pd, pdb = ring()
                        S.op("pe", lambda e, half=half, pd=pd: e.matmul(pd[:, :], lhsT=kd[0:cs, half * 128:(half + 1) * 128], rhs=vsb[0:cs, c, :], start=True, stop=True),
                             [bf("kd"), bf("vsb")], [pdb])
                        S.op("dve", lambda e, half=half, pd=pd: e.scalar_tensor_tensor(out=Ssb[:, 2 * h + half, :], in0=Ssb[:, 2 * h + half, :], scalar=dcy, in1=pd[:, :],
                                                                                      op0=ALU.mult, op1=ALU.add), [pdb, bf("S%d" % h)], [bf("S%d" % h)])
                    S.op("dve", lambda e: e.bn_stats(out=stats[0:cs, :], in_=accA[0:cs, :]), [bf("accA")], [bf("stats")])
                    S.op("dve", lambda e: e.bn_aggr(out=mv[0:cs, :], in_=stats[0:cs, :]), [bf("stats")], [bf("mv")])
                    S.op("act", lambda e: e.activation(out=rstd[0:cs, :], in_=mv[0:cs, 1:2], func=AF.Ln, bias=GN_EPS), [bf("mv")], [bf("rstd")])
                    S.op("act", lambda e: e.activation(out=rstd[0:cs, :], in_=rstd[0:cs, :], func=AF.Exp, scale=-0.5), [bf("rstd")], [bf("rstd")])
                    S.op("dve", lambda e: e.tensor_scalar(out=onb[0:cs, :], in0=accA[0:cs, :], scalar1=mv[0:cs, 0:1], scalar2=rstd[0:cs, 0:1],
                                                          op0=ALU.subtract, op1=ALU.mult), [bf("accA"), bf("mv"), bf("rstd")], [bf("onb")])
                    if c + 1 < nchunk:
                        S.op("act", lambda e: e.copy(out=Sbf[:, :, :], in_=Ssb[:, 2 * h:2 * h + 2, :]), [bf("S%d" % h)], [bf("Sbf")])
                    for vc in range(4):
                        S.op("pe", lambda e, vc=vc: e.transpose(tpb[:, 512 + vc * 128:512 + vc * 128 + cs], onb[0:cs, vc * 128:(vc + 1) * 128], identb[0:cs, 0:cs]),
                             [bf("onb"), bf("identb")], [bf("tpbB")])
                    S.op("dve", lambda e: e.tensor_tensor(out=og[:, :, cr], in0=tpb[:, 512:1024].rearrange("p (v i) -> p v i", v=4)[:, :, 0:cs],
                                                          in1=sg[:, :, cr], op=ALU.mult), [bf("tpbB"), bf("sg")], [bf("og")])
                for j in range(2):
                    wap, wb_ = wget(tab["out"][0] + 2 * h + j)
                    wv = wap[:].rearrange("p (f k m) -> p f k m", f=8, k=4)
                    for fl in range(8):
                        fc = 8 * j + fl
                        pb, pbb = ring()
                        for kc in range(4):
                            S.op("pe", lambda e, pb=pb, fl=fl, kc=kc: e.matmul(pb[:, 0:N], lhsT=wv[:, fl, kc, :], rhs=og[:, kc, 0:N], start=(kc == 0), stop=(kc == 3)),
                                 [wb_, bf("og")], [pbb])
                        add_into_xacc(cx, N)(fc, pb, pbb)

        def kv_proj(cx, N, cs, nchunk, t, sample):
            xbt = cx.xb

            def consk(fc, pb, pbb):
                i = fc % 2
                S.op("act", lambda e: e.copy(out=tmp[i][:, 0:N], in_=pb[:, 0:N]), [pbb], [Bt[i]])
                if sample:
                    S.op("dve", lambda e: e.tensor_copy(out=ksn[:, fc, :], in_=tmp[i][:, 0:N]), [Bt[i]], [bf("ksn")])
                    S.dma("sp", ksTo[:, fc * NS:(fc + 1) * NS], tmp[i][:, 0:N], reads=[Bt[i]], writes=[bf("o_ksT")])
                else:
                    S.op("dve", lambda e: e.tensor_copy(out=kb16[i][:, 0:N], in_=tmp[i][:, 0:N]), [Bt[i]], [bf("kb16_%d" % i)])
                    S.dma("sp", kTo[fc, :, t * TT:t * TT + N], tmp[i][:, 0:N], reads=[Bt[i]], writes=[bf("o_kT")])
                    S.dma("sp", Ks[fc, :, t * TT:t * TT + N], kb16[i][:, 0:N], reads=[bf("kb16_%d" % i)], writes=[bf("Ks%d" % t)])
            lin_fm([tab["k"][0] + i for i in range(8)], 2, KC, xb_rhs(cx, N), N, consk)
            cnt = 0
            for j in range(8):
                wap, wb_ = wget(tab["v2"][0] + j)
                wv = wap[:].rearrange("p (k n) -> p k n", k=KC)
                for c in range(nchunk):
                    pb, pbb = ring()
                    for kc in range(KC):
                        S.op("pe", lambda e, pb=pb, kc=kc, c=c: e.matmul(pb[0:cs, 0:256], lhsT=xbt[:, kc, c * 128:c * 128 + cs], rhs=wv[:, kc, :],
                                                                      start=(kc == 0), stop=(kc == KC - 1)), [wb_, cx.Bxb[kc]], [pbb])
                    i = cnt % 2
                    cnt += 1
                    S.op("act", lambda e, pb=pb, i=i: e.copy(out=vf[i][0:cs, :], in_=pb[0:cs, 0:256]), [pbb], [bf("vf%d" % i)])
                    if sample:
                        S.op("dve", lambda e, i=i, j=j: e.tensor_copy(out=vsn[0:cs, j * 256:(j + 1) * 256], in_=vf[i][0:cs, :]), [bf("vf%d" % i)], [bf("vsn")])
                        S.dma("sp", vso[0:cs, j * 256:(j + 1) * 256], vf[i][0:cs, :], reads=[bf("vf%d" % i)], writes=[bf("o_vs")])
                    else:
                        S.op("dve", lambda e, i=i: e.tensor_copy(out=vb16[i][0:cs, :], in_=vf[i][0:cs, :]), [bf("vf%d" % i)], [bf("vb16_%d" % i)])
                        r0 = t * TT + c * 128
                        S.dma("sp", vo[r0:r0 + cs, j * 256:(j + 1) * 256], vf[i][0:cs, :], reads=[bf("vf%d" % i)], writes=[bf("o_v")])
                        kb = t * 4 + c
                        for hh in range(2):
                            S.dma("sp", Vs[2 * j + hh, 0:cs, kb * 128:(kb + 1) * 128],
                                  vb16[i][0:cs, hh * 128:(hh + 1) * 128], reads=[bf("vb16_%d" % i)], writes=[bf("Vs%d" % t)])

        def sb_attention_prompt(cx, j):
            N = TT
            KTh = kvb[:, 0:SEQ]
            Vh = kvb[:, NKB * 128:2 * NKB * 128]
            ksb = [bf("Ks%d" % t) for t in range(NT)]
            vsb_ = [bf("Vs%d" % t) for t in range(NT)]
            e_t = [tmp[0], tmp[1], tmp[2]]
            sp_t = [tmp[3], tmp[4], tmp[5]]
            Be = [Bt[0], Bt[1], Bt[2]]
            Bsp = [Bt[3], Bt[4], Bt[5]]
            dqm, lacc = sg[:, 0, :], sg[:, 1, :]
            u_t = [sg[:, 2, :], sg[:, 3, :]]
            Bdqm, Blacc, Bu = bf("sg"), bf("lacc"), [bf("u0"), bf("u1")]
            S.op("dve", lambda e: e.tensor_scalar(out=dqm, in0=misc_t[:, 384:896], scalar1=pcf[:, j:j + 1], scalar2=None, op0=ALU.add),
                 [bf("misc"), bf("pcf")], [Bdqm])
            nkb = NKB // 2 if (NOWN == 2 and j == 0) else NKB
            kbs = list(reversed(range(nkb)))
            for hp in range(8):
                def consq(fc, pb, pbb):
                    S.op("act", lambda e: e.copy(out=qT2[:, fc, 0:N], in_=pb[:, 0:N]), [pbb], [bf("qT2")])
                lin_fm([tab["q"][0] + hp], 2, KC, xb_rhs(cx, N), N, consq)
                wprefetch()
                for jj in range(2):
                    h = 2 * hp + jj
                    S.dma("sp", KTh[:, 0:nkb * 128], Ks[h][:, 0:nkb * 128], reads=ksb, writes=[bf("KTh")])
                    S.dma("sp", Vh[:, 0:nkb * 128], Vs[h][:, 0:nkb * 128], reads=vsb_, writes=[bf("Vh")])
                    S.op("dve", lambda e: e.memset(lacc, 0.0), [], [Blacc])

                    def stage1(n):
                        kb = kbs[n]
                        i3 = n % 3
                        z, zb = ring()
                        S.op("pe", lambda e: e.matmul(z[:, :], lhsT=KTh[:, kb * 128:(kb + 1) * 128], rhs=qT2[:, jj, :], start=True, stop=True),
                             [bf("KTh"), bf("qT2")], [zb])
                        S.op("act", lambda e: e.activation(out=e_t[i3], in_=z[:, :], func=AF.Exp, scale=scale, bias=sbb[:, h:h + 1]),
                             [zb, bf("sbb")], [Be[i3]])
                        S.op("dve", lambda e: e.scalar_tensor_tensor(out=e_t[i3], in0=dqm, scalar=float(kb * 128), in1=e_t[i3],
                                                                     op0=ALU.is_gt, op1=ALU.mult), [Bdqm, Be[i3]], [Be[i3]])
                        S.op("act", lambda e: e.activation(out=sp_t[i3], in_=e_t[i3], func=AF.Ln, bias=1.0), [Be[i3]], [Bsp[i3]])

                    def stage2(n):
                        i3, i2 = n % 3, n % 2
                        a, ab = ring()
                        S.op("pe", lambda e: e.matmul(a[:, :], lhsT=misc_t[:, 128:256], rhs=sp_t[i3], start=True, stop=False),
                             [bf("misc"), Bsp[i3]], [ab])
                        S.op("pe", lambda e: e.matmul(a[:, :], lhsT=ones[:, :], rhs=lacc, start=False, stop=True), [bf("ones"), Blacc], [ab])
                        S.op("act", lambda e: e.activation(out=u_t[i2], in_=a[:, :], func=AF.Exp, scale=-1.0), [ab], [Bu[i2]])
                        S.op("dve", lambda e: e.tensor_tensor(out=w_t[i2][:, :], in0=e_t[i3], in1=u_t[i2], op=ALU.mult),
                             [Be[i3], Bu[i2]], [bf("w%d" % i2)])
                        S.op("dve", lambda e: e.tensor_tensor(out=lacc, in0=lacc, in1=sp_t[i3], op=ALU.add),
                             [Blacc, Bsp[i3]], [Blacc])

                    def stage3(n):
                        kb = kbs[n]
                        i2 = n % 2
                        S.op("pe", lambda e: e.matmul(accB[:, :], lhsT=Vh[:, kb * 128:(kb + 1) * 128], rhs=w_t[i2][:, :], start=(n == 0), stop=(n == nkb - 1)),
                             [bf("Vh"), bf("w%d" % i2)], [bf("accB")])

                    for n in range(nkb + 2):
                        if n < nkb:
                            stage1(n)
                        if 0 <= n - 1 < nkb:
                            stage2(n - 1)
                        if 0 <= n - 2 < nkb:
                            stage3(n - 2)
                    S.op("act", lambda e: e.copy(out=hbuf[:, h, :], in_=accB[:, :]), [bf("accB")], [Bh[h]])

        def layer1_tail(cx, N, ob_rhs):
            lin_fm([tab["o"][0] + i for i in range(8)], 2, KC, ob_rhs, N, add_into_xacc(cx, N))
            layer_norm(cx, N, 2, True)
            mlp(cx, N, 1)
            layer_norm(cx, N, 3, False)

        def sb_attention_sample(cx):
            N = NS
            P_ = NPG
            Eall = xacc[:].rearrange("p c n -> p (c n)")[:, 0:PAGE * HQ].rearrange("p (r q) -> p r q", q=HQ)
            Aall = Ssb[:].rearrange("p c n -> p (c n)")[:, 0:PAGE * HQ].rearrange("p (r q) -> p r q", q=HQ)
            Wall = xb[:].rearrange("p c n -> p (c n)")[:, 0:PAGE * HQ].rearrange("p (r q) -> p r q", q=HQ)
            BE, BA, BW = bf("Eall"), bf("Aall"), bf("Wall")
            allS = [bf("S%d" % h) for h in range(RH)]
            hflat = hbuf[:].rearrange("p c n -> p (c n)")
            Kb = hflat[:, 0:D]
            KT = hflat[:, D:2 * D].rearrange("p (h g) -> p h g", h=SH)
            Vb = hflat[:, 2 * D:3 * D]
            pflat = pool[:].rearrange("p c n -> p (c n)")
            gat = [pflat[:, 0:D], pflat[:, D:2 * D]]
            Bg = [bf("gat0"), bf("gat1")]

            def consq(fc, pb, pbb):
                S.op("act", lambda e: e.copy(out=qTs[:, fc, :], in_=pb[:, 0:N]), [pbb], [bf("qTs")])
            lin_fm([tab["q"][0] + i for i in range(8)], 2, KC, xb_rhs(cx, N), N, consq)
            wprefetch()
            S.dma("sp", pti[0:P_, :], pt[:, :], writes=[bf("pti")])
            S.op("dve", lambda e: e.tensor_copy(out=ptf[0:P_, :], in_=pti[0:P_, :]), [bf("pti")], [bf("ptf")])
            S.op("dve", lambda e: e.tensor_scalar(out=ptf[0:P_, :], in0=ptf[0:P_, :], scalar1=float(PAGE), scalar2=None, op0=ALU.mult), [bf("ptf")], [bf("ptf")])
            S.op("dve", lambda e: e.tensor_scalar(out=idxf[0:P_, :], in0=misc_t[0:P_, 896 + HQ:896 + HQ + PAGE], scalar1=ptf[0:P_, 0:1], scalar2=None, op0=ALU.add),
                 [bf("ptf"), bf("misc")], [bf("idxf")])
            S.op("dve", lambda e: e.tensor_copy(out=idxi[0:P_, :], in_=idxf[0:P_, :]), [bf("idxf")], [bf("pti")])
            S.op("dve", lambda e: e.tensor_copy(out=sbb64[:].rearrange("p (h q) -> p h q", h=SH), in_=sbb[:].unsqueeze(2).to_broadcast([128, SH, NS])),
                 [bf("sbb")], [bf("sbb64")])
            for r in range(PAGE):
                g = r % 2
                S.dma("pool", None, None, reads=[bf("pti")], writes=[Bg[g]] + (Bt if r < 2 else []),
                      fn=lambda e, g=g, r=r: e.indirect_dma_start(out=gat[g][0:P_, :], out_offset=None, in_=cache_k[:, :],
                                                                  in_offset=bass.IndirectOffsetOnAxis(ap=idxi[0:P_, r:r + 1], axis=0)))
                if r % 2 == 0:
                    S.op("act", lambda e, g=g: e.copy(out=Kb[0:P_, :], in_=gat[g][0:P_, :]), [Bg[g]], [bf("Kb")] + (Bh if r == 0 else []))
                else:
                    S.op("dve", lambda e, g=g: e.tensor_copy(out=Kb[0:P_, :], in_=gat[g][0:P_, :]), [Bg[g]], [bf("Kb")])
                z, zb = ring()
                for hh in range(2):
                    for hl in range(8):
                        h = hh * 8 + hl
                        S.op("pe", lambda e, hl=hl, h=h: e.transpose(tpb[:, hl * 128:hl * 128 + P_], Kb[0:P_, h * 128:(h + 1) * 128], identb[0:P_, 0:P_]),
                             [bf("Kb"), bf("identb")], [bf("tpbA"), bf("tpbB")])
                    S.op("act" if hh == 0 else "dve",
                         (lambda e, hh=hh: e.copy(out=KT[:, hh * 8:(hh + 1) * 8, 0:P_], in_=tpb[:, :].rearrange("p (h g) -> p h g", h=8)[:, :, 0:P_])) if hh == 0 else
                         (lambda e, hh=hh: e.tensor_copy(out=KT[:, hh * 8:(hh + 1) * 8, 0:P_], in_=tpb[:, :].rearrange("p (h g) -> p h g", h=8)[:, :, 0:P_])),
                         [bf("tpbA"), bf("tpbB")], [bf("KT%d" % hh)] + (Bh if r == 0 else []))
                    for hl in range(8):
                        h = hh * 8 + hl
                        S.op("pe", lambda e, h=h: e.matmul(z[0:P_, h * NS:(h + 1) * NS], lhsT=KT[:, h, 0:P_], rhs=qTs[:, h, :], start=True, stop=True),
                             [bf("KT%d" % hh), bf("qTs")], [zb])
                S.op("dve", lambda e, r=r: e.scalar_tensor_tensor(out=Eall[0:P_, r, :], in0=z[0:P_, 0:HQ], scalar=scale, in1=sbb64[0:P_, :], op0=ALU.mult, op1=ALU.add),
                     [zb, bf("sbb64")], [BE] + Bx)
            S.op("act", lambda e: e.activation(out=Eall[0:P_], in_=Eall[0:P_], func=AF.Exp), [BE], [BE])
            S.op("act", lambda e: e.activation(out=Aall[0:P_], in_=Eall[0:P_], func=AF.Ln, bias=1.0), [BE] + allS, [BA] + allS)
            S.op("dve", lambda e: e.tensor_reduce(out=Tt[0:P_, :], in_=Aall[0:P_].rearrange("p r q -> p q r"), axis=mybir.AxisListType.X, op=ALU.add),
                 [BA], [bf("Tt")])
            zn, znb = ring()
            for h in range(SH):
                S.op("pe", lambda e, h=h: e.matmul(zn[0:NS, h * NS:(h + 1) * NS], lhsT=ksn[:, h, :], rhs=qTs[:, h, :], start=True, stop=True),
                     [bf("ksn"), bf("qTs")], [znb])
            S.op("dve", lambda e: e.scalar_tensor_tensor(out=en[0:NS, :], in0=zn[0:NS, 0:HQ], scalar=scale, in1=sbb64[0:NS, :], op0=ALU.mult, op1=ALU.add),
                 [znb, bf("sbb64")], [bf("en")])
            S.op("act", lambda e: e.activation(out=en[0:NS, :], in_=en[0:NS, :], func=AF.Exp), [bf("en")], [bf("en")])
            S.op("dve", lambda e: e.tensor_tensor(out=en[0:NS, :], in0=en[0:NS, :], in1=misc_t[0:NS, 896:896 + HQ], op=ALU.mult), [bf("en"), bf("misc")], [bf("en")])
            S.op("act", lambda e: e.activation(out=spn[0:NS, :], in_=en[0:NS, :], func=AF.Ln, bias=1.0), [bf("en")], [bf("spn")])
            ct, ctb = ring()
            S.op("pe", lambda e: e.matmul(ct[0:P_, 0:HQ], lhsT=misc_t[0:P_, 256:256 + P_], rhs=Tt[0:P_, :], start=True, stop=False), [bf("misc"), bf("Tt")], [ctb])
            S.op("pe", lambda e: e.matmul(ct[0:P_, 0:HQ], lhsT=ones[0:NS, 0:P_], rhs=spn[0:NS, :], start=False, stop=True), [bf("ones"), bf("spn")], [ctb])
            S.op("dve", lambda e: e.tensor_tensor(out=Aall[0:P_, PAGE - 1, :], in0=Aall[0:P_, PAGE - 1, :], in1=ct[0:P_, 0:HQ], op=ALU.add), [BA, ctb], [BA])
            sh = 1
            while sh < PAGE:
                S.op("dve", lambda e, sh=sh: e.tensor_tensor(out=Aall[0:P_, 0:PAGE - sh, :], in0=Aall[0:P_, 0:PAGE - sh, :], in1=Aall[0:P_, sh:PAGE, :], op=ALU.add),
                     [BA], [BA])
                sh *= 2
            S.op("act", lambda e: e.activation(out=Aall[0:P_], in_=Aall[0:P_], func=AF.Exp, scale=-1.0), [BA], [BA])
            S.op("dve", lambda e: e.tensor_tensor(out=Wall[0:P_], in0=Eall[0:P_], in1=Aall[0:P_], op=ALU.mult), [BE, BA] + Bxb, [BW] + Bxb)
            an, anb = ring()
            S.op("pe", lambda e: e.matmul(an[0:NS, 0:HQ], lhsT=misc_t[0:NS, 128:128 + NS], rhs=spn[0:NS, :], start=True, stop=True), [bf("misc"), bf("spn")], [anb])
            S.op("act", lambda e: e.activation(out=un[0:NS, :], in_=an[0:NS, 0:HQ], func=AF.Exp, scale=-1.0), [anb], [bf("un")])
            S.op("dve", lambda e: e.tensor_tensor(out=wn[0:NS, :], in0=en[0:NS, :], in1=un[0:NS, :], op=ALU.mult), [bf("en"), bf("un")], [bf("wn")])
            for r in range(PAGE + 1):
                o_, ob_ = ring()
                if r < PAGE:
                    g = r % 2
                    S.dma("pool", None, None, reads=[bf("pti")], writes=[Bg[g]],
                          fn=lambda e, g=g, r=r: e.indirect_dma_start(out=gat[g][0:P_, :], out_offset=None, in_=cache_v[:, :],
                                                                      in_offset=bass.IndirectOffsetOnAxis(ap=idxi[0:P_, r:r + 1], axis=0)))
                    if r % 2 == 0:
                        S.op("act", lambda e, g=g: e.copy(out=Vb[0:P_, :], in_=gat[g][0:P_, :]), [Bg[g]], [bf("Vb")] + (Bh if r == 0 else []))
                    else:
                        S.op("dve", lambda e, g=g: e.tensor_copy(out=Vb[0:P_, :], in_=gat[g][0:P_, :]), [Bg[g]], [bf("Vb")])
                    for h in range(SH):
                        S.op("pe", lambda e, h=h, r=r: e.matmul(o_[:, h * NS:(h + 1) * NS], lhsT=Vb[0:P_, h * 128:(h + 1) * 128], rhs=Wall[0:P_, r, h * NS:(h + 1) * NS],
                                                                start=True, stop=True), [bf("Vb"), BW], [ob_])
                else:
                    for h in range(SH):
                        S.op("pe", lambda e, h=h: e.matmul(o_[:, h * NS:(h + 1) * NS], lhsT=vsn[0:NS, h * 128:(h + 1) * 128], rhs=wn[0:NS, h * NS:(h + 1) * NS],
                                                           start=True, stop=True), [bf("vsn"), bf("wn")], [ob_])
                if r == 0:
                    S.op("dve", lambda e: e.tensor_copy(out=oacc[:, :], in_=o_[:, 0:HQ]), [ob_], [bf("oacc")])
                else:
                    S.op("dve", lambda e: e.tensor_tensor(out=oacc[:, :], in0=oacc[:, :], in1=o_[:, 0:HQ], op=ALU.add), [ob_, bf("oacc")], [bf("oacc")])
            S.op("act", lambda e: e.copy(out=obs[:].rearrange("p h q -> p (h q)"), in_=oacc[:, :]), [bf("oacc")], [bf("obs")])

        @blk.sync
        def _(sync):
            S.dma("sp", dt_t[:].rearrange("p h i -> p (h i)"), dt_d[:, :], writes=[bf("dt")])
            S.dma("sp", dq_t[:].rearrange("p h i -> p (h i)"), dq_d[:, :], writes=[bf("dq")])
            S.dma("sp", dk_t[:], dk_d[:, :], writes=[bf("dk")])
            S.dma("sp", misc_t[:], misc_d[:, :], writes=[bf("misc")])
            S.dma("sp", lnp[:], lnp_d[:, :], writes=[bf("lnp")])
            S.dma("sp", sbb[:], sbb_d.partition_broadcast(128), writes=[bf("sbb")])
            S.dma("sp", pcf[:], pcf_d[:, :], writes=[bf("pcf")])
            S.dma("sp", pci[:], pci_d[:, :], writes=[bf("pci")])
            S.op("dve", lambda e: e.tensor_copy(out=identb[:], in_=misc_t[:, 0:128]), [bf("misc")], [bf("identb")])
            S.op("dve", lambda e: e.memset(ones[:], 1.0), [], [bf("ones")])
            S.op("dve", lambda e: e.tensor_scalar(out=lnpa[:], in0=lnp[:], scalar1=ALPHA, scalar2=None, op0=ALU.mult), [bf("lnp")], [bf("lnp")])

            def load_x(cx, src, N, c0):
                S.dma("sp", cx.xacc[:, :, 0:N], src[:, :, c0:c0 + N], writes=cx.Bx[0:1] if cx is ctxS else cx.Bx)
                S.dma("pool", cx.xb[:, :, 0:N], src[:, :, c0:c0 + N], writes=cx.Bxb[0:1] if cx is ctxS else cx.Bxb)
                for c in range(KC):
                    S.op("act", lambda e, c=c: e.mul(out=cx.xacc[:, c, 0:N], in_=cx.xacc[:, c, 0:N], mul=ALPHA), [cx.Bx[c]], [cx.Bx[c]])

            def layer0(cx, N, cs, nchunk, t, sample):
                retention(cx, N, cs, nchunk, (SEQ if sample else t * TT), sample)
                layer_norm(cx, N, 0, True)
                if stop_after == "ln0":
                    return
                mlp(cx, N, 0)
                layer_norm(cx, N, 1, True)
                if stop_after == "mlp0":
                    return
                kv_proj(cx, N, cs, nchunk, t, sample)

            for h in range(RH):
                S.op("dve", lambda e, h=h: e.memset(Ssb[:, 2 * h:2 * h + 2, :], 0.0), [], [bf("S%d" % h)])
            for t in range(NT):
                load_x(ctxP, xT, TT, t * TT)
                if stop_after == "ret0":
                    retention(ctxP, TT, 128, 4, 0, False)
                    S.dma("sp", yT[0], xacc[:].rearrange("p c n -> p (c n)"), reads=Bx, writes=[bf("o_y")])
                    S.finish(list(B.values()))
                    return
                layer0(ctxP, TT, 128, 4, t, False)
                if stop_after in ("t0", "ln0", "mlp0"):
                    S.dma("sp", yT[0], xacc[:].rearrange("p c n -> p (c n)"), reads=Bx, writes=[bf("o_y")])
                    S.finish(list(B.values()))
                    return
                S.dma("sp", X1s[t * 128:(t + 1) * 128, :], xacc[:].rearrange("p c n -> p (c n)"), reads=Bx, writes=[bf("X1s%d" % t)])
                S.dma("sp", XBs[t * 128:(t + 1) * 128, :], xb[:].rearrange("p c n -> p (c n)"), reads=Bxb, writes=[bf("XBs%d" % t)])
            for h in range(RH):
                S.dma("sp", retp[h].rearrange("(c p) v -> p c v", p=128), Ssb[:, 2 * h:2 * h + 2, :], reads=[bf("S%d" % h)], writes=[bf("o_retp")])
            if stop_after == "l0":
                S.finish(list(B.values()))
                return
            for j in range(NOWN):
                x1b = [bf("X1s%d" % t) for t in range(NT)]
                xbb = [bf("XBs%d" % t) for t in range(NT)]
                S.dma("pool", None, None, reads=x1b + [bf("pci")], writes=Bx,
                      fn=lambda e, j=j: e.indirect_dma_start(out=xacc[:].rearrange("p c n -> p (c n)"), out_offset=None, in_=X1s[:, :],
                                                             in_offset=bass.IndirectOffsetOnAxis(ap=pci[:, j:j + 1], axis=0)))
                S.dma("pool", None, None, reads=xbb + [bf("pci")], writes=Bxb,
                      fn=lambda e, j=j: e.indirect_dma_start(out=xb[:].rearrange("p c n -> p (c n)"), out_offset=None, in_=XBs[:, :],
                                                             in_offset=bass.IndirectOffsetOnAxis(ap=pci[:, j:j + 1], axis=0)))
                sb_attention_prompt(ctxP, j)
                layer1_tail(ctxP, TT, lambda kc: (hbuf[:, kc, 0:TT], [Bh[kc]]))
                S.dma("sp", yT[j], xacc[:].rearrange("p c n -> p (c n)"), reads=Bx, writes=[bf("o_y")])
            if stop_after is None:
                for h in range(RH):
                    S.dma("sp", Ssb[:, 2 * h:2 * h + 2, :], state[h].rearrange("(c p) v -> p c v", p=128), writes=[bf("S%d" % h)])
                load_x(ctxS, xsT, NS, 0)
                layer0(ctxS, NS, NS, 1, 0, True)
                for h in range(RH):
                    S.dma("sp", rets[h].rearrange("(c p) v -> p c v", p=128), Ssb[:, 2 * h:2 * h + 2, :], reads=[bf("S%d" % h)], writes=[bf("o_rets")])
                sb_attention_sample(ctxS)
                layer1_tail(ctxS, NS, lambda kc: (obs[:, kc, :], [bf("obs")]))
                S.dma("sp", ysT[:, :].rearrange("p (c n) -> p c n", c=KC), xsacc[:, :, :], reads=ctxS.Bx[0:1], writes=[bf("o_ys")])
            S.finish(list(B.values()))
        print("instructions:", S.n_inst, "waits:", S.n_wait, "pieces:", wpos[0], "/", len(plan), flush=True)
    return nc


_PROG_CACHE = {}
_LAST_RES = None


def _run(inputs, SEQ, NS, NPG, stop_after=None):
    x_prompt = np.asarray(inputs["x_prompt"], np.float32)
    x_sample = np.asarray(inputs["x_sample"], np.float32)
    state_ret = np.asarray(inputs["state_ret"], np.float32)
    cache_k = np.asarray(inputs["cache_k"], np.float32)
    cache_v = np.asarray(inputs["cache_v"], np.float32)
    page_table = np.asarray(inputs["page_table"], np.int32)
    NPOOL = cache_k.shape[0]
    BATCH = x_prompt.shape[0]
    NT = SEQ // TT
    NOWN = NT // 4
    key = (SEQ, NS, NPG, NPOOL, stop_after)
    if key not in _PROG_CACHE:
        _PROG_CACHE[key] = build_program(SEQ, NS, NPG, NPOOL, stop_after)
    nc = _PROG_CACHE[key]
    Wp = pack_weights(*[np.asarray(inputs[k], np.float32) for k in ("w_ret_in", "w_ret_out", "w_kv", "w_sb_q", "w_sb_o", "w_ff1", "w_ff2")])
    rope, dtab, dqtab, dktab, misc, _, _ = const_tables(SEQ, NS, NPG * PAGE)
    ln_g = np.asarray(inputs["ln_g"], np.float32).reshape(4, KC, 128)
    ln_b = np.asarray(inputs["ln_b"], np.float32).reshape(4, KC, 128)
    lnp = np.concatenate([ln_g.transpose(2, 0, 1).reshape(128, 4 * KC), ln_b.transpose(2, 0, 1).reshape(128, 4 * KC)], axis=1)
    sbb = np.asarray(inputs["sb_bias"], np.float32).reshape(1, SH)
    ck = cache_k.reshape(NPOOL * PAGE, D)
    cv = cache_v.reshape(NPOOL * PAGE, D)
    in_maps = []
    for c in range(NCORE):
        b, s = (c // 4) % BATCH, c % 4
        xT = np.ascontiguousarray(x_prompt[b].reshape(SEQ, KC, 128).transpose(2, 1, 0))
        xsT = np.ascontiguousarray(x_sample[c].reshape(NS, KC, 128).transpose(2, 1, 0))
        own = [s, NT - 1 - s] if NOWN == 2 else [s * NOWN + j for j in range(NOWN)]
        pcf = np.broadcast_to(np.array([o * TT for o in own], np.float32)[None], (128, NOWN)).copy()
        pci = (np.array(own, np.int32)[None, :] * 128 + np.arange(128, dtype=np.int32)[:, None]).astype(np.int32)
        in_maps.append(dict(xT=xT, xsT=xsT, state=np.ascontiguousarray(state_ret[0, c]), cache_k=ck, cache_v=cv,
                            pt=np.ascontiguousarray(page_table[c].reshape(NPG, 1)), W=Wp, rope=rope, dtab=dtab, dqtab=dqtab,
                            dktab=dktab, misc=misc, lnp=lnp, sbb=sbb, pcf=pcf, pci=pci))
    import os as _os
    _n = int(_os.environ.get("KNCORE", NCORE))
    if _os.environ.get("KTRACE"):
        res = run_bass_kernel_spmd(nc, in_maps[:_n], core_ids=list(range(_n)), trace=True)
        print("KTRACE exec_time_ns", res.exec_time_ns, flush=True)
        global _LAST_RES
        _LAST_RES = res
    else:
        res = run_bass_kernel_spmd(nc, in_maps[:_n], core_ids=list(range(_n)))
    R = list(res.results) + [res.results[0]] * (NCORE - _n)
    y_prompt = np.zeros((BATCH, SEQ, D), np.float32)
    for c in range(NCORE):
        b, s = (c // 4) % BATCH, c % 4
        own = [s, NT - 1 - s] if NOWN == 2 else [s * NOWN + j for j in range(NOWN)]
        for j in range(NOWN):
            t = own[j]
            y_prompt[b, t * TT:(t + 1) * TT] = R[c]["yT"][j].reshape(128, KC, TT).transpose(2, 1, 0).reshape(TT, D)
    y_sample = np.stack([R[c]["ysT"].reshape(128, KC, NS).transpose(2, 1, 0).reshape(NS, D) for c in range(NCORE)])
    ret_prompt = np.stack([R[4 * b]["retp"] for b in range(BATCH)])[None]
    k_prompt = np.stack([R[4 * b]["kTo"].transpose(2, 0, 1) for b in range(BATCH)])
    v_prompt = np.stack([R[4 * b]["vo"].reshape(SEQ, SH, 128) for b in range(BATCH)])
    ret_sample = np.stack([R[c]["rets"] for c in range(NCORE)])[None]
    k_sample = np.stack([R[c]["ksTo"].reshape(128, SH, NS).transpose(2, 1, 0) for c in range(NCORE)])
    v_sample = np.stack([R[c]["vso"].reshape(NS, SH, 128) for c in range(NCORE)])
    return (y_prompt, y_sample, np.ascontiguousarray(ret_prompt), np.ascontiguousarray(k_prompt), v_prompt,
            np.ascontiguousarray(ret_sample), np.ascontiguousarray(k_sample), v_sample)


def kernel(**inputs):
    SEQ = inputs["x_prompt"].shape[1]
    NS = inputs["x_sample"].shape[1]
    NPG = inputs["page_table"].shape[1]
    return _run(inputs, SEQ, NS, NPG)
```
